# Optimizing a Trainium2 kernel written in Bass

```python
import math
import jax
import jax.numpy as jnp
from jax import lax
import numpy as np

D_MODEL = 1024
BATCH = 32
SEQ = 256
DEPTH = 4
DEC_BATCH = 2
DEC_SEQ = 2048
PAST_LEN = 256

GRID_W = 64
N_MIXERS = 3
N_GLA = (DEPTH + 2) // 3
N_FN = (DEPTH + 1) // 3
N_HY = DEPTH // 3
EPS = 1e-6

GLA_HEADS = 4
GLA_KEY = D_MODEL // 2
GLA_VAL = D_MODEL
GLA_DK = GLA_KEY // GLA_HEADS
GLA_DV = GLA_VAL // GLA_HEADS
GLA_RANK = 16
GLA_GATE_NORM = 16.0
GLA_CHUNK = 32
GLA_IN = 2 * GLA_KEY + 2 * GLA_VAL + 2 * GLA_RANK

FN_WIDTH = D_MODEL
FN_GROUPS = 4
FN_GC = FN_WIDTH // FN_GROUPS

HY_WIDTH = D_MODEL
HY_ORDER = 2
HY_SHORT = 3
HY_EMB = 33
HY_BANDS = (HY_EMB - 1) // 2
HY_FFN = 64
HY_TARGET = 1e-2
HY_MIN_DECAY = math.log(HY_TARGET) / 1.5
HY_MAX_DECAY = math.log(HY_TARGET) / 0.3

kernel_name = 'hybrid_gla_fnet_hyena_prefix_dit_step'

F32 = jnp.float32


def rms_norm(x, g):
    xf = x.astype(F32)
    y = xf * lax.rsqrt(jnp.mean(xf * xf, axis=-1, keepdims=True) + EPS)
    return (y * g.astype(F32)).astype(x.dtype)


def gla_scan(q, k, v, logg, s0):
    bsz, L, H, _ = q.shape
    dv = v.shape[-1]
    n = L // GLA_CHUNK

    def chunks(a):
        return a.reshape(bsz, n, GLA_CHUNK, *a.shape[2:]).swapaxes(0, 1)

    tri = jnp.tril(jnp.ones((GLA_CHUNK, GLA_CHUNK), dtype=bool))[None, :, :, None, None]

    def step(s, inp):
        qi, ki, vi, gi = inp
        bcum = jnp.cumsum(gi, axis=1)
        o_inter = jnp.einsum('bchk,bhkv->bchv', qi * jnp.exp(bcum), s)
        diff = bcum[:, :, None] - bcum[:, None, :]
        dec = jnp.where(tri, jnp.exp(jnp.where(tri, diff, 0.0)), 0.0)
        att = jnp.einsum('bqhk,bshk,bqshk->bhqs', qi.astype(F32), ki.astype(F32), dec)
        o_intra = jnp.einsum('bhqs,bshv->bqhv', att, vi.astype(F32))
        blast = bcum[:, -1]
        s_new = jnp.exp(blast)[..., None] * s + jnp.einsum(
            'bshk,bshv->bhkv', ki * jnp.exp(blast[:, None] - bcum), vi.astype(F32))
        return s_new, o_inter + o_intra

    s_fin, o = lax.scan(step, s0.astype(F32), (chunks(q), chunks(k), chunks(v), chunks(logg)))
    return o.swapaxes(0, 1).reshape(bsz, L, H, dv), s_fin


def gla_mixer(h, w_in, w_dec, b_dec, onorm_g, w_out, s0_f, s0_b):
    bsz, L, _ = h.shape
    proj = h @ w_in
    o1 = GLA_KEY
    o2 = 2 * GLA_KEY
    o3 = o2 + GLA_VAL
    o4 = o3 + GLA_VAL
    o5 = o4 + GLA_RANK
    q = (proj[..., :o1] * GLA_DK ** -0.5).reshape(bsz, L, GLA_HEADS, GLA_DK)
    k = proj[..., o1:o2].reshape(bsz, L, GLA_HEADS, GLA_DK)
    v = proj[..., o2:o3].reshape(bsz, L, GLA_HEADS, GLA_DV)
    r = proj[..., o3:o4]

    def log_decay(lr, d):
        logit = (lr @ w_dec[d] + b_dec[d]).astype(F32)
        return (jax.nn.log_sigmoid(logit) / GLA_GATE_NORM).reshape(bsz, L, GLA_HEADS, GLA_DK)

    g_f = log_decay(proj[..., o4:o5], 0)
    g_b = log_decay(proj[..., o5:], 1)
    o_f, s_f = gla_scan(q, k, v, g_f, s0_f)
    o_b, s_b = gla_scan(jnp.flip(q, 1), jnp.flip(k, 1), jnp.flip(v, 1), jnp.flip(g_b, 1), s0_b)
    o = o_f + jnp.flip(o_b, 1)
    o = o * lax.rsqrt(jnp.mean(o * o, axis=-1, keepdims=True) + EPS) * onorm_g.astype(F32)
    o = o.reshape(bsz, L, GLA_VAL) * jax.nn.silu(r.astype(F32))
    return o.astype(h.dtype) @ w_out, s_f, s_b


def fnet_mixer(h, w_in, w_out):
    bsz, L, _ = h.shape
    proj = h @ w_in
    u, z = proj[..., :FN_WIDTH], proj[..., FN_WIDTH:]
    ug = u.astype(F32).reshape(bsz, L, FN_GROUPS, FN_GC)
    f = jnp.fft.fft2(ug, axes=(1, 3), norm='ortho').real.reshape(bsz, L, FN_WIDTH)
    y = f * jax.nn.silu(z.astype(F32))
    return y.astype(h.dtype) @ w_out


def hyena_filters(L, w1, b1, w2, b2, w3, b3, w4, freq):
    t = jnp.linspace(0.0, 1.0, L, dtype=F32)[:, None]
    w = 2.0 * math.pi * jnp.arange(L, dtype=F32)[:, None] / L
    f = jnp.linspace(1e-4, HY_BANDS - 1, HY_BANDS, dtype=F32)[None, :]
    zpos = jnp.concatenate([t, jnp.cos(f * w), -jnp.sin(f * w)], axis=-1)
    fr = freq.astype(F32)
    a = jnp.sin(fr * (zpos @ w1 + b1))
    a = jnp.sin(fr * (a @ w2 + b2))
    a = jnp.sin(fr * (a @ w3 + b3))
    hf = (a @ w4).astype(F32).reshape(L, HY_ORDER, 2, HY_WIDTH)
    deltas = jnp.abs(jnp.linspace(HY_MIN_DECAY, HY_MAX_DECAY, HY_WIDTH, dtype=F32))
    hf = hf * jnp.exp(-t * deltas)[:, None, None, :]
    fwd, bwd = hf[:, :, 0], hf[:, :, 1]
    two_sided = jnp.concatenate(
        [fwd, jnp.zeros((1, HY_ORDER, HY_WIDTH), F32), jnp.flip(bwd[1:], axis=0)], axis=0)
    return jnp.fft.rfft(two_sided, axis=0)


def long_conv(u, filt_f):
    L = u.shape[1]
    U = jnp.fft.rfft(u, n=2 * L, axis=1)
    return jnp.fft.irfft(U * filt_f[None], n=2 * L, axis=1)[:, :L]


def hyena_mixer(h, w_in, conv_w, conv_b, w1, b1, w2, b2, w3, b3, w4, freq, d_skip, w_out):
    L = h.shape[1]
    proj = h @ w_in
    u, z = proj[..., :3 * HY_WIDTH], proj[..., 3 * HY_WIDTH:]
    up = jnp.pad(u, ((0, 0), (1, 1), (0, 0)))
    u = up[:, :-2] * conv_w[0] + up[:, 1:-1] * conv_w[1] + up[:, 2:] * conv_w[2] + conv_b
    x1, x2, v = jnp.split(u.astype(F32), 3, axis=-1)
    filt_f = hyena_filters(L, w1, b1, w2, b2, w3, b3, w4, freq)
    y = v
    for n, gate in enumerate((x1, x2)):
        y = gate * (long_conv(y, filt_f[:, n]) + y * d_skip[n].astype(F32))
    y = y * jax.nn.silu(z.astype(F32))
    return y.astype(h.dtype) @ w_out


def setup_inputs(seed: int = 0) -> dict:
    key = jax.random.key(seed)
    ks = iter(jax.random.split(key, 40))

    def nrm(shape, scale):
        return jax.random.normal(next(ks), shape, F32) * scale

    D = D_MODEL
    return {
        'x_prompt': nrm((BATCH, SEQ, D), 1.0),
        'x_sample': nrm((DEC_BATCH, DEC_SEQ, D), 1.0),
        'state_gla': nrm((DEC_BATCH, N_GLA, 2, GLA_HEADS, GLA_DK, GLA_DV), 1.0),
        'c': nrm((DEC_BATCH, D), 1.0),
        'c_ctx': nrm((D,), 1.0),
        'mod_w': nrm((DEPTH, D, 3 * D), 0.5 * D ** -0.5),
        'mod_b': nrm((DEPTH, 3 * D), 0.01),
        'norm_g': 1.0 + nrm((DEPTH, D), 0.05),
        'final_norm_g': 1.0 + nrm((D,), 0.05),
        'gla_w_in': nrm((N_GLA, D, GLA_IN), D ** -0.5),
        'gla_w_dec': nrm((N_GLA, 2, GLA_RANK, GLA_KEY), GLA_RANK ** -0.5),
        'gla_b_dec': nrm((N_GLA, 2, GLA_KEY), 0.5),
        'gla_onorm_g': 1.0 + nrm((N_GLA, GLA_DV), 0.05),
        'gla_w_out': nrm((N_GLA, GLA_VAL, D), GLA_VAL ** -0.5),
        'fn_w_in': nrm((N_FN, D, 2 * FN_WIDTH), D ** -0.5),
        'fn_w_out': nrm((N_FN, FN_WIDTH, D), FN_WIDTH ** -0.5),
        'hy_w_in': nrm((N_HY, D, 4 * HY_WIDTH), D ** -0.5),
        'hy_conv_w': nrm((N_HY, HY_SHORT, 3 * HY_WIDTH), HY_SHORT ** -0.5),
        'hy_conv_b': nrm((N_HY, 3 * HY_WIDTH), 0.02),
        'hy_ffn_w1': nrm((N_HY, HY_EMB, HY_FFN), HY_EMB ** -0.5),
        'hy_ffn_b1': nrm((N_HY, HY_FFN), 0.1),
        'hy_ffn_w2': nrm((N_HY, HY_FFN, HY_FFN), HY_FFN ** -0.5),
        'hy_ffn_b2': nrm((N_HY, HY_FFN), 0.1),
        'hy_ffn_w3': nrm((N_HY, HY_FFN, HY_FFN), HY_FFN ** -0.5),
        'hy_ffn_b3': nrm((N_HY, HY_FFN), 0.1),
        'hy_ffn_w4': nrm((N_HY, HY_FFN, HY_ORDER * 2 * HY_WIDTH), 0.1 * HY_FFN ** -0.5),
        'hy_freq': 1.0 + nrm((N_HY, HY_FFN), 0.1),
        'hy_d': nrm((N_HY, HY_ORDER, HY_WIDTH), 0.5),
        'hy_w_out': nrm((N_HY, HY_WIDTH, D), HY_WIDTH ** -0.5),
    }


def reference(x_prompt, x_sample, state_gla, c, c_ctx, mod_w, mod_b, norm_g, final_norm_g,
              gla_w_in, gla_w_dec, gla_b_dec, gla_onorm_g, gla_w_out,
              fn_w_in, fn_w_out,
              hy_w_in, hy_conv_w, hy_conv_b, hy_ffn_w1, hy_ffn_b1, hy_ffn_w2, hy_ffn_b2,
              hy_ffn_w3, hy_ffn_b3, hy_ffn_w4, hy_freq, hy_d, hy_w_out):
    ctx_mod = jnp.einsum('d,lde->le', jax.nn.silu(c_ctx), mod_w) + mod_b
    lat_mod = jnp.einsum('bd,lde->lbe', jax.nn.silu(c), mod_w) + mod_b[:, None]
    xp, xs = x_prompt, x_sample
    new_states = []
    for i in range(DEPTH):
        kind, j = i % N_MIXERS, i // N_MIXERS
        sh_c, sc_c, g_c = jnp.split(ctx_mod[i], 3, axis=-1)
        sh_s, sc_s, g_s = jnp.split(lat_mod[i][:, None, :], 3, axis=-1)
        hp = rms_norm(xp, norm_g[i]) * (1.0 + sc_c) + sh_c
        hs = rms_norm(xs, norm_g[i]) * (1.0 + sc_s) + sh_s
        if kind == 0:
            zero = jnp.zeros((xp.shape[0], GLA_HEADS, GLA_DK, GLA_DV), F32)
            prm = (gla_w_in[j], gla_w_dec[j], gla_b_dec[j], gla_onorm_g[j], gla_w_out[j])
            op, s_f, s_b = gla_mixer(hp, *prm, zero, zero)
            os_, _, _ = gla_mixer(hs, *prm, state_gla[:, j, 0], state_gla[:, j, 1])
            new_states.append(jnp.stack([s_f, s_b], axis=1))
        elif kind == 1:
            op = fnet_mixer(hp, fn_w_in[j], fn_w_out[j])
            os_ = fnet_mixer(hs, fn_w_in[j], fn_w_out[j])
        else:
            prm = (hy_w_in[j], hy_conv_w[j], hy_conv_b[j], hy_ffn_w1[j], hy_ffn_b1[j],
                   hy_ffn_w2[j], hy_ffn_b2[j], hy_ffn_w3[j], hy_ffn_b3[j], hy_ffn_w4[j],
                   hy_freq[j], hy_d[j], hy_w_out[j])
            op = hyena_mixer(hp, *prm)
            os_ = hyena_mixer(hs, *prm)
        xp = xp + g_c * op
        xs = xs + g_s * os_
    y_prompt = rms_norm(xp, final_norm_g)
    y_sample = rms_norm(xs, final_norm_g)
    new_state_gla = jnp.stack(new_states, axis=1).astype(x_prompt.dtype)
    return (y_prompt, y_sample, new_state_gla)
```

```python
import os
import math
import numpy as np
import ml_dtypes
from contextlib import ExitStack
import concourse.bass as bass
import concourse.mybir as mybir
from concourse.bass_utils import run_bass_kernel_spmd

F32 = mybir.dt.float32
BF16 = mybir.dt.bfloat16
AF = mybir.ActivationFunctionType
ALU = mybir.AluOpType

NCORES = 8
D = 1024
DC = 8
EPS = 1e-6
LP = 256
LS = 2048
NAR = 21184


class Buf:
    __slots__ = ("name", "w", "r", "dsem", "dcnt")

    def __init__(self, name):
        self.name = name
        self.w = None
        self.r = {}
        self.dsem = None
        self.dcnt = 0


class Prog:
    ENG = ("pe", "act", "dve", "pool", "sp")

    def __init__(self, nc, es):
        self.nc = nc
        self.es = es
        self.streams = {e: [] for e in self.ENG}
        self.sem = {}
        for e in self.ENG:
            self.sem[e] = es.enter_context(nc.semaphore("s_" + e))
        self.cnt = {e: 0 for e in self.ENG}
        self.waited = {e: {} for e in self.ENG}
        self.dsems = []
        self.nsem = 0
        self.final = []
        self.stores = []
        self.psb = []
        self.psi = 0
        self.ninstr = 0

    def _wait(self, eng, deps):
        best = {}
        for d in deps:
            if d is None:
                continue
            s, v = d
            if best.get(s, (None, 0))[1] < v:
                best[s] = (s, v)
        for s, v in best.values():
            if eng == "pe" and s is self.sem["pe"]:
                continue
            if self.waited[eng].get(s, 0) >= v:
                continue
            self.waited[eng][s] = v
            self.streams[eng].append(lambda e, s=s, v=v: e.wait_ge(s, v))

    def op(self, eng, fn, reads=(), writes=()):
        deps = []
        for b in reads:
            deps.append(b.w)
        for b in writes:
            deps.append(b.w)
            deps.extend(b.r.items())
        self._wait(eng, deps)
        self.cnt[eng] += 1
        sem = self.sem[eng]
        ev = (sem, self.cnt[eng])
        self.streams[eng].append(lambda e, fn=fn, sem=sem: fn(e).then_inc(sem, 1))
        self.ninstr += 1
        for b in writes:
            b.w = ev
            b.r = {}
        for b in reads:
            if b not in writes:
                b.r[sem] = ev[1]

    def new_dsem(self):
        self.nsem += 1
        return self.es.enter_context(self.nc.semaphore("d%d" % self.nsem))

    def dma(self, q, out, in_, dst=None, src=None, final=False, slow=False):
        buf = dst if dst is not None else src
        if buf.dsem is None:
            buf.dsem = self.new_dsem()
        deps = []
        if dst is not None:
            if dst.w is not None and dst.w[0] is not dst.dsem:
                deps.append(dst.w)
            deps.extend(dst.r.items())
        if src is not None:
            deps.append(src.w)
        self._wait(q, deps)
        buf.dcnt += 16
        ds = buf.dsem
        ev = (ds, buf.dcnt)
        if slow:
            self.streams[q].append(lambda e, out=out, in_=in_, ds=ds: e.dma_start(out=out, in_=in_, allow_slow_non_contiguous=True).then_inc(ds, 16))
        else:
            self.streams[q].append(lambda e, out=out, in_=in_, ds=ds: e.dma_start(out=out, in_=in_).then_inc(ds, 16))
        self.ninstr += 1
        if dst is not None:
            dst.w = ev
            dst.r = {}
        if src is not None:
            src.r[ds] = ev[1]
            self.stores.append(ev)
        if final:
            self.final.append(ev)

    def barrier(self):
        evs = [(self.sem[e], self.cnt[e]) for e in self.ENG if self.cnt[e] > 0] + self.stores
        self.stores = []
        for e in self.ENG:
            self._wait(e, evs)

    def ps(self):
        b = self.psb[self.psi % len(self.psb)]
        self.psi += 1
        return b

    def mm(self, psbuf, mms, reads):
        def fn(e, mms=mms):
            r = None
            for (o, l, rh, st, sp) in mms:
                r = e.matmul(o, lhsT=l, rhs=rh, start=st, stop=sp)
            return r
        self.op("pe", fn, reads=reads, writes=[psbuf])
        self.ninstr += len(mms) - 1


def build_program(nl):
    nc = bass.Bass("TRN2", target_bir_lowering=False)
    es = ExitStack()
    with es:
        def din(name, shape, dt=F32):
            return nc.dram_tensor(name, list(shape), dt, kind="ExternalInput").ap()

        def dout(name, shape):
            return nc.dram_tensor(name, list(shape), F32, kind="ExternalOutput").ap()

        xp_d = din("xp", [4 * LP, D])
        xs_d = din("xs", [LS, D])
        st_d = din("st", [2, 2, 4, 128, 256])
        cv_d = din("cv", [128, DC, 2])
        modw_d = din("modw", [4, D, 3 * D])
        modb_d = din("modb", [128, 4, 24])
        ng_d = din("ng", [128, 4, DC])
        fng_d = din("fng", [128, DC])
        gwin_d = din("gwin", [2, D, 3104])
        gwdec_d = din("gwdec", [2, 33, 1024])
        gon_d = din("gon", [128, 2, 2])
        gwout_d = din("gwout", [2, D, D])
        fwin_d = din("fwin", [D, 2 * D])
        fwout_d = din("fwout", [D, D])
        hwin_d = din("hwin", [D, 4 * D])
        hcw_d = din("hcw", [128, 3, 24])
        hcb_d = din("hcb", [128, 24])
        hw1_d = din("hw1", [33, 64])
        hw2_d = din("hw2", [64, 64])
        hw3_d = din("hw3", [64, 64])
        hw4_d = din("hw4", [64, 4 * D])
        hb_d = din("hb", [64, 4])
        hd_d = din("hd", [128, 2, DC])
        hwout_d = din("hwout", [D, D])
        ident_d = din("ident", [128, 128])
        cm_d = din("cm", [128, 4, 129], BF16)
        mk_d = din("mk", [128, 2, 128])
        fcs_d = din("fcs", [256, 512], BF16)
        flt2_ = {LP: min(256, LP // 2), LS: min(256, LS // 2)}
        ftab_d = {L_: din("ftab" + nm_, [(L_ // 2) // flt2_[L_], 128, 2 * (L_ // 128) * flt2_[L_]], BF16) for nm_, L_ in (("p", LP), ("s", LS))}
        fcol_d = {L_: din("fcol" + nm_, [128, L_ // 128], BF16) for nm_, L_ in (("p", LP), ("s", LS))}
        nkc_ = {LP: (LP + 1 + 127) // 128, LS: (LS + 1 + 127) // 128}
        htf_d = {LP: din("htfp", [2, nkc_[LP], 128, LP], BF16), LS: din("htfs", [2, nkc_[LS], 128, LS], BF16)}
        nt2_ = {LP: min(LP // 2, 512), LS: min(LS // 2, 512)}
        hti_d = {L_: din("hti" + nm_, [nkc_[L_], (L_ // 2) // nt2_[L_], 128, 2 * nt2_[L_]], BF16) for nm_, L_ in (("p", LP), ("s", LS))}
        htic_d = {L_: din("htic" + nm_, [128, nkc_[L_] * 2], BF16) for nm_, L_ in (("p", LP), ("s", LS))}
        zpos_d = {LP: din("zposp", [33, LP]), LS: din("zposs", [33, LS])}
        tcol_d = {LP: din("tcolp", [128, LP // 128]), LS: din("tcols", [128, LS // 128])}
        bm_d = din("bm", [128, 1])
        wk_d = {LP: din("wkp", [128, 3]), LS: din("wks", [128, 17])}
        delta_d = din("delta", [128, D])

        yp_d = dout("yp", [4 * LP, D])
        ys_d = dout("ys", [LS, D])
        ns_d = dout("ns", [4, 2, 2, 4, 128, 256])

        pg = Prog(nc, es)
        dbg_on = bool(os.environ.get("KDBG"))
        dbg_seen = set()

        def dbg(name, ap, buf, dt=F32):
            if not dbg_on or name in dbg_seen:
                return
            dbg_seen.add(name)
            shp = list(ap.shape)
            dd = nc.dram_tensor("dbg_" + name, shp, dt, kind="ExternalOutput").ap()
            pg.dma("sp", dd, ap, src=buf, final=True)

        def sb(name, shape, dt=F32):
            return es.enter_context(nc.sbuf_tensor("sb_" + name, list(shape), dt))

        ident = sb("ident", [128, 128])
        ones_bf = sb("ones_bf", [128, 128], BF16)
        cm = sb("cm", [128, 4, 129], BF16)
        mk = sb("mk", [128, 2, 128])
        SH = sb("SH", [128, 4, DC, 2])
        AA = sb("AA", [128, 4, DC, 2])
        GG = sb("GG", [128, 4, DC, 2])
        ng = sb("ng", [128, 4, DC])
        fng = sb("fng", [128, DC])
        gon = sb("gon", [128, 2, 2])
        xT = sb("xT", [128, DC, LS])
        hT = sb("hT", [128, DC, LS], BF16)
        Wr = [sb("W0", [128, DC, 768], BF16), sb("W1", [128, DC, 768], BF16)]
        AR = sb("arena", [128, NAR])
        for i in range(6):
            pst = es.enter_context(nc.psum_tensor("ps%d" % i, [128, 512], F32))
            pg.psb.append((Buf("ps%d" % i), pst))
        psU = [es.enter_context(nc.psum_tensor("psU%d" % i, [128, 512], F32)) for i in range(2)]
        U_slots = [[(Buf("U%d%d" % (d, i)), psU[d][:, i * 256:(i + 1) * 256]) for i in range(2)] for d in range(2)]

        B_const = Buf("const")
        B_mod = Buf("mod")
        B_W = [Buf("W0"), Buf("W1")]
        wi = [0]

        def next_w():
            i = wi[0] % 2
            wi[0] += 1
            return B_W[i], Wr[i]

        ar = {"off": 0}

        def ar_reset():
            ar["off"] = 0

        pbufs = {}

        def pb(name):
            if name not in pbufs:
                pbufs[name] = Buf(name)
            return pbufs[name]

        def af(n, name="a"):
            o = ar["off"]
            ar["off"] += n
            assert ar["off"] <= NAR, ("arena overflow", name, ar["off"])
            ar["hw"] = max(ar.get("hw", 0), ar["off"])
            return AR[:, o:o + n], pb(name)

        def ab(n, name="a"):
            n2 = (n + 1) // 2
            o = ar["off"]
            ar["off"] += n2
            assert ar["off"] <= NAR, ("arena overflow", name, ar["off"])
            ar["hw"] = max(ar.get("hw", 0), ar["off"])
            return AR[:, o:o + n2].bitcast(BF16)[:, 0:n], pb(name)

        pg.dma("sp", ident[:], ident_d[:, :], dst=B_const)
        pg.dma("sp", cm[:], cm_d[:, :, :], dst=B_const)
        pg.dma("sp", mk[:], mk_d[:, :, :], dst=B_const)
        pg.dma("sp", ng[:], ng_d[:, :, :], dst=B_const)
        pg.dma("sp", fng[:], fng_d[:, :], dst=B_const)
        pg.dma("sp", gon[:], gon_d[:, :, :], dst=B_const)
        pg.op("pool", lambda e: e.memset(ones_bf[:], 1.0), writes=[B_const])

        cv = sb("cv", [128, DC * 2])
        cvb = sb("cvb", [128, DC * 2], BF16)
        modb = sb("modbs", [128, 4 * 24])
        mv = sb("mv", [128, 4 * 24 * 2])
        B_cv, B_cvb, B_modb = Buf("cv"), Buf("cvb"), Buf("modb")
        B_mvL = [Buf("mv%d" % l) for l in range(4)]
        B_modL = [Buf("mod%d" % l) for l in range(4)]
        cv3 = cv[:, :].rearrange("p (c m) -> p c m", m=2)
        cvb3 = cvb[:, :].rearrange("p (c m) -> p c m", m=2)
        modb3 = modb[:, :].rearrange("p (l e) -> p l e", e=24)
        mv4 = mv[:, :].rearrange("p (l e m) -> p l e m", l=4, e=24)
        pg.dma("sp", cv3, cv_d[:, :, :], dst=B_cv)
        pg.dma("sp", modb3, modb_d[:, :, :], dst=B_modb)
        pg.op("act", lambda e: e.activation(out=cvb[:, :], in_=cv[:, :], func=AF.Silu), reads=[B_cv], writes=[B_cvb])

        def emit_mod(l):
            psB, psT = pg.ps()
            for quarter in range(4):
                Bw, Wt = next_w()
                pg.dma("pool", Wt[:, :, :], modw_d[l, :, quarter * 768:(quarter + 1) * 768].rearrange("(c p) n -> p c n", p=128), dst=Bw)
                mms = []
                for e6 in range(6):
                    ec = quarter * 6 + e6
                    for kc in range(DC):
                        mms.append((psT[:, ec * 2:ec * 2 + 2], Wt[:, kc, e6 * 128:(e6 + 1) * 128], cvb3[:, kc, :],
                                    kc == 0, kc == DC - 1))
                pg.mm(psB, mms, reads=[Bw, B_cvb])
            pg.op("dve", lambda e: e.tensor_tensor(
                out=mv4[:, l], in0=psT[:, 0:48].rearrange("p (e m) -> p e m", m=2),
                in1=modb3[:, l, :].unsqueeze(2).to_broadcast([128, 24, 2]), op=ALU.add),
                reads=[psB, B_modb], writes=[B_mvL[l]])
            pg.op("dve", lambda e: e.tensor_copy(out=SH[:, l], in_=mv4[:, l, 0:8, :]), reads=[B_mvL[l]], writes=[B_modL[l]])
            pg.op("dve", lambda e: e.tensor_copy(out=GG[:, l], in_=mv4[:, l, 16:24, :]), reads=[B_mvL[l]], writes=[B_modL[l]])
            pg.op("dve", lambda e: e.tensor_scalar(out=AA[:, l], in0=mv4[:, l, 8:16, :], scalar1=1.0, scalar2=None,
                                                   op0=ALU.add), reads=[B_mvL[l]], writes=[B_modL[l]])
            pg.op("dve", lambda e: e.tensor_tensor(out=AA[:, l], in0=AA[:, l], in1=ng[:, l, :].unsqueeze(2).to_broadcast([128, DC, 2]),
                                                   op=ALU.mult), reads=[B_const, B_modL[l]], writes=[B_modL[l]])

        if nl > 0:
            emit_mod(0)

        def run_group(gi, x_d, y_d, T, nseq, L, m):
            ntt = T // 512
            nbt = T // 128
            nb = L // 128
            pg.barrier()
            ar_reset()
            pre = {"w": None}
            B_x = [pb("x%d" % i) for i in range(ntt)]
            B_h = [pb("h%d" % i) for i in range(ntt)]

            stg = [af(D, "stg%d" % i) for i in range(2)]
            for bt in range(nbt):
                s_ap, s_b = stg[bt % 2]
                pg.dma("sp", s_ap, x_d[bt * 128:(bt + 1) * 128, :], dst=s_b)
                for hf in range(2):
                    psB, psT = pg.ps()

                    def fn(e, s_ap=s_ap, psT=psT, hf=hf):
                        r = None
                        for c4 in range(4):
                            c = hf * 4 + c4
                            r = e.transpose(out=psT[:, c4 * 128:(c4 + 1) * 128], in_=s_ap[:, c * 128:(c + 1) * 128],
                                            identity=ident[:])
                        return r
                    pg.op("pe", fn, reads=[s_b, B_const], writes=[psB])
                    eng = "act" if hf == 0 else "dve"
                    o_ap = xT[:, hf * 4:(hf + 1) * 4, bt * 128:(bt + 1) * 128]
                    i_ap = psT[:, :].rearrange("p (c t) -> p c t", c=4)
                    if eng == "act":
                        pg.op("act", lambda e, o_ap=o_ap, i_ap=i_ap: e.activation(out=o_ap, in_=i_ap, func=AF.Copy),
                              reads=[psB], writes=[B_x[bt // 4]])
                    else:
                        pg.op("dve", lambda e, o_ap=o_ap, i_ap=i_ap: e.tensor_copy(out=o_ap, in_=i_ap),
                              reads=[psB], writes=[B_x[bt // 4]])

            ar_reset()

            def rms_rstd(tt, tmp_sq, B_sq, tmp_r, B_r):
                ts = slice(tt * 512, (tt + 1) * 512)
                sq3 = tmp_sq.rearrange("p (c t) -> p c t", c=DC)
                pg.op("act", lambda e: e.activation(out=sq3, in_=xT[:, :, ts], func=AF.Square),
                      reads=[B_x[tt]], writes=[B_sq])
                psB, psT = pg.ps()
                pg.mm(psB, [(psT[:, :], ones_bf[:], sq3[:, c, :], c == 0, c == DC - 1) for c in range(DC)],
                      reads=[B_sq, B_const])
                pg.op("act", lambda e: e.activation(out=tmp_r, in_=psT[:, :], func=AF.Ln, scale=1.0 / D, bias=EPS),
                      reads=[psB], writes=[B_r])
                pg.op("act", lambda e: e.activation(out=tmp_r, in_=tmp_r, func=AF.Exp, scale=-0.5),
                      reads=[B_r], writes=[B_r])

            def compute_h(li):
                mark = ar["off"]
                gens = []
                for tt in range(ntt):
                    sq, B_sq = ab(DC * 512, "sq%d" % tt)
                    rr, B_r = af(512, "rstd%d" % tt)
                    tmps = [af(512, "htmp%d_%d" % (tt, i)) for i in range(2)]
                    gens.append(h_tile(li, tt, sq, B_sq, rr, B_r, tmps))
                run_staged(gens)
                ar["off"] = mark
                pg.barrier()

            def h_tile(li, tt, sq, B_sq, rr, B_r, tmps):
                ts = slice(tt * 512, (tt + 1) * 512)
                sq3 = sq.rearrange("p (c t) -> p c t", c=DC)
                pg.op("act", lambda e: e.activation(out=sq3, in_=xT[:, :, ts], func=AF.Square),
                      reads=[B_x[tt]], writes=[B_sq])
                yield
                psB, psT = pg.ps()
                pg.mm(psB, [(psT[:, :], ones_bf[:], sq3[:, c, :], c == 0, c == DC - 1) for c in range(DC)],
                      reads=[B_sq, B_const])
                yield
                pg.op("act", lambda e: e.activation(out=rr, in_=psT[:, :], func=AF.Ln, scale=1.0 / D, bias=EPS),
                      reads=[psB], writes=[B_r])
                yield
                pg.op("act", lambda e: e.activation(out=rr, in_=rr, func=AF.Exp, scale=-0.5),
                      reads=[B_r], writes=[B_r])
                yield
                for c in range(DC):
                    t_ap, t_b = tmps[c % 2]
                    pg.op("dve", lambda e, t_ap=t_ap, c=c: e.tensor_tensor(
                        out=t_ap, in0=xT[:, c, ts], in1=rr, op=ALU.mult), reads=[B_x[tt], B_r], writes=[t_b])
                    pg.op("act", lambda e, t_ap=t_ap, c=c: e.activation(
                        out=hT[:, c, ts], in_=t_ap, func=AF.Identity, scale=AA[:, li, c, m:m + 1],
                        bias=SH[:, li, c, m:m + 1]), reads=[t_b, B_modL[li]], writes=[B_h[tt]])
                    yield

            def run_staged(gens):
                outs = [None] * len(gens)
                live = list(range(len(gens)))
                while live:
                    nxt = []
                    for gi_ in live:
                        try:
                            next(gens[gi_])
                            nxt.append(gi_)
                        except StopIteration as si:
                            outs[gi_] = si.value
                    live = nxt
                return outs

            def load_w_cols(dram2d, colspecs, queue="pool"):
                Bw, Wt = next_w()
                o = 0
                for (c0, n) in colspecs:
                    pg.dma(queue, Wt[:, :, o:o + n], dram2d[:, c0:c0 + n].rearrange("(c p) n -> p c n", p=128), dst=Bw)
                    o += n
                return Bw, Wt

            def out_proj_acc(li, wo_ap, B_wo, nk, yT_ap, B_y):
                for tt in range(ntt):
                    ts = slice(tt * 512, (tt + 1) * 512)
                    for dc in range(DC):
                        psB, psT = pg.ps()
                        pg.mm(psB, [(psT[:, :], wo_ap[:, k, dc * 128:(dc + 1) * 128], yT_ap[:, k, ts], k == 0, k == nk - 1)
                                    for k in range(nk)], reads=[B_wo] + (list(B_y) if isinstance(B_y, (list, tuple)) else [B_y]))
                        pg.op("dve", lambda e, psT=psT, dc=dc, ts=ts: e.scalar_tensor_tensor(
                            out=xT[:, dc, ts], in0=psT[:, :], scalar=GG[:, li, dc, m:m + 1], in1=xT[:, dc, ts],
                            op0=ALU.mult, op1=ALU.add), reads=[psB, B_modL[li]], writes=[B_x[tt]])

            def proj_fm(Bw, Wt, c0, dst_fn, reads_extra=()):
                for tt in range(ntt):
                    ts = slice(tt * 512, (tt + 1) * 512)
                    psB, psT = pg.ps()
                    pg.mm(psB, [(psT[:, :], Wt[:, c, c0:c0 + 128], hT[:, c, ts], c == 0, c == DC - 1) for c in range(DC)],
                          reads=[Bw, B_h[tt]])
                    dst_fn(tt, ts, psB, psT)

            def gla_layer(li, j):
                mark0 = ar["off"]
                lra, B_lra = ab(T, "lra")
                w2a, B_w2a = ab(1024, "w2a")
                wlr, B_wlr = ab(DC * 32, "wlr")
                wlr3 = wlr.rearrange("p (c n) -> p c n", c=DC)
                pg.dma("pool", w2a[0:33, :], gwdec_d[j, :, :], dst=B_w2a)
                pg.dma("pool", wlr3, gwin_d[j, :, 3072:3104].rearrange("(c p) n -> p c n", p=128), dst=B_wlr)
                pg.op("pool", lambda e: e.memset(lra[32:33, :], 1.0), writes=[B_lra])
                for tt in range(ntt):
                    ts = slice(tt * 512, (tt + 1) * 512)
                    psB, psT = pg.ps()
                    pg.mm(psB, [(psT[0:32, :], wlr3[:, c, :], hT[:, c, ts], c == 0, c == DC - 1) for c in range(DC)],
                          reads=[B_wlr, B_h[tt]])
                    pg.op("act", lambda e, psT=psT, ts=ts: e.activation(out=lra[0:32, ts], in_=psT[0:32, :], func=AF.Copy),
                          reads=[psB], writes=[B_lra])
                mark1 = ar["off"]
                for h in range(4):
                    ar["off"] = mark1
                    gwin = gwin_d[j]
                    if h == 0 and pre["w"] is not None:
                        Bw, Wt = pre["w"]
                        pre["w"] = None
                    else:
                        Bw, Wt = load_w_cols(gwin, [(h * 128, 128), (512 + h * 128, 128), (1024 + h * 256, 256),
                                                    (2048 + h * 256, 256)])
                    wo, B_wo = ab(2 * D, "wo")
                    wo3 = wo.rearrange("p (k n) -> p k n", k=2)
                    pg.dma("pool", wo3, gwout_d[j, h * 256:(h + 1) * 256, :].rearrange("(k p) n -> p k n", p=128), dst=B_wo)
                    qky, B_qky = ab(2 * T, "qky")
                    qk3 = qky.rearrange("p (a t) -> p a t", a=2)
                    kvt, B_kvt = ab(nbt * 384, "kvt")
                    kvt3 = kvt.rearrange("p (b n) -> p b n", n=384)
                    rs, B_rs = ab(2 * T, "rs")
                    rs3 = rs.rearrange("p (a t) -> p a t", a=2)
                    oT, _ = af(2 * L, "oT")
                    oT3 = oT.rearrange("p (a t) -> p a t", a=2)
                    B_o = [pb("o%d" % i) for i in range(nb)]
                    proj_fm(Bw, Wt, 0, lambda tt, ts, psB, psT: pg.op(
                        "act", lambda e: e.activation(out=qk3[:, 0, ts], in_=psT[:, :], func=AF.Copy, scale=128.0 ** -0.5),
                        reads=[psB], writes=[B_qky]))
                    proj_fm(Bw, Wt, 128, lambda tt, ts, psB, psT: pg.op(
                        "dve", lambda e: e.tensor_copy(out=qk3[:, 1, ts], in_=psT[:, :]), reads=[psB], writes=[B_qky]))
                    for a in range(2):
                        proj_fm(Bw, Wt, 512 + a * 128, lambda tt, ts, psB, psT, a=a: pg.op(
                            "act", lambda e: e.activation(out=rs3[:, a, ts], in_=psT[:, :], func=AF.Silu),
                            reads=[psB], writes=[B_rs]))
                    for bt in range(nbt):
                        bs = slice(bt * 128, (bt + 1) * 128)
                        psB, psT = pg.ps()
                        pg.mm(psB, [(psT[:, 0:384], hT[:, c, bs], Wt[:, c, 128:512], c == 0, c == DC - 1) for c in range(DC)],
                              reads=[Bw, B_h[bt // 4]])
                        pg.op("dve" if bt % 2 else "act",
                              (lambda e, psT=psT, bt=bt: e.tensor_copy(out=kvt3[:, bt, :], in_=psT[:, 0:384])) if bt % 2 else
                              (lambda e, psT=psT, bt=bt: e.activation(out=kvt3[:, bt, :], in_=psT[:, 0:384], func=AF.Copy)),
                              reads=[psB], writes=[B_kvt])
                    dbg("hT", hT[:, :, 0:512], B_h[0], BF16)
                    dbg("qk", qk3[:, :, 0:256], B_qky, BF16)
                    dbg("kvt", kvt3[:, 0:2, :], B_kvt, BF16)
                    dbg("rs", rs3[:, :, 0:256], B_rs, BF16)
                    dbg("lra", lra[0:33, 0:256], B_lra, BF16)
                    mark2 = ar["off"]
                    Sf = [af(256, "S%d" % d) for d in range(2)]
                    Sb = [[ab(256, "Sb%d%d" % (d, i)) for i in range(2)] for d in range(2)]
                    t1s = [[af(128, "t1_%d_%d" % (d, i)) for i in range(2)] for d in range(2)]
                    gps = [[ab(128, "gp%d_%d" % (d, i)) for i in range(2)] for d in range(2)]
                    E1s = [[af(129, "E1_%d_%d" % (d, i)) for i in range(2)] for d in range(2)]
                    E2s = [[af(128, "E2_%d_%d" % (d, i)) for i in range(2)] for d in range(2)]
                    E3s = [[af(128, "E3_%d_%d" % (d, i)) for i in range(2)] for d in range(2)]
                    qts = [[ab(128, "qt%d_%d" % (d, i)) for i in range(2)] for d in range(2)]
                    kts = [[ab(128, "kt%d_%d" % (d, i)) for i in range(2)] for d in range(2)]
                    khs = [[ab(128, "kh%d_%d" % (d, i)) for i in range(2)] for d in range(2)]
                    ats = [[ab(128, "at%d_%d" % (d, i)) for i in range(2)] for d in range(2)]
                    nt = min(L, 512)
                    sq, B_sq = ab(2 * nt, "osq")
                    sq3 = sq.rearrange("p (a t) -> p a t", a=2)
                    rr, B_rr = af(nt, "orstd")
                    tmp, B_tmp = af(2 * nt, "otmp")
                    tmp3 = tmp.rearrange("p (a t) -> p a t", a=2)
                    def gla_prep(s, step, d):
                        blk = step if d == 0 else nb - 1 - step
                        tok0 = s * L + blk * 128
                        tk = slice(tok0, tok0 + 128)
                        btg = tok0 // 128
                        pi = step % 2
                        t1, B_t1 = t1s[d][pi]
                        gp, B_gp = gps[d][pi]
                        E1, B_E1 = E1s[d][pi]
                        E2, B_E2 = E2s[d][pi]
                        E3, B_E3 = E3s[d][pi]
                        qt, B_qt = qts[d][pi]
                        kt, B_kt = kts[d][pi]
                        kh, B_kh = khs[d][pi]
                        at, B_at = ats[d][pi]
                        psB, psT = pg.ps()
                        c0 = d * 512 + h * 128
                        pg.mm(psB, [(psT[:, 0:128], lra[0:33, tk], w2a[0:33, c0:c0 + 128], True, True)],
                              reads=[B_lra, B_w2a])
                        yield
                        pg.op("act", lambda e: e.activation(out=t1, in_=psT[:, 0:128], func=AF.Exp, scale=-1.0),
                              reads=[psB], writes=[B_t1])
                        yield
                        pg.op("act", lambda e: e.activation(out=gp, in_=t1, func=AF.Ln, bias=1.0),
                              reads=[B_t1], writes=[B_gp])
                        yield
                        psB2, psT2 = pg.ps()
                        pg.mm(psB2, [(psT2[:, 0:129], gp, cm[:, 2 * d, 0:129], True, True),
                                     (psT2[:, 256:384], cm[:, 2 * d + 1, 0:128], gp, True, True)],
                              reads=[B_gp, B_const])
                        yield
                        pg.op("act", lambda e: e.activation(out=E1, in_=psT2[:, 0:129], func=AF.Exp, scale=-1.0 / 16),
                              reads=[psB2], writes=[B_E1])
                        pg.op("act", lambda e: e.activation(out=E2, in_=psT2[:, 0:128], func=AF.Exp, scale=1.0 / 16),
                              reads=[psB2], writes=[B_E2])
                        pg.op("act", lambda e: e.activation(out=E3, in_=psT2[:, 256:384], func=AF.Exp, scale=-1.0 / 16),
                              reads=[psB2], writes=[B_E3])
                        yield
                        pg.op("dve", lambda e: e.tensor_tensor(out=qt, in0=qk3[:, 0, tk], in1=E1[:, 0:128], op=ALU.mult),
                              reads=[B_qky, B_E1], writes=[B_qt])
                        pg.op("pool", lambda e: e.tensor_tensor(out=kt, in0=qk3[:, 1, tk], in1=E2, op=ALU.mult),
                              reads=[B_qky, B_E2], writes=[B_kt])
                        pg.op("pool", lambda e: e.tensor_tensor(out=kh, in0=kvt3[:, btg, 0:128], in1=E3, op=ALU.mult),
                              reads=[B_kvt, B_E3], writes=[B_kh])
                        yield
                        psB3, psT3 = pg.ps()
                        pg.mm(psB3, [(psT3[:, 0:128], kt, qt, True, True)], reads=[B_kt, B_qt])
                        psB5, psT5 = U_slots[d][pi]
                        pg.mm(psB5, [(psT5[:, 0:256], kh, kvt3[:, btg, 128:384], True, True)], reads=[B_kh, B_kvt])
                        yield
                        pg.op("dve", lambda e: e.tensor_tensor(out=at, in0=psT3[:, 0:128], in1=mk[:, d, :], op=ALU.mult),
                              reads=[psB3, B_const], writes=[B_at])
                        return (psB5, psT5)

                    def gla_state(s, step, d, ctx):
                        psB5, psT5 = ctx
                        blk = step if d == 0 else nb - 1 - step
                        tok0 = s * L + blk * 128
                        btg = tok0 // 128
                        pi = step % 2
                        E1, B_E1 = E1s[d][pi]
                        qt, B_qt = qts[d][pi]
                        at, B_at = ats[d][pi]
                        S_ap, S_b = Sf[d]
                        if step == 0:
                            if m == 1:
                                pg.dma("sp", S_ap, st_d[j, d, h, :, :], dst=S_b)
                            else:
                                pg.op("pool", lambda e: e.memset(S_ap, 0.0), writes=[S_b])
                            sb_ap, sb_b = Sb[d][0]
                            pg.op("act", lambda e: e.activation(out=sb_ap, in_=S_ap, func=AF.Copy),
                                  reads=[S_b], writes=[sb_b])
                        sbi_ap, sbi_b = Sb[d][step % 2]
                        sbo_ap, sbo_b = Sb[d][(step + 1) % 2]
                        psB4, psT4 = pg.ps()
                        mms = []
                        for hf in range(2):
                            mms.append((psT4[:, hf * 128:(hf + 1) * 128], kvt3[:, btg, 128 + hf * 128:256 + hf * 128], at, True, False))
                            mms.append((psT4[:, hf * 128:(hf + 1) * 128], sbi_ap[:, hf * 128:(hf + 1) * 128], qt, False, True))
                        pg.mm(psB4, mms, reads=[B_kvt, B_at, sbi_b, B_qt])
                        o_ap = oT3[:, :, blk * 128:(blk + 1) * 128]
                        p_ap = psT4[:, 0:256].rearrange("p (a t) -> p a t", a=2)
                        other = nb - 1 - step
                        is_first = (step <= other) if d == 0 else (step < other)
                        if is_first:
                            pg.op("act", lambda e: e.activation(out=o_ap, in_=p_ap, func=AF.Copy),
                                  reads=[psB4], writes=[B_o[blk]])
                        else:
                            pg.op("dve", lambda e: e.tensor_tensor(out=o_ap, in0=o_ap, in1=p_ap, op=ALU.add),
                                  reads=[psB4], writes=[B_o[blk]])
                        pg.op("dve", lambda e: e.scalar_tensor_tensor(
                            out=S_ap, in0=S_ap, scalar=E1[:, 128:129], in1=psT5[:, 0:256], op0=ALU.mult, op1=ALU.add),
                            reads=[psB5, B_E1], writes=[S_b])
                        if step < nb - 1:
                            pg.op("act", lambda e: e.activation(out=sbo_ap, in_=S_ap, func=AF.Copy),
                                  reads=[S_b], writes=[sbo_b])
                        elif m == 0:
                            pg.dma("sp", ns_d[s, j, d, h, :, :], S_ap, src=S_b, final=True)

                    for s in range(nseq):
                        for step0 in range(0, nb, 2):
                            chains = [(st, d) for st in (step0, step0 + 1) for d in range(2)]
                            ctxs = run_staged([gla_prep(s, st, d) for (st, d) in chains])
                            for (st, d), ctx in zip(chains, ctxs):
                                gla_state(s, st, d, ctx)
                        for t0 in range(0, L, nt):
                            blks = range(t0 // 128, (t0 + nt) // 128)
                            gs = slice(s * L + t0, s * L + t0 + nt)
                            ls = slice(t0, t0 + nt)
                            Bos = [B_o[b] for b in blks]
                            pg.op("act", lambda e, sq3=sq3, ls=ls: e.activation(out=sq3, in_=oT3[:, :, ls], func=AF.Square),
                                  reads=Bos, writes=[B_sq])
                            psB, psT = pg.ps()
                            pg.mm(psB, [(psT[:, 0:nt], ones_bf[:], sq3[:, a, :], a == 0, a == 1) for a in range(2)],
                                  reads=[B_sq, B_const])
                            pg.op("act", lambda e, rr=rr, psT=psT, nt=nt: e.activation(out=rr, in_=psT[:, 0:nt], func=AF.Ln, scale=1.0 / 256, bias=EPS),
                                  reads=[psB], writes=[B_rr])
                            pg.op("act", lambda e, rr=rr: e.activation(out=rr, in_=rr, func=AF.Exp, scale=-0.5),
                                  reads=[B_rr], writes=[B_rr])
                            pg.op("dve", lambda e, tmp3=tmp3, ls=ls, rr=rr, nt=nt: e.tensor_tensor(
                                out=tmp3, in0=oT3[:, :, ls], in1=rr.unsqueeze(1).to_broadcast([128, 2, nt]), op=ALU.mult),
                                reads=Bos + [B_rr], writes=[B_tmp])
                            for a in range(2):
                                pg.op("dve", lambda e, tmp3=tmp3, a=a, gs=gs: e.scalar_tensor_tensor(
                                    out=qk3[:, a, gs], in0=tmp3[:, a, :], scalar=gon[:, j, a:a + 1], in1=rs3[:, a, gs],
                                    op0=ALU.mult, op1=ALU.mult), reads=[B_tmp, B_rs, B_const], writes=[B_qky])
                    out_proj_acc(li, wo3, B_wo, 2, qk3, B_qky)
                ar["off"] = mark0
                pg.barrier()


            def fnet_layer(li):
                mark0 = ar["off"]
                fcs, B_fcs = ab(2 * 512, "fcs")
                fcs3 = fcs.rearrange("p (c n) -> p c n", c=2)
                pg.dma("sp", fcs3, fcs_d.rearrange("(c p) n -> p c n", p=128), dst=B_fcs)
                LT = min(256, L // 2)
                nlt = (L // 2) // LT
                fcol, B_fcol = ab(nb, "fcol")
                pg.dma("sp", fcol, fcol_d[L][:, :], dst=B_fcol)
                mark1 = ar["off"]
                for g in range(4):
                    ar["off"] = mark1
                    if g == 0 and pre["w"] is not None:
                        Bw, Wt = pre["w"]
                        pre["w"] = None
                    else:
                        Bw, Wt = load_w_cols(fwin_d, [(g * 256, 256), (1024 + g * 256, 256)])
                    wo, B_wo = ab(2 * D, "wo")
                    wo3 = wo.rearrange("p (k n) -> p k n", k=2)
                    pg.dma("pool", wo3, fwout_d[g * 256:(g + 1) * 256, :].rearrange("(k p) n -> p k n", p=128), dst=B_wo)
                    uy, B_uy = ab(2 * T, "uy")
                    uy3 = uy.rearrange("p (a t) -> p a t", a=2)
                    zs, B_zs = ab(2 * T, "zs")
                    zs3 = zs.rearrange("p (a t) -> p a t", a=2)
                    PQ, B_PQ = ab(nb * 512, "PQ")
                    PQ3 = PQ.rearrange("p (b n) -> p b n", n=512)
                    tabs = [ab(2 * nb * LT, "ftab%d" % i) for i in range(2)]
                    etA, B_etA = af(LT, "fetA")
                    ft1, B_ft1 = af(LT, "fft1")
                    ft2, B_ft2 = af(LT, "fft2")
                    for a in range(2):
                        proj_fm(Bw, Wt, a * 128, lambda tt, ts, psB, psT, a=a: pg.op(
                            "dve", lambda e: e.tensor_copy(out=uy3[:, a, ts], in_=psT[:, :]), reads=[psB], writes=[B_uy]))
                        proj_fm(Bw, Wt, 256 + a * 128, lambda tt, ts, psB, psT, a=a: pg.op(
                            "act", lambda e: e.activation(out=zs3[:, a, ts], in_=psT[:, :], func=AF.Silu),
                            reads=[psB], writes=[B_zs]))
                    ti = 0
                    for s in range(nseq):
                        for blk in range(nb):
                            tk = slice(s * L + blk * 128, s * L + blk * 128 + 128)
                            psB, psT = pg.ps()
                            pg.mm(psB, [(psT[:, :], uy3[:, cc, tk], fcs3[:, cc, :], cc == 0, cc == 1) for cc in range(2)],
                                  reads=[B_uy, B_fcs])
                            if blk % 2:
                                pg.op("dve", lambda e, psT=psT, blk=blk: e.tensor_copy(out=PQ3[:, blk, :], in_=psT[:, :]),
                                      reads=[psB], writes=[B_PQ])
                            else:
                                pg.op("act", lambda e, psT=psT, blk=blk: e.activation(out=PQ3[:, blk, :], in_=psT[:, :], func=AF.Copy),
                                      reads=[psB], writes=[B_PQ])
                        for lt in range(nlt):
                            tab_ap, tab_b = tabs[ti % 2]
                            ti += 1
                            tab4 = tab_ap.rearrange("p (a c n) -> p a c n", a=2, c=nb)
                            pg.dma("sp", tab_ap, ftab_d[L][lt, :, :], dst=tab_b)
                            t0 = lt * LT
                            gs = slice(s * L + t0, s * L + t0 + LT)
                            c1 = 1 if t0 == 0 else 0
                            n2 = LT - c1
                            jlo = s * L + L - t0 - LT + 1
                            for cc in range(2):
                                psAB, psA = pg.ps()
                                pg.mm(psAB, [(psA[:, 0:LT], PQ3[:, lc, cc * 128:(cc + 1) * 128], tab4[:, 0, lc, :], lc == 0, lc == nb - 1)
                                             for lc in range(nb)], reads=[B_PQ, tab_b])
                                psBB, psBt = pg.ps()
                                pg.mm(psBB, [(psBt[:, 0:LT], PQ3[:, lc, 256 + cc * 128:256 + (cc + 1) * 128], tab4[:, 1, lc, :], lc == 0, lc == nb - 1)
                                             for lc in range(nb)], reads=[B_PQ, tab_b])
                                pg.op("act", lambda e, psA=psA: e.activation(out=etA, in_=psA[:, 0:LT], func=AF.Copy),
                                      reads=[psAB], writes=[B_etA])
                                pg.op("dve", lambda e, psBt=psBt: e.tensor_tensor(out=ft1, in0=etA, in1=psBt[:, 0:LT], op=ALU.add),
                                      reads=[B_etA, psBB], writes=[B_ft1])
                                pg.op("dve", lambda e, cc=cc, gs=gs: e.tensor_tensor(out=uy3[:, cc, gs], in0=ft1, in1=zs3[:, cc, gs], op=ALU.mult),
                                      reads=[B_ft1, B_zs], writes=[B_uy])
                                pg.op("dve", lambda e, psBt=psBt, c1=c1, n2=n2: e.tensor_tensor(
                                    out=ft2[:, 0:n2], in0=etA[:, c1:LT][:, ::-1], in1=psBt[:, c1:LT][:, ::-1], op=ALU.subtract),
                                    reads=[B_etA, psBB], writes=[B_ft2])
                                pg.op("dve", lambda e, cc=cc, jlo=jlo, n2=n2: e.tensor_tensor(
                                    out=uy3[:, cc, jlo:jlo + n2], in0=ft2[:, 0:n2], in1=zs3[:, cc, jlo:jlo + n2], op=ALU.mult),
                                    reads=[B_ft2, B_zs], writes=[B_uy])
                        hh = s * L + L // 2
                        for cc in range(2):
                            psB, psT = pg.ps()
                            pg.mm(psB, [(psT[:, 0:1], PQ3[:, lc, cc * 128:(cc + 1) * 128], fcol[:, lc:lc + 1], lc == 0, lc == nb - 1)
                                        for lc in range(nb)], reads=[B_PQ, B_fcol])
                            pg.op("dve", lambda e, psT=psT, cc=cc, hh=hh: e.tensor_tensor(
                                out=uy3[:, cc, hh:hh + 1], in0=psT[:, 0:1], in1=zs3[:, cc, hh:hh + 1], op=ALU.mult),
                                reads=[psB, B_zs], writes=[B_uy])
                    out_proj_acc(li, wo3, B_wo, 2, uy3, B_uy)
                ar["off"] = mark0
                pg.barrier()

            def hyena_layer(li):
                mark0 = ar["off"]
                PI = math.pi
                nkc = (L + 1 + 127) // 128
                NT = min(L, 512)
                a3, B_a3 = ab(L, "a3")
                hbt, B_hb = af(8, "hbt")
                hws = [af(64, "hw%d" % i) for i in range(3)]
                tcol, B_small = af(nb, "tcol")
                bm, _ = af(1, "bm")
                wk, _ = af(nkc, "wk")
                hcw, _ = af(72, "hcw")
                hcw3 = hcw.rearrange("p (k c) -> p k c", k=3)
                hcb, _ = af(24, "hcb")
                hd, _ = af(16, "hd")
                hd3 = hd.rearrange("p (n c) -> p n c", n=2)
                pg.dma("sp", hbt[0:64, 0:4], hb_d[:, :], dst=B_hb)
                pg.dma("sp", hws[0][0][0:33, :], hw1_d[:, :], dst=hws[0][1])
                pg.dma("sp", hws[1][0][0:64, :], hw2_d[:, :], dst=hws[1][1])
                pg.dma("sp", hws[2][0][0:64, :], hw3_d[:, :], dst=hws[2][1])
                pg.dma("sp", tcol, tcol_d[L][:, :], dst=B_small)
                pg.dma("sp", bm, bm_d[:, :], dst=B_small)
                pg.dma("sp", wk, wk_d[L][:, :], dst=B_small)
                pg.dma("sp", hcw3, hcw_d[:, :, :], dst=B_small)
                pg.dma("sp", hcb, hcb_d[:, :], dst=B_small)
                pg.dma("sp", hd3, hd_d[:, :, :], dst=B_small)
                pg.op("dve", lambda e: e.tensor_scalar(out=hbt[0:64, 4:7], in0=hbt[0:64, 0:3], scalar1=hbt[0:64, 3:4], scalar2=None,
                                                       op0=ALU.mult), reads=[B_hb], writes=[B_hb])
                markf = ar["off"]
                zp, B_zp = af(L, "zp")
                aA, B_aA = af(L, "aA")
                aB, B_aB = af(L, "aB")
                arg, B_arg = af(512, "arg")
                mw, B_mw = af(512, "mwrap")
                pg.dma("sp", zp[0:33, :], zpos_d[L][:, :], dst=B_zp)
                srcs = [(zp, B_zp, 33), (aA, B_aA, 64), (aB, B_aB, 64)]
                dsts = [(aA, B_aA), (aB, B_aB), (a3, B_a3)]
                for i in range(3):
                    s_ap, s_b, K = srcs[i]
                    d_ap, d_b = dsts[i]
                    w_ap, w_b = hws[i]
                    for t0 in range(0, L, 512):
                        n = min(512, L - t0)
                        psB, psT = pg.ps()
                        pg.mm(psB, [(psT[0:64, 0:n], w_ap[0:K, :], s_ap[0:K, t0:t0 + n], True, True)], reads=[w_b, s_b])
                        pg.op("act", lambda e, psT=psT, n=n, i=i: e.activation(out=arg[0:64, 0:n], in_=psT[0:64, 0:n], func=AF.Identity,
                                                                            scale=hbt[0:64, 3:4], bias=hbt[0:64, 4 + i:5 + i]),
                              reads=[psB, B_hb], writes=[B_arg])
                        pg.op("dve", lambda e, n=n: e.tensor_scalar(out=mw[0:64, 0:n], in0=arg[0:64, 0:n], scalar1=PI, scalar2=2 * PI,
                                                                    op0=ALU.is_gt, op1=ALU.mult), reads=[B_arg], writes=[B_mw])
                        pg.op("dve", lambda e, n=n: e.tensor_tensor(out=arg[0:64, 0:n], in0=arg[0:64, 0:n], in1=mw[0:64, 0:n],
                                                                    op=ALU.subtract), reads=[B_mw], writes=[B_arg])
                        pg.op("dve", lambda e, n=n: e.tensor_scalar(out=mw[0:64, 0:n], in0=arg[0:64, 0:n], scalar1=-PI, scalar2=2 * PI,
                                                                    op0=ALU.is_lt, op1=ALU.mult), reads=[B_arg], writes=[B_mw])
                        pg.op("dve", lambda e, n=n: e.tensor_tensor(out=arg[0:64, 0:n], in0=arg[0:64, 0:n], in1=mw[0:64, 0:n],
                                                                    op=ALU.add), reads=[B_mw], writes=[B_arg])
                        pg.op("act", lambda e, n=n, d_ap=d_ap, t0=t0: e.activation(out=d_ap[0:64, t0:t0 + n], in_=arg[0:64, 0:n], func=AF.Sin),
                              reads=[B_arg], writes=[d_b])
                ar["off"] = markf
                pg.barrier()
                resident = (L <= 256)
                NT2 = min(L // 2, 512)
                ntile2 = (L // 2) // NT2
                nodd = (L // 2) // 128
                if resident:
                    NFR, NIR = 2 * nkc, 2 * nkc * ntile2
                else:
                    NFR, NIR = 5, 8
                ftr = [ab(nb * 128, "ftr%d" % i) for i in range(NFR)]
                itr = [ab(NT2, "itr%d" % i) for i in range(NIR)]
                hcol, B_hcol = ab(nkc * 2, "hcol")
                pg.dma("sp", hcol, htic_d[L][:, :], dst=B_hcol)
                fi = [0]
                ii = [0]
                fcache = {}
                icache = {}

                def get_ftab(a, kc):
                    if resident and (a, kc) in fcache:
                        return fcache[(a, kc)]
                    f_ap, f_b = ftr[fi[0] % NFR]
                    fi[0] += 1
                    if not (os.environ.get("KNODMA") and fi[0] > 3):
                        pg.dma("sp", f_ap, htf_d[L][a, kc, :, :], dst=f_b)
                    r = (f_ap.rearrange("p (c n) -> p c n", c=nb), f_b)
                    fcache[(a, kc)] = r
                    return r

                def get_itab(kc, ti, a):
                    if resident and (kc, ti, a) in icache:
                        return icache[(kc, ti, a)]
                    i_ap, i_b = itr[ii[0] % NIR]
                    ii[0] += 1
                    if not (os.environ.get("KNODMA") and ii[0] > 2):
                        pg.dma("sp", i_ap, hti_d[L][kc, ti, :, a * NT2:(a + 1) * NT2], dst=i_b)
                    r = (i_ap, i_b)
                    icache[(kc, ti, a)] = r
                    return r

                mark1 = ar["off"]
                for cb in range(8):
                    ar["off"] = mark1
                    if cb == 0 and pre["w"] is not None:
                        Bw, Wt = pre["w"]
                        pre["w"] = None
                    else:
                        Bw, Wt = load_w_cols(hwin_d, [(cb * 128, 128), (1024 + cb * 128, 128), (2048 + cb * 128, 128), (3072 + cb * 128, 128)])
                    wo, B_wo = ab(D, "wo")
                    wo3 = wo.rearrange("p (k n) -> p k n", k=1)
                    pg.dma("pool", wo3[:, 0, :], hwout_d[cb * 128:(cb + 1) * 128, :], dst=B_wo)
                    w4c, B_w4c = ab(512, "w4c")
                    w4c3 = w4c.rearrange("p (a n) -> p a n", a=4)
                    for nd in range(4):
                        pg.dma("pool", w4c3[0:64, nd, :], hw4_d[:, nd * 1024 + cb * 128:nd * 1024 + (cb + 1) * 128], dst=B_w4c)
                    dlc, B_dlc = af(128, "dlc")
                    pg.dma("sp", dlc, delta_d[:, cb * 128:(cb + 1) * 128], dst=B_dlc)
                    G2, B_G2 = ab(T, "G2")
                    G23 = G2.rearrange("p (k t) -> p k t", k=1)
                    B_G2s = [pb("G2s%d" % s_) for s_ in range(nseq)]

                    def hy_seq(s, cb=cb, Bw=Bw, Wt=Wt, w4c3=w4c3, B_w4c=B_w4c, dlc=dlc, B_dlc=B_dlc, G2=G2, B_G2s=B_G2s):
                        sfx = "_s%d" % s
                        YAB, B_AB = ab(nb * 384, "YAB" + sfx)
                        YAB3 = YAB.rearrange("p (b n) -> p b n", n=384)
                        B_ytok = pb("ytok" + sfx)
                        tFs = [af(128, "tF%d%s" % (i, sfx)) for i in range(2)]
                        tBs = [af(128, "tB%d%s" % (i, sfx)) for i in range(2)]
                        dws = [af(128, "dw%d%s" % (i, sfx)) for i in range(2)]
                        raw = YAB.bitcast(F32)[:, 0:L + 2]
                        B_raw = pb("raw" + sfx)
                        yT, B_y = af(L, "yT" + sfx)
                        R1, B_R1 = af(max(L, nkc * 128), "R1" + sfx)
                        ctmp = R1[:, 0:L]
                        Zb = R1[:, 0:nkc * 128].bitcast(BF16).rearrange("p (k a n) -> p k a n", k=nkc, a=2)
                        G1, B_G1 = ab(L, "G1" + sfx)
                        etmp, B_etmp = af(NT2, "etmp" + sfx)
                        Xs, B_Xs = af(256, "Xs" + sfx)
                        Hs, B_Hs = af(256, "Hs" + sfx)
                        ttb, _ = af(max(512, NT2), "ttb" + sfx)
                        tt4 = [(ttb[:, i * 128:(i + 1) * 128], pb("tt%d%s" % (i, sfx))) for i in range(4)]
                        etmp2, B_etmp2 = ttb[:, 0:NT2], pb("etmp2" + sfx)
                        B_G2 = B_G2s[s]
                        x2c = G2[:, s * L:(s + 1) * L]
                        yield

                        def proj_seq(c0, evac):
                            for t0 in range(0, L, NT):
                                gs = slice(s * L + t0, s * L + t0 + NT)
                                psB, psT = pg.ps()
                                pg.mm(psB, [(psT[:, 0:NT], Wt[:, c, c0:c0 + 128], hT[:, c, gs], c == 0, c == DC - 1) for c in range(DC)],
                                      reads=[Bw] + B_h)
                                yield
                                evac(t0, psB, psT)
                                yield

                        def shortconv(ci, dst_ap, dst_b):
                            ch = ci * 8 + cb
                            pg.op("dve", lambda e: e.tensor_scalar(out=ctmp, in0=raw[:, 1:L + 1], scalar1=hcw3[:, 1, ch:ch + 1],
                                                                   scalar2=hcb[:, ch:ch + 1], op0=ALU.mult, op1=ALU.add),
                                  reads=[B_raw, B_small], writes=[B_R1])
                            yield
                            pg.op("dve", lambda e: e.scalar_tensor_tensor(out=ctmp, in0=raw[:, 0:L], scalar=hcw3[:, 0, ch:ch + 1], in1=ctmp,
                                                                          op0=ALU.mult, op1=ALU.add), reads=[B_raw, B_small], writes=[B_R1])
                            yield
                            pg.op("dve", lambda e: e.scalar_tensor_tensor(out=dst_ap, in0=raw[:, 2:L + 2], scalar=hcw3[:, 2, ch:ch + 1], in1=ctmp,
                                                                          op0=ALU.mult, op1=ALU.add), reads=[B_raw, B_small, B_R1], writes=[dst_b])
                            yield

                        dsts = [(G1, B_G1), (x2c, B_G2), (yT, B_y)]
                        pg.op("pool", lambda e: e.memset(raw[:, 0:1], 0.0), writes=[B_raw, B_AB, B_ytok])
                        pg.op("pool", lambda e: e.memset(raw[:, L + 1:L + 2], 0.0), writes=[B_raw, B_AB, B_ytok])
                        yield
                        for ci in range(3):
                            yield from proj_seq(ci * 128, lambda t0, psB, psT: pg.op(
                                "act", lambda e: e.activation(out=raw[:, 1 + t0:1 + t0 + NT], in_=psT[:, 0:NT], func=AF.Copy),
                                reads=[psB], writes=[B_raw]))
                            yield from shortconv(ci, dsts[ci][0], dsts[ci][1])
                        pg.op("pool", lambda e: e.memset(raw[:, 0:1], 0.0), reads=[B_raw], writes=[B_AB, B_ytok])
                        yield
                        for n in range(2):
                            gate_ap, gate_b = (G1, B_G1) if n == 0 else (x2c, B_G2)
                            for blk in range(nb):
                                tF, B_tF = tFs[blk % 2]
                                tB, B_tB = tBs[blk % 2]
                                dw, B_dw = dws[blk % 2]
                                psB, psT = pg.ps()
                                pg.mm(psB, [(psT[:, 0:256], a3[0:64, blk * 128:(blk + 1) * 128],
                                             w4c3[0:64, 2 * n:2 * n + 2, :], True, True)], reads=[B_a3, B_w4c])
                                pg.op("act", lambda e, blk=blk, dw=dw: e.activation(out=dw, in_=dlc, func=AF.Exp, scale=tcol[:, blk:blk + 1]),
                                      reads=[B_dlc, B_small], writes=[B_dw])
                                yield
                                pg.op("dve", lambda e, psT=psT, tF=tF, dw=dw: e.tensor_tensor(out=tF, in0=psT[:, 0:128], in1=dw, op=ALU.mult),
                                      reads=[psB, B_dw], writes=[B_tF])
                                if blk == 0:
                                    pg.op("dve", lambda e, psT=psT, tB=tB, dw=dw: e.scalar_tensor_tensor(out=tB, in0=psT[:, 128:256], scalar=bm[:, 0:1], in1=dw,
                                                                                                     op0=ALU.mult, op1=ALU.mult),
                                          reads=[psB, B_dw, B_small], writes=[B_tB])
                                else:
                                    pg.op("dve", lambda e, psT=psT, tB=tB, dw=dw: e.tensor_tensor(out=tB, in0=psT[:, 128:256], in1=dw, op=ALU.mult),
                                          reads=[psB, B_dw], writes=[B_tB])
                                yield
                                pg.op("pool", lambda e, blk=blk, tF=tF, tB=tB: e.tensor_tensor(out=YAB3[:, blk, 0:128], in0=tF, in1=tB, op=ALU.add),
                                      reads=[B_tF, B_tB], writes=[B_AB])
                                pg.op("pool", lambda e, blk=blk, tF=tF, tB=tB: e.tensor_tensor(out=YAB3[:, blk, 256:384], in0=tF, in1=tB, op=ALU.subtract),
                                      reads=[B_tF, B_tB], writes=[B_AB])
                                yield
                            for b0 in range(0, nb, 4):
                                nbb = min(4, nb - b0)
                                psB, psT = pg.ps()

                                def fn(e, psT=psT, b0=b0, nbb=nbb):
                                    r = None
                                    for bb in range(nbb):
                                        r = e.transpose(out=psT[:, bb * 128:(bb + 1) * 128], in_=yT[:, (b0 + bb) * 128:(b0 + bb + 1) * 128],
                                                        identity=ident[:])
                                    return r
                                pg.op("pe", fn, reads=[B_y, B_const], writes=[psB])
                                yield
                                pg.op("act", lambda e, psT=psT, b0=b0, nbb=nbb: e.activation(
                                    out=YAB3[:, b0:b0 + nbb, 128:256], in_=psT[:, 0:nbb * 128].rearrange("p (b n) -> p b n", n=128), func=AF.Copy),
                                    reads=[psB], writes=[B_ytok])
                                yield
                            for kc in range(nkc):
                                kn = min(128, L + 1 - kc * 128)
                                tabs2 = [get_ftab(a, kc) for a in range(2)]
                                psB, psT = pg.ps()
                                mms = []
                                for sc in range(nb):
                                    mms.append((psT[0:kn, 0:256], tabs2[0][0][:, sc, 0:kn], YAB3[:, sc, 0:256], sc == 0, sc == nb - 1))
                                for sc in range(nb):
                                    mms.append((psT[0:kn, 256:512], tabs2[1][0][:, sc, 0:kn], YAB3[:, sc, 128:384], sc == 0, sc == nb - 1))
                                pg.mm(psB, mms, reads=[tabs2[0][1], tabs2[1][1], B_ytok, B_AB])
                                yield
                                pg.op("act", lambda e, psT=psT, kn=kn: e.activation(out=Xs[0:kn, :], in_=psT[0:kn, 128:384], func=AF.Copy),
                                      reads=[psB], writes=[B_Xs])
                                pg.op("act", lambda e, psT=psT, kn=kn, kc=kc: e.activation(out=Hs[0:kn, 0:128], in_=psT[0:kn, 0:128], func=AF.Copy,
                                                                                        scale=wk[0:kn, kc:kc + 1]),
                                      reads=[psB, B_small], writes=[B_Hs])
                                pg.op("act", lambda e, psT=psT, kn=kn, kc=kc: e.activation(out=Hs[0:kn, 128:256], in_=psT[0:kn, 384:512], func=AF.Copy,
                                                                                        scale=wk[0:kn, kc:kc + 1]),
                                      reads=[psB, B_small], writes=[B_Hs])
                                yield
                                if kc == 0:
                                    dbg("hy_Xs", Xs, B_Xs)
                                    dbg("hy_Hs", Hs, B_Hs)
                                (t1, b1), (t2, b2), (t3, b3), (t4, b4) = tt4
                                pg.op("dve", lambda e, kn=kn, t1=t1: e.tensor_tensor(out=t1[0:kn, :], in0=Xs[0:kn, 0:128], in1=Hs[0:kn, 0:128], op=ALU.mult),
                                      reads=[B_Xs, B_Hs], writes=[b1])
                                pg.op("dve", lambda e, kn=kn, t2=t2: e.tensor_tensor(out=t2[0:kn, :], in0=Xs[0:kn, 128:256], in1=Hs[0:kn, 128:256], op=ALU.mult),
                                      reads=[B_Xs, B_Hs], writes=[b2])
                                pg.op("pool", lambda e, kn=kn, t3=t3: e.tensor_tensor(out=t3[0:kn, :], in0=Xs[0:kn, 0:128], in1=Hs[0:kn, 128:256], op=ALU.mult),
                                      reads=[B_Xs, B_Hs], writes=[b3])
                                pg.op("pool", lambda e, kn=kn, t4=t4: e.tensor_tensor(out=t4[0:kn, :], in0=Xs[0:kn, 128:256], in1=Hs[0:kn, 0:128], op=ALU.mult),
                                      reads=[B_Xs, B_Hs], writes=[b4])
                                yield
                                pg.op("dve", lambda e, kn=kn, kc=kc, t1=t1, t2=t2: e.tensor_tensor(out=Zb[0:kn, kc, 0, :], in0=t1[0:kn, :], in1=t2[0:kn, :], op=ALU.subtract),
                                      reads=[b1, b2], writes=[B_R1])
                                pg.op("pool", lambda e, kn=kn, kc=kc, t3=t3, t4=t4: e.tensor_tensor(out=Zb[0:kn, kc, 1, :], in0=t3[0:kn, :], in1=t4[0:kn, :], op=ALU.add),
                                      reads=[b3, b4], writes=[B_R1])
                                yield
                            dbg("hy_Z", R1[:, 0:nkc * 128].bitcast(BF16), B_R1, BF16)
                            dcol = hd3[:, n, cb:cb + 1]
                            for ti in range(ntile2):
                                t0 = ti * NT2
                                psEB, psE = pg.ps()
                                psOB, psO = pg.ps()
                                for (pB, pT, isE) in ((psEB, psE, True), (psOB, psO, False)):
                                    for kc in range(nkc):
                                        kn = min(128, L + 1 - kc * 128)
                                        odd = kc < nodd
                                        a = (1 if odd else 0) if isE else (0 if odd else 1)
                                        i_ap, i_b = get_itab(kc, ti, a)
                                        pg.mm(pB, [(pT[:, 0:NT2], Zb[0:kn, kc, a, :], i_ap[0:kn, :], kc == 0, kc == nkc - 1)],
                                              reads=[B_R1, i_b])
                                pg.op("dve", lambda e, psE=psE, t0=t0, dcol=dcol: e.scalar_tensor_tensor(
                                    out=etmp[:, 0:NT2], in0=yT[:, t0:t0 + NT2], scalar=dcol, in1=psE[:, 0:NT2],
                                    op0=ALU.mult, op1=ALU.add), reads=[psEB, B_y, B_small], writes=[B_etmp])
                                pg.op("dve", lambda e, psO=psO: e.tensor_tensor(
                                    out=etmp[:, 0:NT2], in0=etmp[:, 0:NT2], in1=psO[:, 0:NT2], op=ALU.add),
                                    reads=[psOB], writes=[B_etmp])
                                dbg("hy_etmp", etmp[:, 0:NT2], B_etmp)
                                if resident:
                                    for q_ in range(6):
                                        dbg("hy_itr%d" % q_, itr[q_][0], itr[q_][1], BF16)
                                pg.op("dve", lambda e, t0=t0, gate_ap=gate_ap: e.tensor_tensor(
                                    out=yT[:, t0:t0 + NT2], in0=etmp[:, 0:NT2], in1=gate_ap[:, t0:t0 + NT2], op=ALU.mult),
                                    reads=[B_etmp, gate_b], writes=[B_y])
                                c1 = 1 if t0 == 0 else 0
                                n2 = NT2 - c1
                                jlo = L - t0 - NT2 + 1
                                pg.op("dve", lambda e, psE=psE, jlo=jlo, n2=n2, c1=c1, dcol=dcol: e.scalar_tensor_tensor(
                                    out=etmp2[:, 0:n2], in0=yT[:, jlo:jlo + n2], scalar=dcol, in1=psE[:, c1:NT2][:, ::-1],
                                    op0=ALU.mult, op1=ALU.add), reads=[psEB, B_y, B_small], writes=[B_etmp2])
                                pg.op("dve", lambda e, psO=psO, n2=n2, c1=c1: e.tensor_tensor(
                                    out=etmp2[:, 0:n2], in0=etmp2[:, 0:n2], in1=psO[:, c1:NT2][:, ::-1], op=ALU.subtract),
                                    reads=[psOB], writes=[B_etmp2])
                                pg.op("dve", lambda e, jlo=jlo, n2=n2, gate_ap=gate_ap: e.tensor_tensor(
                                    out=yT[:, jlo:jlo + n2], in0=etmp2[:, 0:n2], in1=gate_ap[:, jlo:jlo + n2], op=ALU.mult),
                                    reads=[B_etmp2, gate_b], writes=[B_y])
                                yield
                            psB, psT = pg.ps()
                            mms = []
                            for kc in range(nkc):
                                kn = min(128, L + 1 - kc * 128)
                                for a in range(2):
                                    mms.append((psT[:, 0:1], Zb[0:kn, kc, a, :], hcol[0:kn, kc * 2 + a:kc * 2 + a + 1],
                                                kc == 0 and a == 0, kc == nkc - 1 and a == 1))
                            pg.mm(psB, mms, reads=[B_R1, B_hcol])
                            yield
                            hh = L // 2
                            pg.op("dve", lambda e, psT=psT, dcol=dcol: e.scalar_tensor_tensor(
                                out=etmp[:, 0:1], in0=yT[:, hh:hh + 1], scalar=dcol, in1=psT[:, 0:1],
                                op0=ALU.mult, op1=ALU.add), reads=[psB, B_y, B_small], writes=[B_etmp])
                            pg.op("dve", lambda e, gate_ap=gate_ap: e.tensor_tensor(
                                out=yT[:, hh:hh + 1], in0=etmp[:, 0:1], in1=gate_ap[:, hh:hh + 1], op=ALU.mult),
                                reads=[B_etmp, gate_b], writes=[B_y])
                            yield
                            dbg("hy_y1", yT, B_y)
                            if n == 0:
                                yield from proj_seq(3 * 128, lambda t0, psB, psT: pg.op(
                                    "act", lambda e: e.activation(out=G1[:, t0:t0 + NT], in_=psT[:, 0:NT], func=AF.Silu),
                                    reads=[psB], writes=[B_G1]))
                        pg.op("dve", lambda e: e.tensor_tensor(out=x2c, in0=yT, in1=G1, op=ALU.mult),
                              reads=[B_y, B_G1], writes=[B_G2])
                        yield

                    run_staged([hy_seq(s_) for s_ in range(nseq)])
                    B_G2 = B_G2s
                    out_proj_acc(li, wo3, B_wo, 1, G23, B_G2s)
                ar["off"] = mark0
                pg.barrier()

            for li in range(nl):
                pg.barrier()
                kind, j = li % 3, li // 3
                if not (gi == 0 and li + 1 < nl):
                    if kind == 0:
                        pre["w"] = load_w_cols(gwin_d[j], [(0, 128), (512, 128), (1024, 256), (2048, 256)])
                    elif kind == 1:
                        pre["w"] = load_w_cols(fwin_d, [(0, 256), (1024, 256)])
                    else:
                        pre["w"] = load_w_cols(hwin_d, [(0, 128), (1024, 128), (2048, 128), (3072, 128)])
                compute_h(li)
                if gi == 0 and li + 1 < nl:
                    emit_mod(li + 1)
                ar["hw"] = 0
                if kind == 0:
                    gla_layer(li, j)
                elif kind == 1:
                    fnet_layer(li)
                else:
                    hyena_layer(li)
                print("arena high-water group", gi, "layer", li, ar["hw"], "/", NAR)

            pg.barrier()
            mark = ar["off"]
            sq, B_sq = ab(DC * 512, "fsq")
            rr, B_r = af(512, "frstd")
            tmps = [af(512, "ftmp%d" % i) for i in range(2)]
            osts = [af(D, "ost%d" % i) for i in range(2)]
            for tt in range(ntt):
                ts = slice(tt * 512, (tt + 1) * 512)
                rms_rstd(tt, sq, B_sq, rr, B_r)
                for c in range(DC):
                    pg.op("dve", lambda e, c=c, ts=ts: e.scalar_tensor_tensor(
                        out=xT[:, c, ts], in0=xT[:, c, ts], scalar=fng[:, c:c + 1], in1=rr, op0=ALU.mult, op1=ALU.mult),
                        reads=[B_r, B_const], writes=[B_x[tt]])
                for b4 in range(4):
                    bt = tt * 4 + b4
                    o_ap, o_b = osts[bt % 2]
                    for hf in range(2):
                        psB, psT = pg.ps()

                        def fn(e, psT=psT, hf=hf, bt=bt):
                            r = None
                            for c4 in range(4):
                                c = hf * 4 + c4
                                r = e.transpose(out=psT[:, c4 * 128:(c4 + 1) * 128], in_=xT[:, c, bt * 128:(bt + 1) * 128],
                                                identity=ident[:])
                            return r
                        pg.op("pe", fn, reads=[B_x[tt], B_const], writes=[psB])
                        if hf == 0:
                            pg.op("act", lambda e, o_ap=o_ap, psT=psT: e.activation(out=o_ap[:, 0:512], in_=psT[:, :], func=AF.Copy),
                                  reads=[psB], writes=[o_b])
                        else:
                            pg.op("dve", lambda e, o_ap=o_ap, psT=psT: e.tensor_copy(out=o_ap[:, 512:1024], in_=psT[:, :]),
                                  reads=[psB], writes=[o_b])
                    pg.dma("sp", y_d[bt * 128:(bt + 1) * 128, :], o_ap, src=o_b, final=True)
            ar["off"] = mark

        run_group(0, xp_d, yp_d, 4 * LP, 4, LP, 0)
        run_group(1, xs_d, ys_d, LS, 1, LS, 1)

        pg._wait("sp", pg.final)
        pg.barrier()

        with nc.Block() as block:
            @block.tensor
            def _(e):
                for f in pg.streams["pe"]:
                    f(e)

            @block.scalar
            def _(e):
                for f in pg.streams["act"]:
                    f(e)

            @block.vector
            def _(e):
                for f in pg.streams["dve"]:
                    f(e)

            @block.gpsimd
            def _(e):
                for f in pg.streams["pool"]:
                    f(e)

            @block.sync
            def _(e):
                for f in pg.streams["sp"]:
                    f(e)
        print("instructions:", pg.ninstr, "dma sems:", pg.nsem)
    return nc


def _consts():
    bf = ml_dtypes.bfloat16
    c = {}
    c["ident"] = np.eye(128, dtype=np.float32)
    s = np.arange(128)[:, None]
    t = np.arange(128)[None, :]
    cm = np.zeros((128, 4, 129), np.float32)
    cm[:, 0, :128] = (s <= t)
    cm[:, 0, 128] = 1.0
    cm[:, 1, :128] = (s > t)
    cm[:, 2, :128] = (s >= t)
    cm[:, 2, 128] = 1.0
    cm[:, 3, :128] = (s < t)
    c["cm"] = cm.astype(bf)
    mk = np.zeros((128, 2, 128), np.float32)
    mk[:, 0] = (s <= t)
    mk[:, 1] = (s >= t)
    c["mk"] = mk
    return c


_TAB = {}


def _tables():
    if _TAB:
        return _TAB
    bf = ml_dtypes.bfloat16
    f32 = np.float32
    t = {}
    c = np.arange(256)
    ang = 2 * np.pi * ((c[:, None] * c[None, :]) % 256) / 256.0
    t["fcs"] = np.concatenate([np.cos(ang), np.sin(ang)], axis=1).astype(bf)
    for nm, L in (("p", LP), ("s", LS)):
        l = np.arange(L)
        ang = 2 * np.pi * ((l[:, None] * l[None, :]) % L) / float(L)
        sc = 1.0 / math.sqrt(L * 256.0)
        ft = np.stack([np.cos(ang) * sc, -np.sin(ang) * sc])
        nbl = L // 128
        LT2 = min(256, L // 2)
        nlt2 = (L // 2) // LT2
        t["fcol" + nm] = np.ascontiguousarray(ft[0, :, L // 2].reshape(nbl, 128).T).astype(bf)
        ft = ft[:, :, :L // 2].reshape(2, nbl, 128, nlt2, LT2)
        t["ftab" + nm] = np.ascontiguousarray(ft.transpose(3, 2, 0, 1, 4).reshape(nlt2, 128, 2 * nbl * LT2)).astype(bf)
        N = 2 * L
        nkc = (L + 1 + 127) // 128
        F = np.concatenate([np.arange(1, L, 2), np.arange(0, L + 1, 2), -np.ones(nkc * 128 - (L + 1), np.int64)]).astype(np.int64)
        valid = (F >= 0)
        Fk = np.where(valid, F, 0)

        def tfun(r, c):
            ang_ = 2 * np.pi * ((r * c) % N) / float(N)
            return np.stack([np.cos(ang_), -np.sin(ang_)])
        sidx = np.arange(L)
        Tf = tfun(sidx[:, None], Fk[None, :]) * valid[None, None, :]
        hf_ = Tf.reshape(2, nbl, 128, nkc, 128)
        t["htf" + nm] = np.ascontiguousarray(hf_.transpose(0, 3, 2, 1, 4).reshape(2, nkc, 128, L)).astype(bf)
        NT2 = min(L // 2, 512)
        nt2 = (L // 2) // NT2
        tidx = np.arange(L // 2)
        Ti = tfun(Fk[:, None], tidx[None, :]) * valid[None, :, None]
        hi_ = Ti.reshape(2, nkc, 128, nt2, NT2)
        t["hti" + nm] = np.ascontiguousarray(hi_.transpose(1, 3, 2, 0, 4).reshape(nkc, nt2, 128, 2 * NT2)).astype(bf)
        Tc = tfun(Fk, np.full_like(Fk, L // 2)) * valid[None, :]
        t["htic" + nm] = np.ascontiguousarray(Tc.reshape(2, nkc, 128).transpose(2, 1, 0).reshape(128, nkc * 2)).astype(bf)
        wkv = np.where((Fk == 0) | (Fk == L), 1.0, 2.0) / float(N) * valid
        t["wk" + nm] = np.ascontiguousarray(wkv.reshape(nkc, 128).T).astype(f32)
        tl = np.linspace(0.0, 1.0, L, dtype=np.float32).astype(np.float64)
        w = 2.0 * np.pi * np.arange(L, dtype=np.float64) / L
        f = np.linspace(1e-4, 15.0, 16, dtype=np.float32).astype(np.float64)
        zpos = np.concatenate([tl[:, None], np.cos(f[None, :] * w[:, None]), -np.sin(f[None, :] * w[:, None])], axis=1)
        t["zpos" + nm] = np.ascontiguousarray(zpos.T).astype(f32)
        t["tcol" + nm] = np.ascontiguousarray((-tl).reshape(L // 128, 128).T).astype(f32)
    bm = np.ones((128, 1), f32)
    bm[0, 0] = 0.0
    t["bm"] = bm
    mind = math.log(1e-2) / 1.5
    maxd = math.log(1e-2) / 0.3
    delta = np.abs(np.linspace(mind, maxd, D, dtype=np.float32))
    t["delta"] = np.ascontiguousarray(np.broadcast_to(delta[None, :], (128, D))).astype(f32)
    _TAB.update(t)
    return _TAB


def kernel(**inputs):
    nl = int(os.environ.get("KNL", "4"))
    inp = {k: np.asarray(v) for k, v in inputs.items()}
    f32 = np.float32
    consts = _consts()
    dummy = {k: v for k, v in _tables().items() if not k.startswith("_")}

    def pc(v, inner=()):
        return v

    def cols(v):
        return np.ascontiguousarray(v.reshape(-1, 128).T)

    shared = dict(consts)
    shared.update(dummy)
    shared["modw"] = inp["mod_w"]
    shared["modb"] = np.ascontiguousarray(np.stack([cols(inp["mod_b"][l]) for l in range(4)], axis=1))
    shared["ng"] = np.ascontiguousarray(np.stack([cols(inp["norm_g"][l]) for l in range(4)], axis=1))
    shared["fng"] = cols(inp["final_norm_g"])
    shared["gwin"] = inp["gla_w_in"]
    gwdec = np.zeros((2, 33, 1024), f32)
    for j in range(2):
        gwdec[j, 0:16, 0:512] = inp["gla_w_dec"][j, 0]
        gwdec[j, 16:32, 512:1024] = inp["gla_w_dec"][j, 1]
        gwdec[j, 32, 0:512] = inp["gla_b_dec"][j, 0]
        gwdec[j, 32, 512:1024] = inp["gla_b_dec"][j, 1]
    shared["gwdec"] = gwdec
    shared["gon"] = np.ascontiguousarray(np.stack([cols(inp["gla_onorm_g"][j]) for j in range(2)], axis=1))
    shared["gwout"] = inp["gla_w_out"]
    shared["fwin"] = inp["fn_w_in"][0]
    shared["fwout"] = inp["fn_w_out"][0]
    shared["hwin"] = inp["hy_w_in"][0]
    shared["hcw"] = np.ascontiguousarray(np.stack([cols(inp["hy_conv_w"][0, k]) for k in range(3)], axis=1))
    shared["hcb"] = cols(inp["hy_conv_b"][0])
    shared["hw1"] = inp["hy_ffn_w1"][0]
    shared["hw2"] = inp["hy_ffn_w2"][0]
    shared["hw3"] = inp["hy_ffn_w3"][0]
    shared["hw4"] = inp["hy_ffn_w4"][0]
    shared["hb"] = np.ascontiguousarray(np.stack([inp["hy_ffn_b1"][0], inp["hy_ffn_b2"][0], inp["hy_ffn_b3"][0],
                                                  inp["hy_freq"][0]], axis=1))
    shared["hd"] = np.ascontiguousarray(np.stack([cols(inp["hy_d"][0, n]) for n in range(2)], axis=1))
    shared["hwout"] = inp["hy_w_out"][0]
    shared = {k: np.ascontiguousarray(v) for k, v in shared.items()}

    in_maps = []
    for i in range(NCORES):
        s = i // 4
        mp = dict(shared)
        mp["xp"] = np.ascontiguousarray(inp["x_prompt"][4 * i:4 * i + 4].reshape(4 * LP, D))
        mp["xs"] = np.ascontiguousarray(inp["x_sample"][s])
        mp["st"] = np.ascontiguousarray(inp["state_gla"][s])
        cvv = np.stack([cols(inp["c_ctx"]), cols(inp["c"][s])], axis=2)
        mp["cv"] = np.ascontiguousarray(cvv)
        in_maps.append(mp)

    nc = build_program(nl)
    res = run_bass_kernel_spmd(nc, in_maps, core_ids=list(range(NCORES)), trace=bool(os.environ.get("KTRACE")))
    if os.environ.get("KTRACE"):
        print("EXEC_TIME_NS", res.exec_time_ns)
        _TAB["_res"] = res
    r = res.results
    if os.environ.get("KDBG"):
        _TAB["_dbg"] = {k: np.asarray(v).astype(np.float32) for k, v in r[0].items() if k.startswith("dbg_")}
    y_prompt = np.concatenate([r[i]["yp"].reshape(4, LP, D) for i in range(NCORES)], axis=0)
    y_sample = np.stack([r[0]["ys"], r[4]["ys"]], axis=0)
    new_state = np.concatenate([r[i]["ns"] for i in range(NCORES)], axis=0)
    return (y_prompt.astype(f32), y_sample.astype(f32), new_state.astype(f32))
```

```python
import os
import math
import numpy as np
import ml_dtypes
from contextlib import ExitStack
import concourse.bass as bass
import concourse.mybir as mybir
from concourse.bass_utils import run_bass_kernel_spmd

F32 = mybir.dt.float32
BF16 = mybir.dt.bfloat16
AF = mybir.ActivationFunctionType
ALU = mybir.AluOpType

NCORES = 8
D = 1024
DC = 8
EPS = 1e-6
LP = 256
LS = 2048
NAR = 21184


class Buf:
    __slots__ = ("name", "w", "r", "dsem", "dcnt")

    def __init__(self, name):
        self.name = name
        self.w = None
        self.r = {}
        self.dsem = None
        self.dcnt = 0


class Prog:
    ENG = ("pe", "act", "dve", "pool", "sp")

    def __init__(self, nc, es):
        self.nc = nc
        self.es = es
        self.streams = {e: [] for e in self.ENG}
        self.sem = {}
        for e in self.ENG:
            self.sem[e] = es.enter_context(nc.semaphore("s_" + e))
        self.cnt = {e: 0 for e in self.ENG}
        self.waited = {e: {} for e in self.ENG}
        self.dsems = []
        self.nsem = 0
        self.final = []
        self.stores = []
        self.psb = []
        self.psi = 0
        self.ninstr = 0

    def _wait(self, eng, deps):
        best = {}
        for d in deps:
            if d is None:
                continue
            s, v = d
            if best.get(s, (None, 0))[1] < v:
                best[s] = (s, v)
        for s, v in best.values():
            if eng == "pe" and s is self.sem["pe"]:
                continue
            if self.waited[eng].get(s, 0) >= v:
                continue
            self.waited[eng][s] = v
            self.streams[eng].append(lambda e, s=s, v=v: e.wait_ge(s, v))

    def op(self, eng, fn, reads=(), writes=()):
        deps = []
        for b in reads:
            deps.append(b.w)
        for b in writes:
            deps.append(b.w)
            deps.extend(b.r.items())
        self._wait(eng, deps)
        self.cnt[eng] += 1
        sem = self.sem[eng]
        ev = (sem, self.cnt[eng])
        self.streams[eng].append(lambda e, fn=fn, sem=sem: fn(e).then_inc(sem, 1))
        self.ninstr += 1
        for b in writes:
            b.w = ev
            b.r = {}
        for b in reads:
            if b not in writes:
                b.r[sem] = ev[1]

    def new_dsem(self):
        self.nsem += 1
        return self.es.enter_context(self.nc.semaphore("d%d" % self.nsem))

    def dma(self, q, out, in_, dst=None, src=None, final=False, slow=False):
        buf = dst if dst is not None else src
        if buf.dsem is None:
            buf.dsem = self.new_dsem()
        deps = []
        if dst is not None:
            if dst.w is not None and dst.w[0] is not dst.dsem:
                deps.append(dst.w)
            deps.extend(dst.r.items())
        if src is not None:
            deps.append(src.w)
        self._wait(q, deps)
        buf.dcnt += 16
        ds = buf.dsem
        ev = (ds, buf.dcnt)
        if slow:
            self.streams[q].append(lambda e, out=out, in_=in_, ds=ds: e.dma_start(out=out, in_=in_, allow_slow_non_contiguous=True).then_inc(ds, 16))
        else:
            self.streams[q].append(lambda e, out=out, in_=in_, ds=ds: e.dma_start(out=out, in_=in_).then_inc(ds, 16))
        self.ninstr += 1
        if dst is not None:
            dst.w = ev
            dst.r = {}
        if src is not None:
            src.r[ds] = ev[1]
            self.stores.append(ev)
        if final:
            self.final.append(ev)

    def barrier(self):
        evs = [(self.sem[e], self.cnt[e]) for e in self.ENG if self.cnt[e] > 0] + self.stores
        self.stores = []
        for e in self.ENG:
            self._wait(e, evs)

    def ps(self):
        b = self.psb[self.psi % len(self.psb)]
        self.psi += 1
        return b

    def mm(self, psbuf, mms, reads):
        def fn(e, mms=mms):
            r = None
            for (o, l, rh, st, sp) in mms:
                r = e.matmul(o, lhsT=l, rhs=rh, start=st, stop=sp)
            return r
        self.op("pe", fn, reads=reads, writes=[psbuf])
        self.ninstr += len(mms) - 1


def build_program(nl):
    nc = bass.Bass("TRN2", target_bir_lowering=False)
    es = ExitStack()
    with es:
        def din(name, shape, dt=F32):
            return nc.dram_tensor(name, list(shape), dt, kind="ExternalInput").ap()

        def dout(name, shape):
            return nc.dram_tensor(name, list(shape), F32, kind="ExternalOutput").ap()

        xp_d = din("xp", [4 * LP, D])
        xs_d = din("xs", [LS, D])
        st_d = din("st", [2, 2, 4, 128, 256])
        cv_d = din("cv", [128, DC, 2])
        modw_d = din("modw", [4, D, 3 * D])
        modb_d = din("modb", [128, 4, 24])
        ng_d = din("ng", [128, 4, DC])
        fng_d = din("fng", [128, DC])
        gwin_d = din("gwin", [2, D, 3104])
        gwdec_d = din("gwdec", [2, 33, 1024])
        gon_d = din("gon", [128, 2, 2])
        gwout_d = din("gwout", [2, D, D])
        fwin_d = din("fwin", [D, 2 * D])
        fwout_d = din("fwout", [D, D])
        hwin_d = din("hwin", [D, 4 * D])
        hcw_d = din("hcw", [128, 3, 24])
        hcb_d = din("hcb", [128, 24])
        hw1_d = din("hw1", [33, 64])
        hw2_d = din("hw2", [64, 64])
        hw3_d = din("hw3", [64, 64])
        hw4_d = din("hw4", [64, 4 * D])
        hb_d = din("hb", [64, 4])
        hd_d = din("hd", [128, 2, DC])
        hwout_d = din("hwout", [D, D])
        ident_d = din("ident", [128, 128])
        cm_d = din("cm", [128, 4, 129], BF16)
        mk_d = din("mk", [128, 2, 128])
        fcs_d = din("fcs", [256, 512], BF16)
        flt2_ = {LP: min(256, LP // 2), LS: min(256, LS // 2)}
        ftab_d = {L_: din("ftab" + nm_, [(L_ // 2) // flt2_[L_], 128, 2 * (L_ // 128) * flt2_[L_]], BF16) for nm_, L_ in (("p", LP), ("s", LS))}
        fcol_d = {L_: din("fcol" + nm_, [128, L_ // 128], BF16) for nm_, L_ in (("p", LP), ("s", LS))}
        nkc_ = {LP: (LP + 1 + 127) // 128, LS: (LS + 1 + 127) // 128}
        htf_d = {LP: din("htfp", [2, nkc_[LP], 128, LP], BF16), LS: din("htfs", [2, nkc_[LS], 128, LS], BF16)}
        nt2_ = {LP: min(LP // 2, 512), LS: min(LS // 2, 512)}
        hti_d = {L_: din("hti" + nm_, [nkc_[L_], (L_ // 2) // nt2_[L_], 128, 2 * nt2_[L_]], BF16) for nm_, L_ in (("p", LP), ("s", LS))}
        htic_d = {L_: din("htic" + nm_, [128, nkc_[L_] * 2], BF16) for nm_, L_ in (("p", LP), ("s", LS))}
        zpos_d = {LP: din("zposp", [33, LP]), LS: din("zposs", [33, LS])}
        tcol_d = {LP: din("tcolp", [128, LP // 128]), LS: din("tcols", [128, LS // 128])}
        bm_d = din("bm", [128, 1])
        wk_d = {LP: din("wkp", [128, 3]), LS: din("wks", [128, 17])}
        delta_d = din("delta", [128, D])

        yp_d = dout("yp", [4 * LP, D])
        ys_d = dout("ys", [LS, D])
        ns_d = dout("ns", [4, 2, 2, 4, 128, 256])

        pg = Prog(nc, es)
        dbg_on = bool(os.environ.get("KDBG"))
        dbg_seen = set()

        def dbg(name, ap, buf, dt=F32):
            if not dbg_on or name in dbg_seen:
                return
            dbg_seen.add(name)
            shp = list(ap.shape)
            dd = nc.dram_tensor("dbg_" + name, shp, dt, kind="ExternalOutput").ap()
            pg.dma("sp", dd, ap, src=buf, final=True)

        def sb(name, shape, dt=F32):
            return es.enter_context(nc.sbuf_tensor("sb_" + name, list(shape), dt))

        ident = sb("ident", [128, 128])
        ones_bf = sb("ones_bf", [128, 128], BF16)
        cm = sb("cm", [128, 4, 129], BF16)
        mk = sb("mk", [128, 2, 128])
        SH = sb("SH", [128, 4, DC, 2])
        AA = sb("AA", [128, 4, DC, 2])
        GG = sb("GG", [128, 4, DC, 2])
        ng = sb("ng", [128, 4, DC])
        fng = sb("fng", [128, DC])
        gon = sb("gon", [128, 2, 2])
        xT = sb("xT", [128, DC, LS])
        hT = sb("hT", [128, DC, LS], BF16)
        Wr = [sb("W0", [128, DC, 768], BF16), sb("W1", [128, DC, 768], BF16)]
        AR = sb("arena", [128, NAR])
        for i in range(8):
            pst = es.enter_context(nc.psum_tensor("ps%d" % i, [128, 512], F32))
            pg.psb.append((Buf("ps%d" % i), pst))

        B_const = Buf("const")
        B_mod = Buf("mod")
        B_W = [Buf("W0"), Buf("W1")]
        wi = [0]

        def next_w():
            i = wi[0] % 2
            wi[0] += 1
            return B_W[i], Wr[i]

        ar = {"off": 0}

        def ar_reset():
            ar["off"] = 0

        pbufs = {}

        def pb(name):
            if name not in pbufs:
                pbufs[name] = Buf(name)
            return pbufs[name]

        def af(n, name="a"):
            o = ar["off"]
            ar["off"] += n
            assert ar["off"] <= NAR, ("arena overflow", name, ar["off"])
            ar["hw"] = max(ar.get("hw", 0), ar["off"])
            return AR[:, o:o + n], pb(name)

        def ab(n, name="a"):
            n2 = (n + 1) // 2
            o = ar["off"]
            ar["off"] += n2
            assert ar["off"] <= NAR, ("arena overflow", name, ar["off"])
            ar["hw"] = max(ar.get("hw", 0), ar["off"])
            return AR[:, o:o + n2].bitcast(BF16)[:, 0:n], pb(name)

        pg.dma("sp", ident[:], ident_d[:, :], dst=B_const)
        pg.dma("sp", cm[:], cm_d[:, :, :], dst=B_const)
        pg.dma("sp", mk[:], mk_d[:, :, :], dst=B_const)
        pg.dma("sp", ng[:], ng_d[:, :, :], dst=B_const)
        pg.dma("sp", fng[:], fng_d[:, :], dst=B_const)
        pg.dma("sp", gon[:], gon_d[:, :, :], dst=B_const)
        pg.op("pool", lambda e: e.memset(ones_bf[:], 1.0), writes=[B_const])

        cv = sb("cv", [128, DC * 2])
        cvb = sb("cvb", [128, DC * 2], BF16)
        modb = sb("modbs", [128, 4 * 24])
        mv = sb("mv", [128, 4 * 24 * 2])
        B_cv, B_cvb, B_modb = Buf("cv"), Buf("cvb"), Buf("modb")
        B_mvL = [Buf("mv%d" % l) for l in range(4)]
        B_modL = [Buf("mod%d" % l) for l in range(4)]
        cv3 = cv[:, :].rearrange("p (c m) -> p c m", m=2)
        cvb3 = cvb[:, :].rearrange("p (c m) -> p c m", m=2)
        modb3 = modb[:, :].rearrange("p (l e) -> p l e", e=24)
        mv4 = mv[:, :].rearrange("p (l e m) -> p l e m", l=4, e=24)
        pg.dma("sp", cv3, cv_d[:, :, :], dst=B_cv)
        pg.dma("sp", modb3, modb_d[:, :, :], dst=B_modb)
        pg.op("act", lambda e: e.activation(out=cvb[:, :], in_=cv[:, :], func=AF.Silu), reads=[B_cv], writes=[B_cvb])

        def emit_mod(l):
            psB, psT = pg.ps()
            for quarter in range(4):
                Bw, Wt = next_w()
                pg.dma("pool", Wt[:, :, :], modw_d[l, :, quarter * 768:(quarter + 1) * 768].rearrange("(c p) n -> p c n", p=128), dst=Bw)
                mms = []
                for e6 in range(6):
                    ec = quarter * 6 + e6
                    for kc in range(DC):
                        mms.append((psT[:, ec * 2:ec * 2 + 2], Wt[:, kc, e6 * 128:(e6 + 1) * 128], cvb3[:, kc, :],
                                    kc == 0, kc == DC - 1))
                pg.mm(psB, mms, reads=[Bw, B_cvb])
            pg.op("dve", lambda e: e.tensor_tensor(
                out=mv4[:, l], in0=psT[:, 0:48].rearrange("p (e m) -> p e m", m=2),
                in1=modb3[:, l, :].unsqueeze(2).to_broadcast([128, 24, 2]), op=ALU.add),
                reads=[psB, B_modb], writes=[B_mvL[l]])
            pg.op("dve", lambda e: e.tensor_copy(out=SH[:, l], in_=mv4[:, l, 0:8, :]), reads=[B_mvL[l]], writes=[B_modL[l]])
            pg.op("dve", lambda e: e.tensor_copy(out=GG[:, l], in_=mv4[:, l, 16:24, :]), reads=[B_mvL[l]], writes=[B_modL[l]])
            pg.op("dve", lambda e: e.tensor_scalar(out=AA[:, l], in0=mv4[:, l, 8:16, :], scalar1=1.0, scalar2=None,
                                                   op0=ALU.add), reads=[B_mvL[l]], writes=[B_modL[l]])
            pg.op("dve", lambda e: e.tensor_tensor(out=AA[:, l], in0=AA[:, l], in1=ng[:, l, :].unsqueeze(2).to_broadcast([128, DC, 2]),
                                                   op=ALU.mult), reads=[B_const, B_modL[l]], writes=[B_modL[l]])

        if nl > 0:
            emit_mod(0)

        def run_group(gi, x_d, y_d, T, nseq, L, m):
            ntt = T // 512
            nbt = T // 128
            nb = L // 128
            pg.barrier()
            ar_reset()
            B_x = [pb("x%d" % i) for i in range(ntt)]
            B_h = [pb("h%d" % i) for i in range(ntt)]

            stg = [af(D, "stg%d" % i) for i in range(2)]
            for bt in range(nbt):
                s_ap, s_b = stg[bt % 2]
                pg.dma("sp", s_ap, x_d[bt * 128:(bt + 1) * 128, :], dst=s_b)
                for hf in range(2):
                    psB, psT = pg.ps()

                    def fn(e, s_ap=s_ap, psT=psT, hf=hf):
                        r = None
                        for c4 in range(4):
                            c = hf * 4 + c4
                            r = e.transpose(out=psT[:, c4 * 128:(c4 + 1) * 128], in_=s_ap[:, c * 128:(c + 1) * 128],
                                            identity=ident[:])
                        return r
                    pg.op("pe", fn, reads=[s_b, B_const], writes=[psB])
                    eng = "act" if hf == 0 else "dve"
                    o_ap = xT[:, hf * 4:(hf + 1) * 4, bt * 128:(bt + 1) * 128]
                    i_ap = psT[:, :].rearrange("p (c t) -> p c t", c=4)
                    if eng == "act":
                        pg.op("act", lambda e, o_ap=o_ap, i_ap=i_ap: e.activation(out=o_ap, in_=i_ap, func=AF.Copy),
                              reads=[psB], writes=[B_x[bt // 4]])
                    else:
                        pg.op("dve", lambda e, o_ap=o_ap, i_ap=i_ap: e.tensor_copy(out=o_ap, in_=i_ap),
                              reads=[psB], writes=[B_x[bt // 4]])

            ar_reset()

            def rms_rstd(tt, tmp_sq, B_sq, tmp_r, B_r):
                ts = slice(tt * 512, (tt + 1) * 512)
                sq3 = tmp_sq.rearrange("p (c t) -> p c t", c=DC)
                pg.op("act", lambda e: e.activation(out=sq3, in_=xT[:, :, ts], func=AF.Square),
                      reads=[B_x[tt]], writes=[B_sq])
                psB, psT = pg.ps()
                pg.mm(psB, [(psT[:, :], ones_bf[:], sq3[:, c, :], c == 0, c == DC - 1) for c in range(DC)],
                      reads=[B_sq, B_const])
                pg.op("act", lambda e: e.activation(out=tmp_r, in_=psT[:, :], func=AF.Ln, scale=1.0 / D, bias=EPS),
                      reads=[psB], writes=[B_r])
                pg.op("act", lambda e: e.activation(out=tmp_r, in_=tmp_r, func=AF.Exp, scale=-0.5),
                      reads=[B_r], writes=[B_r])

            def compute_h(li):
                mark = ar["off"]
                gens = []
                for tt in range(ntt):
                    sq, B_sq = ab(DC * 512, "sq%d" % tt)
                    rr, B_r = af(512, "rstd%d" % tt)
                    tmps = [af(512, "htmp%d_%d" % (tt, i)) for i in range(2)]
                    gens.append(h_tile(li, tt, sq, B_sq, rr, B_r, tmps))
                run_staged(gens)
                ar["off"] = mark
                pg.barrier()

            def h_tile(li, tt, sq, B_sq, rr, B_r, tmps):
                ts = slice(tt * 512, (tt + 1) * 512)
                sq3 = sq.rearrange("p (c t) -> p c t", c=DC)
                pg.op("act", lambda e: e.activation(out=sq3, in_=xT[:, :, ts], func=AF.Square),
                      reads=[B_x[tt]], writes=[B_sq])
                yield
                psB, psT = pg.ps()
                pg.mm(psB, [(psT[:, :], ones_bf[:], sq3[:, c, :], c == 0, c == DC - 1) for c in range(DC)],
                      reads=[B_sq, B_const])
                yield
                pg.op("act", lambda e: e.activation(out=rr, in_=psT[:, :], func=AF.Ln, scale=1.0 / D, bias=EPS),
                      reads=[psB], writes=[B_r])
                yield
                pg.op("act", lambda e: e.activation(out=rr, in_=rr, func=AF.Exp, scale=-0.5),
                      reads=[B_r], writes=[B_r])
                yield
                for c in range(DC):
                    t_ap, t_b = tmps[c % 2]
                    pg.op("dve", lambda e, t_ap=t_ap, c=c: e.tensor_tensor(
                        out=t_ap, in0=xT[:, c, ts], in1=rr, op=ALU.mult), reads=[B_x[tt], B_r], writes=[t_b])
                    pg.op("act", lambda e, t_ap=t_ap, c=c: e.activation(
                        out=hT[:, c, ts], in_=t_ap, func=AF.Identity, scale=AA[:, li, c, m:m + 1],
                        bias=SH[:, li, c, m:m + 1]), reads=[t_b, B_modL[li]], writes=[B_h[tt]])
                    yield

            def run_staged(gens):
                outs = [None] * len(gens)
                live = list(range(len(gens)))
                while live:
                    nxt = []
                    for gi_ in live:
                        try:
                            next(gens[gi_])
                            nxt.append(gi_)
                        except StopIteration as si:
                            outs[gi_] = si.value
                    live = nxt
                return outs

            def load_w_cols(dram2d, colspecs, queue="pool"):
                Bw, Wt = next_w()
                o = 0
                for (c0, n) in colspecs:
                    pg.dma(queue, Wt[:, :, o:o + n], dram2d[:, c0:c0 + n].rearrange("(c p) n -> p c n", p=128), dst=Bw)
                    o += n
                return Bw, Wt

            def out_proj_acc(li, wo_ap, B_wo, nk, yT_ap, B_y):
                for tt in range(ntt):
                    ts = slice(tt * 512, (tt + 1) * 512)
                    for dc in range(DC):
                        psB, psT = pg.ps()
                        pg.mm(psB, [(psT[:, :], wo_ap[:, k, dc * 128:(dc + 1) * 128], yT_ap[:, k, ts], k == 0, k == nk - 1)
                                    for k in range(nk)], reads=[B_wo] + (list(B_y) if isinstance(B_y, (list, tuple)) else [B_y]))
                        pg.op("dve", lambda e, psT=psT, dc=dc, ts=ts: e.scalar_tensor_tensor(
                            out=xT[:, dc, ts], in0=psT[:, :], scalar=GG[:, li, dc, m:m + 1], in1=xT[:, dc, ts],
                            op0=ALU.mult, op1=ALU.add), reads=[psB, B_modL[li]], writes=[B_x[tt]])

            def proj_fm(Bw, Wt, c0, dst_fn, reads_extra=()):
                for tt in range(ntt):
                    ts = slice(tt * 512, (tt + 1) * 512)
                    psB, psT = pg.ps()
                    pg.mm(psB, [(psT[:, :], Wt[:, c, c0:c0 + 128], hT[:, c, ts], c == 0, c == DC - 1) for c in range(DC)],
                          reads=[Bw, B_h[tt]])
                    dst_fn(tt, ts, psB, psT)

            def gla_layer(li, j):
                mark0 = ar["off"]
                lra, B_lra = ab(T, "lra")
                w2a, B_w2a = ab(1024, "w2a")
                wlr, B_wlr = ab(DC * 32, "wlr")
                wlr3 = wlr.rearrange("p (c n) -> p c n", c=DC)
                pg.dma("pool", w2a[0:33, :], gwdec_d[j, :, :], dst=B_w2a)
                pg.dma("pool", wlr3, gwin_d[j, :, 3072:3104].rearrange("(c p) n -> p c n", p=128), dst=B_wlr)
                pg.op("pool", lambda e: e.memset(lra[32:33, :], 1.0), writes=[B_lra])
                for tt in range(ntt):
                    ts = slice(tt * 512, (tt + 1) * 512)
                    psB, psT = pg.ps()
                    pg.mm(psB, [(psT[0:32, :], wlr3[:, c, :], hT[:, c, ts], c == 0, c == DC - 1) for c in range(DC)],
                          reads=[B_wlr, B_h[tt]])
                    pg.op("act", lambda e, psT=psT, ts=ts: e.activation(out=lra[0:32, ts], in_=psT[0:32, :], func=AF.Copy),
                          reads=[psB], writes=[B_lra])
                mark1 = ar["off"]
                for h in range(4):
                    ar["off"] = mark1
                    gwin = gwin_d[j]
                    Bw, Wt = load_w_cols(gwin, [(h * 128, 128), (512 + h * 128, 128), (1024 + h * 256, 256),
                                                (2048 + h * 256, 256)])
                    wo, B_wo = ab(2 * D, "wo")
                    wo3 = wo.rearrange("p (k n) -> p k n", k=2)
                    pg.dma("pool", wo3, gwout_d[j, h * 256:(h + 1) * 256, :].rearrange("(k p) n -> p k n", p=128), dst=B_wo)
                    qky, B_qky = ab(2 * T, "qky")
                    qk3 = qky.rearrange("p (a t) -> p a t", a=2)
                    kvt, B_kvt = ab(nbt * 384, "kvt")
                    kvt3 = kvt.rearrange("p (b n) -> p b n", n=384)
                    late_rs = (nseq == 1)
                    if late_rs:
                        rs, B_rs = kvt[:, 0:2 * T], B_kvt
                    else:
                        rs, B_rs = ab(2 * T, "rs")
                    rs3 = rs.rearrange("p (a t) -> p a t", a=2)
                    oT, _ = af(2 * L, "oT")
                    oT3 = oT.rearrange("p (a t) -> p a t", a=2)
                    B_o = [pb("o%d" % i) for i in range(nb)]
                    proj_fm(Bw, Wt, 0, lambda tt, ts, psB, psT: pg.op(
                        "act", lambda e: e.activation(out=qk3[:, 0, ts], in_=psT[:, :], func=AF.Copy, scale=128.0 ** -0.5),
                        reads=[psB], writes=[B_qky]))
                    proj_fm(Bw, Wt, 128, lambda tt, ts, psB, psT: pg.op(
                        "dve", lambda e: e.tensor_copy(out=qk3[:, 1, ts], in_=psT[:, :]), reads=[psB], writes=[B_qky]))
                    def proj_r():
                        for a in range(2):
                            proj_fm(Bw, Wt, 512 + a * 128, lambda tt, ts, psB, psT, a=a: pg.op(
                                "act", lambda e: e.activation(out=rs3[:, a, ts], in_=psT[:, :], func=AF.Silu),
                                reads=[psB], writes=[B_rs]))
                    if not late_rs:
                        proj_r()
                    for bt in range(nbt):
                        bs = slice(bt * 128, (bt + 1) * 128)
                        psB, psT = pg.ps()
                        pg.mm(psB, [(psT[:, 0:384], hT[:, c, bs], Wt[:, c, 128:512], c == 0, c == DC - 1) for c in range(DC)],
                              reads=[Bw, B_h[bt // 4]])
                        pg.op("dve" if bt % 2 else "act",
                              (lambda e, psT=psT, bt=bt: e.tensor_copy(out=kvt3[:, bt, :], in_=psT[:, 0:384])) if bt % 2 else
                              (lambda e, psT=psT, bt=bt: e.activation(out=kvt3[:, bt, :], in_=psT[:, 0:384], func=AF.Copy)),
                              reads=[psB], writes=[B_kvt])
                    dbg("hT", hT[:, :, 0:512], B_h[0], BF16)
                    dbg("qk", qk3[:, :, 0:256], B_qky, BF16)
                    dbg("kvt", kvt3[:, 0:2, :], B_kvt, BF16)
                    dbg("lra", lra[0:33, 0:256], B_lra, BF16)
                    mark2 = ar["off"]
                    Sf = [af(256, "S%d" % d) for d in range(2)]
                    Sb = [[ab(256, "Sb%d%d" % (d, i)) for i in range(2)] for d in range(2)]
                    t1s = [[[af(128, "t1_%d_%d_%d" % (gp_, d, i)) for i in range(2)] for d in range(2)] for gp_ in range(2)]
                    gps = [[[ab(128, "gp_%d_%d_%d" % (gp_, d, i)) for i in range(2)] for d in range(2)] for gp_ in range(2)]
                    E1s = [[[af(129, "E1_%d_%d_%d" % (gp_, d, i)) for i in range(2)] for d in range(2)] for gp_ in range(2)]
                    E2s = [[[af(128, "E2_%d_%d_%d" % (gp_, d, i)) for i in range(2)] for d in range(2)] for gp_ in range(2)]
                    E3s = [[[af(128, "E3_%d_%d_%d" % (gp_, d, i)) for i in range(2)] for d in range(2)] for gp_ in range(2)]
                    qts = [[[ab(128, "qt_%d_%d_%d" % (gp_, d, i)) for i in range(2)] for d in range(2)] for gp_ in range(2)]
                    kts = [[[ab(128, "kt_%d_%d_%d" % (gp_, d, i)) for i in range(2)] for d in range(2)] for gp_ in range(2)]
                    khs = [[[ab(128, "kh_%d_%d_%d" % (gp_, d, i)) for i in range(2)] for d in range(2)] for gp_ in range(2)]
                    ats = [[[ab(128, "at_%d_%d_%d" % (gp_, d, i)) for i in range(2)] for d in range(2)] for gp_ in range(2)]
                    nt = min(L, 256)
                    sq, B_sq = ab(2 * nt, "osq")
                    sq3 = sq.rearrange("p (a t) -> p a t", a=2)
                    rr, B_rr = af(nt, "orstd")
                    tmp, B_tmp = af(2 * nt, "otmp")
                    tmp3 = tmp.rearrange("p (a t) -> p a t", a=2)
                    def gla_prep(s, step, d, gpar):
                        blk = step if d == 0 else nb - 1 - step
                        tok0 = s * L + blk * 128
                        tk = slice(tok0, tok0 + 128)
                        btg = tok0 // 128
                        pi = step % 2
                        t1, B_t1 = t1s[gpar][d][pi]
                        gp, B_gp = gps[gpar][d][pi]
                        E1, B_E1 = E1s[gpar][d][pi]
                        E2, B_E2 = E2s[gpar][d][pi]
                        E3, B_E3 = E3s[gpar][d][pi]
                        qt, B_qt = qts[gpar][d][pi]
                        kt, B_kt = kts[gpar][d][pi]
                        kh, B_kh = khs[gpar][d][pi]
                        at, B_at = ats[gpar][d][pi]
                        psB, psT = pg.ps()
                        c0 = d * 512 + h * 128
                        pg.mm(psB, [(psT[:, 0:128], lra[0:33, tk], w2a[0:33, c0:c0 + 128], True, True)],
                              reads=[B_lra, B_w2a])
                        yield
                        pg.op("act", lambda e: e.activation(out=t1, in_=psT[:, 0:128], func=AF.Exp, scale=-1.0),
                              reads=[psB], writes=[B_t1])
                        yield
                        pg.op("act", lambda e: e.activation(out=gp, in_=t1, func=AF.Ln, bias=1.0),
                              reads=[B_t1], writes=[B_gp])
                        yield
                        psB2, psT2 = pg.ps()
                        pg.mm(psB2, [(psT2[:, 0:129], gp, cm[:, 2 * d, 0:129], True, True),
                                     (psT2[:, 256:384], cm[:, 2 * d + 1, 0:128], gp, True, True)],
                              reads=[B_gp, B_const])
                        yield
                        pg.op("act", lambda e: e.activation(out=E1, in_=psT2[:, 0:129], func=AF.Exp, scale=-1.0 / 16),
                              reads=[psB2], writes=[B_E1])
                        pg.op("act", lambda e: e.activation(out=E2, in_=psT2[:, 0:128], func=AF.Exp, scale=1.0 / 16),
                              reads=[psB2], writes=[B_E2])
                        pg.op("act", lambda e: e.activation(out=E3, in_=psT2[:, 256:384], func=AF.Exp, scale=-1.0 / 16),
                              reads=[psB2], writes=[B_E3])
                        yield
                        pg.op("dve", lambda e: e.tensor_tensor(out=qt, in0=qk3[:, 0, tk], in1=E1[:, 0:128], op=ALU.mult),
                              reads=[B_qky, B_E1], writes=[B_qt])
                        pg.op("pool", lambda e: e.tensor_tensor(out=kt, in0=qk3[:, 1, tk], in1=E2, op=ALU.mult),
                              reads=[B_qky, B_E2], writes=[B_kt])
                        pg.op("pool", lambda e: e.tensor_tensor(out=kh, in0=kvt3[:, btg, 0:128], in1=E3, op=ALU.mult),
                              reads=[B_kvt, B_E3], writes=[B_kh])
                        yield
                        psB3, psT3 = pg.ps()
                        pg.mm(psB3, [(psT3[:, 0:128], kt, qt, True, True)], reads=[B_kt, B_qt])
                        yield
                        pg.op("dve", lambda e: e.tensor_tensor(out=at, in0=psT3[:, 0:128], in1=mk[:, d, :], op=ALU.mult),
                              reads=[psB3, B_const], writes=[B_at])

                    def gla_state_gen(s, step0, gp):
                        for step in (step0, step0 + 1):
                            for d in range(2):
                                blk = step if d == 0 else nb - 1 - step
                                tok0 = s * L + blk * 128
                                btg = tok0 // 128
                                pi = step % 2
                                E1, B_E1 = E1s[gp][d][pi]
                                qt, B_qt = qts[gp][d][pi]
                                at, B_at = ats[gp][d][pi]
                                kh, B_kh = khs[gp][d][pi]
                                S_ap, S_b = Sf[d]
                                if step == 0:
                                    if m == 1:
                                        pg.dma("sp", S_ap, st_d[j, d, h, :, :], dst=S_b)
                                    else:
                                        pg.op("pool", lambda e, S_ap=S_ap: e.memset(S_ap, 0.0), writes=[S_b])
                                    sb_ap, sb_b = Sb[d][0]
                                    pg.op("act", lambda e, S_ap=S_ap, sb_ap=sb_ap: e.activation(out=sb_ap, in_=S_ap, func=AF.Copy),
                                          reads=[S_b], writes=[sb_b])
                                sbi_ap, sbi_b = Sb[d][step % 2]
                                sbo_ap, sbo_b = Sb[d][(step + 1) % 2]
                                psB5, psT5 = pg.ps()
                                pg.mm(psB5, [(psT5[:, 0:256], kh, kvt3[:, btg, 128:384], True, True)], reads=[B_kh, B_kvt])
                                psB4, psT4 = pg.ps()
                                mms = []
                                for hf in range(2):
                                    mms.append((psT4[:, hf * 128:(hf + 1) * 128], kvt3[:, btg, 128 + hf * 128:256 + hf * 128], at, True, False))
                                    mms.append((psT4[:, hf * 128:(hf + 1) * 128], sbi_ap[:, hf * 128:(hf + 1) * 128], qt, False, True))
                                pg.mm(psB4, mms, reads=[B_kvt, B_at, sbi_b, B_qt])
                                o_ap = oT3[:, :, blk * 128:(blk + 1) * 128]
                                p_ap = psT4[:, 0:256].rearrange("p (a t) -> p a t", a=2)
                                other = nb - 1 - step
                                is_first = (step <= other) if d == 0 else (step < other)
                                if is_first:
                                    pg.op("act", lambda e, o_ap=o_ap, p_ap=p_ap: e.activation(out=o_ap, in_=p_ap, func=AF.Copy),
                                          reads=[psB4], writes=[B_o[blk]])
                                else:
                                    pg.op("dve", lambda e, o_ap=o_ap, p_ap=p_ap: e.tensor_tensor(out=o_ap, in0=o_ap, in1=p_ap, op=ALU.add),
                                          reads=[psB4], writes=[B_o[blk]])
                                pg.op("dve", lambda e, S_ap=S_ap, E1=E1, psT5=psT5: e.scalar_tensor_tensor(
                                    out=S_ap, in0=S_ap, scalar=E1[:, 128:129], in1=psT5[:, 0:256], op0=ALU.mult, op1=ALU.add),
                                    reads=[psB5, B_E1], writes=[S_b])
                                if step < nb - 1:
                                    pg.op("act", lambda e, S_ap=S_ap, sbo_ap=sbo_ap: e.activation(out=sbo_ap, in_=S_ap, func=AF.Copy),
                                          reads=[S_b], writes=[sbo_b])
                                elif m == 0:
                                    pg.dma("sp", ns_d[s, j, d, h, :, :], S_ap, src=S_b, final=True)
                                yield

                    def gla_norm(s):
                        for t0 in range(0, L, nt):
                            blks = range(t0 // 128, (t0 + nt) // 128)
                            gs = slice(s * L + t0, s * L + t0 + nt)
                            ls = slice(t0, t0 + nt)
                            Bos = [B_o[b] for b in blks]
                            pg.op("act", lambda e, sq3=sq3, ls=ls: e.activation(out=sq3, in_=oT3[:, :, ls], func=AF.Square),
                                  reads=Bos, writes=[B_sq])
                            psB, psT = pg.ps()
                            pg.mm(psB, [(psT[:, 0:nt], ones_bf[:], sq3[:, a, :], a == 0, a == 1) for a in range(2)],
                                  reads=[B_sq, B_const])
                            pg.op("act", lambda e, rr=rr, psT=psT, nt=nt: e.activation(out=rr, in_=psT[:, 0:nt], func=AF.Ln, scale=1.0 / 256, bias=EPS),
                                  reads=[psB], writes=[B_rr])
                            pg.op("act", lambda e, rr=rr: e.activation(out=rr, in_=rr, func=AF.Exp, scale=-0.5),
                                  reads=[B_rr], writes=[B_rr])
                            pg.op("dve", lambda e, tmp3=tmp3, ls=ls, rr=rr, nt=nt: e.tensor_tensor(
                                out=tmp3, in0=oT3[:, :, ls], in1=rr.unsqueeze(1).to_broadcast([128, 2, nt]), op=ALU.mult),
                                reads=Bos + [B_rr], writes=[B_tmp])
                            for a in range(2):
                                pg.op("dve", lambda e, tmp3=tmp3, a=a, gs=gs: e.scalar_tensor_tensor(
                                    out=qk3[:, a, gs], in0=tmp3[:, a, :], scalar=gon[:, j, a:a + 1], in1=rs3[:, a, gs],
                                    op0=ALU.mult, op1=ALU.mult), reads=[B_tmp, B_rs, B_const], writes=[B_qky])
                    groups = [(s_, st0) for s_ in range(nseq) for st0 in range(0, nb, 2)]
                    prev = None
                    for gidx, (s_, st0) in enumerate(groups):
                        gp = gidx % 2
                        gens = [gla_prep(s_, st, d, gp) for st in (st0, st0 + 1) for d in range(2)]
                        if prev is not None:
                            gens.append(gla_state_gen(*prev))
                        run_staged(gens)
                        if prev is not None and prev[1] == nb - 2 and not late_rs:
                            gla_norm(prev[0])
                        prev = (s_, st0, gp)
                    run_staged([gla_state_gen(*prev)])
                    if late_rs:
                        proj_r()
                    gla_norm(prev[0])
                    out_proj_acc(li, wo3, B_wo, 2, qk3, B_qky)
                ar["off"] = mark0
                pg.barrier()


            def fnet_layer(li):
                mark0 = ar["off"]
                fcs, B_fcs = ab(2 * 512, "fcs")
                fcs3 = fcs.rearrange("p (c n) -> p c n", c=2)
                pg.dma("sp", fcs3, fcs_d.rearrange("(c p) n -> p c n", p=128), dst=B_fcs)
                LT = min(256, L // 2)
                nlt = (L // 2) // LT
                fcol, B_fcol = ab(nb, "fcol")
                pg.dma("sp", fcol, fcol_d[L][:, :], dst=B_fcol)
                mark1 = ar["off"]
                for g in range(4):
                    ar["off"] = mark1
                    Bw, Wt = load_w_cols(fwin_d, [(g * 256, 256), (1024 + g * 256, 256)])
                    wo, B_wo = ab(2 * D, "wo")
                    wo3 = wo.rearrange("p (k n) -> p k n", k=2)
                    pg.dma("pool", wo3, fwout_d[g * 256:(g + 1) * 256, :].rearrange("(k p) n -> p k n", p=128), dst=B_wo)
                    uy, B_uy = ab(2 * T, "uy")
                    uy3 = uy.rearrange("p (a t) -> p a t", a=2)
                    zs, B_zs = ab(2 * T, "zs")
                    zs3 = zs.rearrange("p (a t) -> p a t", a=2)
                    PQ, B_PQ = ab(nb * 512, "PQ")
                    PQ3 = PQ.rearrange("p (b n) -> p b n", n=512)
                    tabs = [ab(2 * nb * LT, "ftab%d" % i) for i in range(2)]
                    etA, B_etA = af(LT, "fetA")
                    ft1, B_ft1 = af(LT, "fft1")
                    ft2, B_ft2 = af(LT, "fft2")
                    for a in range(2):
                        proj_fm(Bw, Wt, a * 128, lambda tt, ts, psB, psT, a=a: pg.op(
                            "dve", lambda e: e.tensor_copy(out=uy3[:, a, ts], in_=psT[:, :]), reads=[psB], writes=[B_uy]))
                        proj_fm(Bw, Wt, 256 + a * 128, lambda tt, ts, psB, psT, a=a: pg.op(
                            "act", lambda e: e.activation(out=zs3[:, a, ts], in_=psT[:, :], func=AF.Silu),
                            reads=[psB], writes=[B_zs]))
                    ti = 0
                    for s in range(nseq):
                        for blk in range(nb):
                            tk = slice(s * L + blk * 128, s * L + blk * 128 + 128)
                            psB, psT = pg.ps()
                            pg.mm(psB, [(psT[:, :], uy3[:, cc, tk], fcs3[:, cc, :], cc == 0, cc == 1) for cc in range(2)],
                                  reads=[B_uy, B_fcs])
                            if blk % 2:
                                pg.op("dve", lambda e, psT=psT, blk=blk: e.tensor_copy(out=PQ3[:, blk, :], in_=psT[:, :]),
                                      reads=[psB], writes=[B_PQ])
                            else:
                                pg.op("act", lambda e, psT=psT, blk=blk: e.activation(out=PQ3[:, blk, :], in_=psT[:, :], func=AF.Copy),
                                      reads=[psB], writes=[B_PQ])
                        for lt in range(nlt):
                            tab_ap, tab_b = tabs[ti % 2]
                            ti += 1
                            tab4 = tab_ap.rearrange("p (a c n) -> p a c n", a=2, c=nb)
                            pg.dma("sp", tab_ap, ftab_d[L][lt, :, :], dst=tab_b)
                            t0 = lt * LT
                            gs = slice(s * L + t0, s * L + t0 + LT)
                            c1 = 1 if t0 == 0 else 0
                            n2 = LT - c1
                            jlo = s * L + L - t0 - LT + 1
                            for cc in range(2):
                                psAB, psA = pg.ps()
                                pg.mm(psAB, [(psA[:, 0:LT], PQ3[:, lc, cc * 128:(cc + 1) * 128], tab4[:, 0, lc, :], lc == 0, lc == nb - 1)
                                             for lc in range(nb)], reads=[B_PQ, tab_b])
                                psBB, psBt = pg.ps()
                                pg.mm(psBB, [(psBt[:, 0:LT], PQ3[:, lc, 256 + cc * 128:256 + (cc + 1) * 128], tab4[:, 1, lc, :], lc == 0, lc == nb - 1)
                                             for lc in range(nb)], reads=[B_PQ, tab_b])
                                pg.op("act", lambda e, psA=psA: e.activation(out=etA, in_=psA[:, 0:LT], func=AF.Copy),
                                      reads=[psAB], writes=[B_etA])
                                pg.op("dve", lambda e, psBt=psBt: e.tensor_tensor(out=ft1, in0=etA, in1=psBt[:, 0:LT], op=ALU.add),
                                      reads=[B_etA, psBB], writes=[B_ft1])
                                pg.op("dve", lambda e, cc=cc, gs=gs: e.tensor_tensor(out=uy3[:, cc, gs], in0=ft1, in1=zs3[:, cc, gs], op=ALU.mult),
                                      reads=[B_ft1, B_zs], writes=[B_uy])
                                pg.op("dve", lambda e, psBt=psBt, c1=c1, n2=n2: e.tensor_tensor(
                                    out=ft2[:, 0:n2], in0=etA[:, c1:LT][:, ::-1], in1=psBt[:, c1:LT][:, ::-1], op=ALU.subtract),
                                    reads=[B_etA, psBB], writes=[B_ft2])
                                pg.op("dve", lambda e, cc=cc, jlo=jlo, n2=n2: e.tensor_tensor(
                                    out=uy3[:, cc, jlo:jlo + n2], in0=ft2[:, 0:n2], in1=zs3[:, cc, jlo:jlo + n2], op=ALU.mult),
                                    reads=[B_ft2, B_zs], writes=[B_uy])
                        hh = s * L + L // 2
                        for cc in range(2):
                            psB, psT = pg.ps()
                            pg.mm(psB, [(psT[:, 0:1], PQ3[:, lc, cc * 128:(cc + 1) * 128], fcol[:, lc:lc + 1], lc == 0, lc == nb - 1)
                                        for lc in range(nb)], reads=[B_PQ, B_fcol])
                            pg.op("dve", lambda e, psT=psT, cc=cc, hh=hh: e.tensor_tensor(
                                out=uy3[:, cc, hh:hh + 1], in0=psT[:, 0:1], in1=zs3[:, cc, hh:hh + 1], op=ALU.mult),
                                reads=[psB, B_zs], writes=[B_uy])
                    out_proj_acc(li, wo3, B_wo, 2, uy3, B_uy)
                ar["off"] = mark0
                pg.barrier()

            def hyena_layer(li):
                mark0 = ar["off"]
                PI = math.pi
                nkc = (L + 1 + 127) // 128
                NT = min(L, 512)
                a3, B_a3 = ab(L, "a3")
                hbt, B_hb = af(8, "hbt")
                hws = [af(64, "hw%d" % i) for i in range(3)]
                tcol, B_small = af(nb, "tcol")
                bm, _ = af(1, "bm")
                wk, _ = af(nkc, "wk")
                hcw, _ = af(72, "hcw")
                hcw3 = hcw.rearrange("p (k c) -> p k c", k=3)
                hcb, _ = af(24, "hcb")
                hd, _ = af(16, "hd")
                hd3 = hd.rearrange("p (n c) -> p n c", n=2)
                pg.dma("sp", hbt[0:64, 0:4], hb_d[:, :], dst=B_hb)
                pg.dma("sp", hws[0][0][0:33, :], hw1_d[:, :], dst=hws[0][1])
                pg.dma("sp", hws[1][0][0:64, :], hw2_d[:, :], dst=hws[1][1])
                pg.dma("sp", hws[2][0][0:64, :], hw3_d[:, :], dst=hws[2][1])
                pg.dma("sp", tcol, tcol_d[L][:, :], dst=B_small)
                pg.dma("sp", bm, bm_d[:, :], dst=B_small)
                pg.dma("sp", wk, wk_d[L][:, :], dst=B_small)
                pg.dma("sp", hcw3, hcw_d[:, :, :], dst=B_small)
                pg.dma("sp", hcb, hcb_d[:, :], dst=B_small)
                pg.dma("sp", hd3, hd_d[:, :, :], dst=B_small)
                pg.op("dve", lambda e: e.tensor_scalar(out=hbt[0:64, 4:7], in0=hbt[0:64, 0:3], scalar1=hbt[0:64, 3:4], scalar2=None,
                                                       op0=ALU.mult), reads=[B_hb], writes=[B_hb])
                markf = ar["off"]
                zp, B_zp = af(L, "zp")
                aA, B_aA = af(L, "aA")
                aB, B_aB = af(L, "aB")
                arg, B_arg = af(512, "arg")
                mw, B_mw = af(512, "mwrap")
                pg.dma("sp", zp[0:33, :], zpos_d[L][:, :], dst=B_zp)
                srcs = [(zp, B_zp, 33), (aA, B_aA, 64), (aB, B_aB, 64)]
                dsts = [(aA, B_aA), (aB, B_aB), (a3, B_a3)]
                for i in range(3):
                    s_ap, s_b, K = srcs[i]
                    d_ap, d_b = dsts[i]
                    w_ap, w_b = hws[i]
                    for t0 in range(0, L, 512):
                        n = min(512, L - t0)
                        psB, psT = pg.ps()
                        pg.mm(psB, [(psT[0:64, 0:n], w_ap[0:K, :], s_ap[0:K, t0:t0 + n], True, True)], reads=[w_b, s_b])
                        pg.op("act", lambda e, psT=psT, n=n, i=i: e.activation(out=arg[0:64, 0:n], in_=psT[0:64, 0:n], func=AF.Identity,
                                                                            scale=hbt[0:64, 3:4], bias=hbt[0:64, 4 + i:5 + i]),
                              reads=[psB, B_hb], writes=[B_arg])
                        pg.op("dve", lambda e, n=n: e.tensor_scalar(out=mw[0:64, 0:n], in0=arg[0:64, 0:n], scalar1=PI, scalar2=2 * PI,
                                                                    op0=ALU.is_gt, op1=ALU.mult), reads=[B_arg], writes=[B_mw])
                        pg.op("dve", lambda e, n=n: e.tensor_tensor(out=arg[0:64, 0:n], in0=arg[0:64, 0:n], in1=mw[0:64, 0:n],
                                                                    op=ALU.subtract), reads=[B_mw], writes=[B_arg])
                        pg.op("dve", lambda e, n=n: e.tensor_scalar(out=mw[0:64, 0:n], in0=arg[0:64, 0:n], scalar1=-PI, scalar2=2 * PI,
                                                                    op0=ALU.is_lt, op1=ALU.mult), reads=[B_arg], writes=[B_mw])
                        pg.op("dve", lambda e, n=n: e.tensor_tensor(out=arg[0:64, 0:n], in0=arg[0:64, 0:n], in1=mw[0:64, 0:n],
                                                                    op=ALU.add), reads=[B_mw], writes=[B_arg])
                        pg.op("act", lambda e, n=n, d_ap=d_ap, t0=t0: e.activation(out=d_ap[0:64, t0:t0 + n], in_=arg[0:64, 0:n], func=AF.Sin),
                              reads=[B_arg], writes=[d_b])
                ar["off"] = markf
                pg.barrier()
                resident = (L <= 256)
                NT2 = min(L // 2, 512)
                ntile2 = (L // 2) // NT2
                nodd = (L // 2) // 128
                if resident:
                    NFR, NIR = 2 * nkc, 2 * nkc * ntile2
                else:
                    NFR, NIR = 5, 8
                ftr = [ab(nb * 128, "ftr%d" % i) for i in range(NFR)]
                itr = [ab(NT2, "itr%d" % i) for i in range(NIR)]
                hcol, B_hcol = ab(nkc * 2, "hcol")
                pg.dma("sp", hcol, htic_d[L][:, :], dst=B_hcol)
                fi = [0]
                ii = [0]
                fcache = {}
                icache = {}

                def get_ftab(a, kc):
                    if resident and (a, kc) in fcache:
                        return fcache[(a, kc)]
                    f_ap, f_b = ftr[fi[0] % NFR]
                    fi[0] += 1
                    if not (os.environ.get("KNODMA") and fi[0] > 3):
                        pg.dma("sp", f_ap, htf_d[L][a, kc, :, :], dst=f_b)
                    r = (f_ap.rearrange("p (c n) -> p c n", c=nb), f_b)
                    fcache[(a, kc)] = r
                    return r

                def get_itab(kc, ti, a):
                    if resident and (kc, ti, a) in icache:
                        return icache[(kc, ti, a)]
                    i_ap, i_b = itr[ii[0] % NIR]
                    ii[0] += 1
                    if not (os.environ.get("KNODMA") and ii[0] > 2):
                        pg.dma("sp", i_ap, hti_d[L][kc, ti, :, a * NT2:(a + 1) * NT2], dst=i_b)
                    r = (i_ap, i_b)
                    icache[(kc, ti, a)] = r
                    return r

                mark1 = ar["off"]
                for cb in range(8):
                    ar["off"] = mark1
                    Bw, Wt = load_w_cols(hwin_d, [(cb * 128, 128), (1024 + cb * 128, 128), (2048 + cb * 128, 128), (3072 + cb * 128, 128)])
                    wo, B_wo = ab(D, "wo")
                    wo3 = wo.rearrange("p (k n) -> p k n", k=1)
                    pg.dma("pool", wo3[:, 0, :], hwout_d[cb * 128:(cb + 1) * 128, :], dst=B_wo)
                    w4c, B_w4c = ab(512, "w4c")
                    w4c3 = w4c.rearrange("p (a n) -> p a n", a=4)
                    for nd in range(4):
                        pg.dma("pool", w4c3[0:64, nd, :], hw4_d[:, nd * 1024 + cb * 128:nd * 1024 + (cb + 1) * 128], dst=B_w4c)
                    dlc, B_dlc = af(128, "dlc")
                    pg.dma("sp", dlc, delta_d[:, cb * 128:(cb + 1) * 128], dst=B_dlc)
                    G2, B_G2 = ab(T, "G2")
                    G23 = G2.rearrange("p (k t) -> p k t", k=1)
                    B_G2s = [pb("G2s%d" % s_) for s_ in range(nseq)]

                    def hy_seq(s, cb=cb, Bw=Bw, Wt=Wt, w4c3=w4c3, B_w4c=B_w4c, dlc=dlc, B_dlc=B_dlc, G2=G2, B_G2s=B_G2s):
                        sfx = "_s%d" % s
                        YAB, B_AB = ab(nb * 384, "YAB" + sfx)
                        YAB3 = YAB.rearrange("p (b n) -> p b n", n=384)
                        B_ytok = pb("ytok" + sfx)
                        tFs = [af(128, "tF%d%s" % (i, sfx)) for i in range(2)]
                        tBs = [af(128, "tB%d%s" % (i, sfx)) for i in range(2)]
                        dws = [af(128, "dw%d%s" % (i, sfx)) for i in range(2)]
                        raw = YAB.bitcast(F32)[:, 0:L + 2]
                        B_raw = pb("raw" + sfx)
                        yT, B_y = af(L, "yT" + sfx)
                        R1, B_R1 = af(max(L, nkc * 128), "R1" + sfx)
                        ctmp = R1[:, 0:L]
                        Zb = R1[:, 0:nkc * 128].bitcast(BF16).rearrange("p (k a n) -> p k a n", k=nkc, a=2)
                        G1, B_G1 = ab(L, "G1" + sfx)
                        etmp, B_etmp = af(NT2, "etmp" + sfx)
                        Xs, B_Xs = af(256, "Xs" + sfx)
                        Hs, B_Hs = af(256, "Hs" + sfx)
                        ttb, _ = af(max(512, NT2), "ttb" + sfx)
                        tt4 = [(ttb[:, i * 128:(i + 1) * 128], pb("tt%d%s" % (i, sfx))) for i in range(4)]
                        etmp2, B_etmp2 = ttb[:, 0:NT2], pb("etmp2" + sfx)
                        B_G2 = B_G2s[s]
                        x2c = G2[:, s * L:(s + 1) * L]
                        yield

                        def proj_seq(c0, evac):
                            for t0 in range(0, L, NT):
                                gs = slice(s * L + t0, s * L + t0 + NT)
                                psB, psT = pg.ps()
                                pg.mm(psB, [(psT[:, 0:NT], Wt[:, c, c0:c0 + 128], hT[:, c, gs], c == 0, c == DC - 1) for c in range(DC)],
                                      reads=[Bw] + B_h)
                                yield
                                evac(t0, psB, psT)
                                yield

                        def shortconv(ci, dst_ap, dst_b):
                            ch = ci * 8 + cb
                            pg.op("dve", lambda e: e.tensor_scalar(out=ctmp, in0=raw[:, 1:L + 1], scalar1=hcw3[:, 1, ch:ch + 1],
                                                                   scalar2=hcb[:, ch:ch + 1], op0=ALU.mult, op1=ALU.add),
                                  reads=[B_raw, B_small], writes=[B_R1])
                            yield
                            pg.op("dve", lambda e: e.scalar_tensor_tensor(out=ctmp, in0=raw[:, 0:L], scalar=hcw3[:, 0, ch:ch + 1], in1=ctmp,
                                                                          op0=ALU.mult, op1=ALU.add), reads=[B_raw, B_small], writes=[B_R1])
                            yield
                            pg.op("dve", lambda e: e.scalar_tensor_tensor(out=dst_ap, in0=raw[:, 2:L + 2], scalar=hcw3[:, 2, ch:ch + 1], in1=ctmp,
                                                                          op0=ALU.mult, op1=ALU.add), reads=[B_raw, B_small, B_R1], writes=[dst_b])
                            yield

                        dsts = [(G1, B_G1), (x2c, B_G2), (yT, B_y)]
                        pg.op("pool", lambda e: e.memset(raw[:, 0:1], 0.0), writes=[B_raw, B_AB, B_ytok])
                        pg.op("pool", lambda e: e.memset(raw[:, L + 1:L + 2], 0.0), writes=[B_raw, B_AB, B_ytok])
                        yield
                        for ci in range(3):
                            yield from proj_seq(ci * 128, lambda t0, psB, psT: pg.op(
                                "act", lambda e: e.activation(out=raw[:, 1 + t0:1 + t0 + NT], in_=psT[:, 0:NT], func=AF.Copy),
                                reads=[psB], writes=[B_raw]))
                            yield from shortconv(ci, dsts[ci][0], dsts[ci][1])
                        pg.op("pool", lambda e: e.memset(raw[:, 0:1], 0.0), reads=[B_raw], writes=[B_AB, B_ytok])
                        yield
                        for n in range(2):
                            gate_ap, gate_b = (G1, B_G1) if n == 0 else (x2c, B_G2)
                            for blk in range(nb):
                                tF, B_tF = tFs[blk % 2]
                                tB, B_tB = tBs[blk % 2]
                                dw, B_dw = dws[blk % 2]
                                psB, psT = pg.ps()
                                pg.mm(psB, [(psT[:, 0:256], a3[0:64, blk * 128:(blk + 1) * 128],
                                             w4c3[0:64, 2 * n:2 * n + 2, :], True, True)], reads=[B_a3, B_w4c])
                                pg.op("act", lambda e, blk=blk, dw=dw: e.activation(out=dw, in_=dlc, func=AF.Exp, scale=tcol[:, blk:blk + 1]),
                                      reads=[B_dlc, B_small], writes=[B_dw])
                                yield
                                pg.op("dve", lambda e, psT=psT, tF=tF, dw=dw: e.tensor_tensor(out=tF, in0=psT[:, 0:128], in1=dw, op=ALU.mult),
                                      reads=[psB, B_dw], writes=[B_tF])
                                if blk == 0:
                                    pg.op("dve", lambda e, psT=psT, tB=tB, dw=dw: e.scalar_tensor_tensor(out=tB, in0=psT[:, 128:256], scalar=bm[:, 0:1], in1=dw,
                                                                                                     op0=ALU.mult, op1=ALU.mult),
                                          reads=[psB, B_dw, B_small], writes=[B_tB])
                                else:
                                    pg.op("dve", lambda e, psT=psT, tB=tB, dw=dw: e.tensor_tensor(out=tB, in0=psT[:, 128:256], in1=dw, op=ALU.mult),
                                          reads=[psB, B_dw], writes=[B_tB])
                                yield
                                pg.op("pool", lambda e, blk=blk, tF=tF, tB=tB: e.tensor_tensor(out=YAB3[:, blk, 0:128], in0=tF, in1=tB, op=ALU.add),
                                      reads=[B_tF, B_tB], writes=[B_AB])
                                pg.op("pool", lambda e, blk=blk, tF=tF, tB=tB: e.tensor_tensor(out=YAB3[:, blk, 256:384], in0=tF, in1=tB, op=ALU.subtract),
                                      reads=[B_tF, B_tB], writes=[B_AB])
                                yield
                            for b0 in range(0, nb, 4):
                                nbb = min(4, nb - b0)
                                psB, psT = pg.ps()

                                def fn(e, psT=psT, b0=b0, nbb=nbb):
                                    r = None
                                    for bb in range(nbb):
                                        r = e.transpose(out=psT[:, bb * 128:(bb + 1) * 128], in_=yT[:, (b0 + bb) * 128:(b0 + bb + 1) * 128],
                                                        identity=ident[:])
                                    return r
                                pg.op("pe", fn, reads=[B_y, B_const], writes=[psB])
                                yield
                                pg.op("act", lambda e, psT=psT, b0=b0, nbb=nbb: e.activation(
                                    out=YAB3[:, b0:b0 + nbb, 128:256], in_=psT[:, 0:nbb * 128].rearrange("p (b n) -> p b n", n=128), func=AF.Copy),
                                    reads=[psB], writes=[B_ytok])
                                yield
                            for kc in range(nkc):
                                kn = min(128, L + 1 - kc * 128)
                                tabs2 = [get_ftab(a, kc) for a in range(2)]
                                psB, psT = pg.ps()
                                mms = []
                                for sc in range(nb):
                                    mms.append((psT[0:kn, 0:256], tabs2[0][0][:, sc, 0:kn], YAB3[:, sc, 0:256], sc == 0, sc == nb - 1))
                                for sc in range(nb):
                                    mms.append((psT[0:kn, 256:512], tabs2[1][0][:, sc, 0:kn], YAB3[:, sc, 128:384], sc == 0, sc == nb - 1))
                                pg.mm(psB, mms, reads=[tabs2[0][1], tabs2[1][1], B_ytok, B_AB])
                                yield
                                pg.op("act", lambda e, psT=psT, kn=kn: e.activation(out=Xs[0:kn, :], in_=psT[0:kn, 128:384], func=AF.Copy),
                                      reads=[psB], writes=[B_Xs])
                                pg.op("act", lambda e, psT=psT, kn=kn, kc=kc: e.activation(out=Hs[0:kn, 0:128], in_=psT[0:kn, 0:128], func=AF.Copy,
                                                                                        scale=wk[0:kn, kc:kc + 1]),
                                      reads=[psB, B_small], writes=[B_Hs])
                                pg.op("act", lambda e, psT=psT, kn=kn, kc=kc: e.activation(out=Hs[0:kn, 128:256], in_=psT[0:kn, 384:512], func=AF.Copy,
                                                                                        scale=wk[0:kn, kc:kc + 1]),
                                      reads=[psB, B_small], writes=[B_Hs])
                                yield
                                if kc == 0:
                                    dbg("hy_Xs", Xs, B_Xs)
                                    dbg("hy_Hs", Hs, B_Hs)
                                (t1, b1), (t2, b2), (t3, b3), (t4, b4) = tt4
                                pg.op("dve", lambda e, kn=kn, t1=t1: e.tensor_tensor(out=t1[0:kn, :], in0=Xs[0:kn, 0:128], in1=Hs[0:kn, 0:128], op=ALU.mult),
                                      reads=[B_Xs, B_Hs], writes=[b1])
                                pg.op("dve", lambda e, kn=kn, t2=t2: e.tensor_tensor(out=t2[0:kn, :], in0=Xs[0:kn, 128:256], in1=Hs[0:kn, 128:256], op=ALU.mult),
                                      reads=[B_Xs, B_Hs], writes=[b2])
                                pg.op("pool", lambda e, kn=kn, t3=t3: e.tensor_tensor(out=t3[0:kn, :], in0=Xs[0:kn, 0:128], in1=Hs[0:kn, 128:256], op=ALU.mult),
                                      reads=[B_Xs, B_Hs], writes=[b3])
                                pg.op("pool", lambda e, kn=kn, t4=t4: e.tensor_tensor(out=t4[0:kn, :], in0=Xs[0:kn, 128:256], in1=Hs[0:kn, 0:128], op=ALU.mult),
                                      reads=[B_Xs, B_Hs], writes=[b4])
                                yield
                                pg.op("dve", lambda e, kn=kn, kc=kc, t1=t1, t2=t2: e.tensor_tensor(out=Zb[0:kn, kc, 0, :], in0=t1[0:kn, :], in1=t2[0:kn, :], op=ALU.subtract),
                                      reads=[b1, b2], writes=[B_R1])
                                pg.op("pool", lambda e, kn=kn, kc=kc, t3=t3, t4=t4: e.tensor_tensor(out=Zb[0:kn, kc, 1, :], in0=t3[0:kn, :], in1=t4[0:kn, :], op=ALU.add),
                                      reads=[b3, b4], writes=[B_R1])
                                yield
                            dbg("hy_Z", R1[:, 0:nkc * 128].bitcast(BF16), B_R1, BF16)
                            dcol = hd3[:, n, cb:cb + 1]
                            for ti in range(ntile2):
                                t0 = ti * NT2
                                psEB, psE = pg.ps()
                                psOB, psO = pg.ps()
                                for (pB, pT, isE) in ((psEB, psE, True), (psOB, psO, False)):
                                    for kc in range(nkc):
                                        kn = min(128, L + 1 - kc * 128)
                                        odd = kc < nodd
                                        a = (1 if odd else 0) if isE else (0 if odd else 1)
                                        i_ap, i_b = get_itab(kc, ti, a)
                                        pg.mm(pB, [(pT[:, 0:NT2], Zb[0:kn, kc, a, :], i_ap[0:kn, :], kc == 0, kc == nkc - 1)],
                                              reads=[B_R1, i_b])
                                pg.op("dve", lambda e, psE=psE, t0=t0, dcol=dcol: e.scalar_tensor_tensor(
                                    out=etmp[:, 0:NT2], in0=yT[:, t0:t0 + NT2], scalar=dcol, in1=psE[:, 0:NT2],
                                    op0=ALU.mult, op1=ALU.add), reads=[psEB, B_y, B_small], writes=[B_etmp])
                                pg.op("dve", lambda e, psO=psO: e.tensor_tensor(
                                    out=etmp[:, 0:NT2], in0=etmp[:, 0:NT2], in1=psO[:, 0:NT2], op=ALU.add),
                                    reads=[psOB], writes=[B_etmp])
                                dbg("hy_etmp", etmp[:, 0:NT2], B_etmp)
                                if resident:
                                    for q_ in range(6):
                                        dbg("hy_itr%d" % q_, itr[q_][0], itr[q_][1], BF16)
                                pg.op("dve", lambda e, t0=t0, gate_ap=gate_ap: e.tensor_tensor(
                                    out=yT[:, t0:t0 + NT2], in0=etmp[:, 0:NT2], in1=gate_ap[:, t0:t0 + NT2], op=ALU.mult),
                                    reads=[B_etmp, gate_b], writes=[B_y])
                                c1 = 1 if t0 == 0 else 0
                                n2 = NT2 - c1
                                jlo = L - t0 - NT2 + 1
                                pg.op("dve", lambda e, psE=psE, jlo=jlo, n2=n2, c1=c1, dcol=dcol: e.scalar_tensor_tensor(
                                    out=etmp2[:, 0:n2], in0=yT[:, jlo:jlo + n2], scalar=dcol, in1=psE[:, c1:NT2][:, ::-1],
                                    op0=ALU.mult, op1=ALU.add), reads=[psEB, B_y, B_small], writes=[B_etmp2])
                                pg.op("dve", lambda e, psO=psO, n2=n2, c1=c1: e.tensor_tensor(
                                    out=etmp2[:, 0:n2], in0=etmp2[:, 0:n2], in1=psO[:, c1:NT2][:, ::-1], op=ALU.subtract),
                                    reads=[psOB], writes=[B_etmp2])
                                pg.op("dve", lambda e, jlo=jlo, n2=n2, gate_ap=gate_ap: e.tensor_tensor(
                                    out=yT[:, jlo:jlo + n2], in0=etmp2[:, 0:n2], in1=gate_ap[:, jlo:jlo + n2], op=ALU.mult),
                                    reads=[B_etmp2, gate_b], writes=[B_y])
                                yield
                            psB, psT = pg.ps()
                            mms = []
                            for kc in range(nkc):
                                kn = min(128, L + 1 - kc * 128)
                                for a in range(2):
                                    mms.append((psT[:, 0:1], Zb[0:kn, kc, a, :], hcol[0:kn, kc * 2 + a:kc * 2 + a + 1],
                                                kc == 0 and a == 0, kc == nkc - 1 and a == 1))
                            pg.mm(psB, mms, reads=[B_R1, B_hcol])
                            yield
                            hh = L // 2
                            pg.op("dve", lambda e, psT=psT, dcol=dcol: e.scalar_tensor_tensor(
                                out=etmp[:, 0:1], in0=yT[:, hh:hh + 1], scalar=dcol, in1=psT[:, 0:1],
                                op0=ALU.mult, op1=ALU.add), reads=[psB, B_y, B_small], writes=[B_etmp])
                            pg.op("dve", lambda e, gate_ap=gate_ap: e.tensor_tensor(
                                out=yT[:, hh:hh + 1], in0=etmp[:, 0:1], in1=gate_ap[:, hh:hh + 1], op=ALU.mult),
                                reads=[B_etmp, gate_b], writes=[B_y])
                            yield
                            dbg("hy_y1", yT, B_y)
                            if n == 0:
                                yield from proj_seq(3 * 128, lambda t0, psB, psT: pg.op(
                                    "act", lambda e: e.activation(out=G1[:, t0:t0 + NT], in_=psT[:, 0:NT], func=AF.Silu),
                                    reads=[psB], writes=[B_G1]))
                        pg.op("dve", lambda e: e.tensor_tensor(out=x2c, in0=yT, in1=G1, op=ALU.mult),
                              reads=[B_y, B_G1], writes=[B_G2])
                        yield

                    run_staged([hy_seq(s_) for s_ in range(nseq)])
                    B_G2 = B_G2s
                    out_proj_acc(li, wo3, B_wo, 1, G23, B_G2s)
                ar["off"] = mark0
                pg.barrier()

            for li in range(nl):
                pg.barrier()
                compute_h(li)
                if gi == 0 and li + 1 < nl:
                    emit_mod(li + 1)
                kind, j = li % 3, li // 3
                ar["hw"] = 0
                if kind == 0:
                    gla_layer(li, j)
                elif kind == 1:
                    fnet_layer(li)
                else:
                    hyena_layer(li)
                print("arena high-water group", gi, "layer", li, ar["hw"], "/", NAR)

            pg.barrier()
            mark = ar["off"]
            sq, B_sq = ab(DC * 512, "fsq")
            rr, B_r = af(512, "frstd")
            tmps = [af(512, "ftmp%d" % i) for i in range(2)]
            osts = [af(D, "ost%d" % i) for i in range(2)]
            for tt in range(ntt):
                ts = slice(tt * 512, (tt + 1) * 512)
                rms_rstd(tt, sq, B_sq, rr, B_r)
                for c in range(DC):
                    pg.op("dve", lambda e, c=c, ts=ts: e.scalar_tensor_tensor(
                        out=xT[:, c, ts], in0=xT[:, c, ts], scalar=fng[:, c:c + 1], in1=rr, op0=ALU.mult, op1=ALU.mult),
                        reads=[B_r, B_const], writes=[B_x[tt]])
                for b4 in range(4):
                    bt = tt * 4 + b4
                    o_ap, o_b = osts[bt % 2]
                    for hf in range(2):
                        psB, psT = pg.ps()

                        def fn(e, psT=psT, hf=hf, bt=bt):
                            r = None
                            for c4 in range(4):
                                c = hf * 4 + c4
                                r = e.transpose(out=psT[:, c4 * 128:(c4 + 1) * 128], in_=xT[:, c, bt * 128:(bt + 1) * 128],
                                                identity=ident[:])
                            return r
                        pg.op("pe", fn, reads=[B_x[tt], B_const], writes=[psB])
                        if hf == 0:
                            pg.op("act", lambda e, o_ap=o_ap, psT=psT: e.activation(out=o_ap[:, 0:512], in_=psT[:, :], func=AF.Copy),
                                  reads=[psB], writes=[o_b])
                        else:
                            pg.op("dve", lambda e, o_ap=o_ap, psT=psT: e.tensor_copy(out=o_ap[:, 512:1024], in_=psT[:, :]),
                                  reads=[psB], writes=[o_b])
                    pg.dma("sp", y_d[bt * 128:(bt + 1) * 128, :], o_ap, src=o_b, final=True)
            ar["off"] = mark

        run_group(0, xp_d, yp_d, 4 * LP, 4, LP, 0)
        run_group(1, xs_d, ys_d, LS, 1, LS, 1)

        pg._wait("sp", pg.final)
        pg.barrier()

        with nc.Block() as block:
            @block.tensor
            def _(e):
                for f in pg.streams["pe"]:
                    f(e)

            @block.scalar
            def _(e):
                for f in pg.streams["act"]:
                    f(e)

            @block.vector
            def _(e):
                for f in pg.streams["dve"]:
                    f(e)

            @block.gpsimd
            def _(e):
                for f in pg.streams["pool"]:
                    f(e)

            @block.sync
            def _(e):
                for f in pg.streams["sp"]:
                    f(e)
        print("instructions:", pg.ninstr, "dma sems:", pg.nsem)
    return nc


def _consts():
    bf = ml_dtypes.bfloat16
    c = {}
    c["ident"] = np.eye(128, dtype=np.float32)
    s = np.arange(128)[:, None]
    t = np.arange(128)[None, :]
    cm = np.zeros((128, 4, 129), np.float32)
    cm[:, 0, :128] = (s <= t)
    cm[:, 0, 128] = 1.0
    cm[:, 1, :128] = (s > t)
    cm[:, 2, :128] = (s >= t)
    cm[:, 2, 128] = 1.0
    cm[:, 3, :128] = (s < t)
    c["cm"] = cm.astype(bf)
    mk = np.zeros((128, 2, 128), np.float32)
    mk[:, 0] = (s <= t)
    mk[:, 1] = (s >= t)
    c["mk"] = mk
    return c


_TAB = {}


def _tables():
    if _TAB:
        return _TAB
    bf = ml_dtypes.bfloat16
    f32 = np.float32
    t = {}
    c = np.arange(256)
    ang = 2 * np.pi * ((c[:, None] * c[None, :]) % 256) / 256.0
    t["fcs"] = np.concatenate([np.cos(ang), np.sin(ang)], axis=1).astype(bf)
    for nm, L in (("p", LP), ("s", LS)):
        l = np.arange(L)
        ang = 2 * np.pi * ((l[:, None] * l[None, :]) % L) / float(L)
        sc = 1.0 / math.sqrt(L * 256.0)
        ft = np.stack([np.cos(ang) * sc, -np.sin(ang) * sc])
        nbl = L // 128
        LT2 = min(256, L // 2)
        nlt2 = (L // 2) // LT2
        t["fcol" + nm] = np.ascontiguousarray(ft[0, :, L // 2].reshape(nbl, 128).T).astype(bf)
        ft = ft[:, :, :L // 2].reshape(2, nbl, 128, nlt2, LT2)
        t["ftab" + nm] = np.ascontiguousarray(ft.transpose(3, 2, 0, 1, 4).reshape(nlt2, 128, 2 * nbl * LT2)).astype(bf)
        N = 2 * L
        nkc = (L + 1 + 127) // 128
        F = np.concatenate([np.arange(1, L, 2), np.arange(0, L + 1, 2), -np.ones(nkc * 128 - (L + 1), np.int64)]).astype(np.int64)
        valid = (F >= 0)
        Fk = np.where(valid, F, 0)

        def tfun(r, c):
            ang_ = 2 * np.pi * ((r * c) % N) / float(N)
            return np.stack([np.cos(ang_), -np.sin(ang_)])
        sidx = np.arange(L)
        Tf = tfun(sidx[:, None], Fk[None, :]) * valid[None, None, :]
        hf_ = Tf.reshape(2, nbl, 128, nkc, 128)
        t["htf" + nm] = np.ascontiguousarray(hf_.transpose(0, 3, 2, 1, 4).reshape(2, nkc, 128, L)).astype(bf)
        NT2 = min(L // 2, 512)
        nt2 = (L // 2) // NT2
        tidx = np.arange(L // 2)
        Ti = tfun(Fk[:, None], tidx[None, :]) * valid[None, :, None]
        hi_ = Ti.reshape(2, nkc, 128, nt2, NT2)
        t["hti" + nm] = np.ascontiguousarray(hi_.transpose(1, 3, 2, 0, 4).reshape(nkc, nt2, 128, 2 * NT2)).astype(bf)
        Tc = tfun(Fk, np.full_like(Fk, L // 2)) * valid[None, :]
        t["htic" + nm] = np.ascontiguousarray(Tc.reshape(2, nkc, 128).transpose(2, 1, 0).reshape(128, nkc * 2)).astype(bf)
        wkv = np.where((Fk == 0) | (Fk == L), 1.0, 2.0) / float(N) * valid
        t["wk" + nm] = np.ascontiguousarray(wkv.reshape(nkc, 128).T).astype(f32)
        tl = np.linspace(0.0, 1.0, L, dtype=np.float32).astype(np.float64)
        w = 2.0 * np.pi * np.arange(L, dtype=np.float64) / L
        f = np.linspace(1e-4, 15.0, 16, dtype=np.float32).astype(np.float64)
        zpos = np.concatenate([tl[:, None], np.cos(f[None, :] * w[:, None]), -np.sin(f[None, :] * w[:, None])], axis=1)
        t["zpos" + nm] = np.ascontiguousarray(zpos.T).astype(f32)
        t["tcol" + nm] = np.ascontiguousarray((-tl).reshape(L // 128, 128).T).astype(f32)
    bm = np.ones((128, 1), f32)
    bm[0, 0] = 0.0
    t["bm"] = bm
    mind = math.log(1e-2) / 1.5
    maxd = math.log(1e-2) / 0.3
    delta = np.abs(np.linspace(mind, maxd, D, dtype=np.float32))
    t["delta"] = np.ascontiguousarray(np.broadcast_to(delta[None, :], (128, D))).astype(f32)
    _TAB.update(t)
    return _TAB


def kernel(**inputs):
    nl = int(os.environ.get("KNL", "4"))
    inp = {k: np.asarray(v) for k, v in inputs.items()}
    f32 = np.float32
    consts = _consts()
    dummy = {k: v for k, v in _tables().items() if not k.startswith("_")}

    def pc(v, inner=()):
        return v

    def cols(v):
        return np.ascontiguousarray(v.reshape(-1, 128).T)

    shared = dict(consts)
    shared.update(dummy)
    shared["modw"] = inp["mod_w"]
    shared["modb"] = np.ascontiguousarray(np.stack([cols(inp["mod_b"][l]) for l in range(4)], axis=1))
    shared["ng"] = np.ascontiguousarray(np.stack([cols(inp["norm_g"][l]) for l in range(4)], axis=1))
    shared["fng"] = cols(inp["final_norm_g"])
    shared["gwin"] = inp["gla_w_in"]
    gwdec = np.zeros((2, 33, 1024), f32)
    for j in range(2):
        gwdec[j, 0:16, 0:512] = inp["gla_w_dec"][j, 0]
        gwdec[j, 16:32, 512:1024] = inp["gla_w_dec"][j, 1]
        gwdec[j, 32, 0:512] = inp["gla_b_dec"][j, 0]
        gwdec[j, 32, 512:1024] = inp["gla_b_dec"][j, 1]
    shared["gwdec"] = gwdec
    shared["gon"] = np.ascontiguousarray(np.stack([cols(inp["gla_onorm_g"][j]) for j in range(2)], axis=1))
    shared["gwout"] = inp["gla_w_out"]
    shared["fwin"] = inp["fn_w_in"][0]
    shared["fwout"] = inp["fn_w_out"][0]
    shared["hwin"] = inp["hy_w_in"][0]
    shared["hcw"] = np.ascontiguousarray(np.stack([cols(inp["hy_conv_w"][0, k]) for k in range(3)], axis=1))
    shared["hcb"] = cols(inp["hy_conv_b"][0])
    shared["hw1"] = inp["hy_ffn_w1"][0]
    shared["hw2"] = inp["hy_ffn_w2"][0]
    shared["hw3"] = inp["hy_ffn_w3"][0]
    shared["hw4"] = inp["hy_ffn_w4"][0]
    shared["hb"] = np.ascontiguousarray(np.stack([inp["hy_ffn_b1"][0], inp["hy_ffn_b2"][0], inp["hy_ffn_b3"][0],
                                                  inp["hy_freq"][0]], axis=1))
    shared["hd"] = np.ascontiguousarray(np.stack([cols(inp["hy_d"][0, n]) for n in range(2)], axis=1))
    shared["hwout"] = inp["hy_w_out"][0]
    shared = {k: np.ascontiguousarray(v) for k, v in shared.items()}

    in_maps = []
    for i in range(NCORES):
        s = i // 4
        mp = dict(shared)
        mp["xp"] = np.ascontiguousarray(inp["x_prompt"][4 * i:4 * i + 4].reshape(4 * LP, D))
        mp["xs"] = np.ascontiguousarray(inp["x_sample"][s])
        mp["st"] = np.ascontiguousarray(inp["state_gla"][s])
        cvv = np.stack([cols(inp["c_ctx"]), cols(inp["c"][s])], axis=2)
        mp["cv"] = np.ascontiguousarray(cvv)
        in_maps.append(mp)

    nc = build_program(nl)
    res = run_bass_kernel_spmd(nc, in_maps, core_ids=list(range(NCORES)), trace=bool(os.environ.get("KTRACE")))
    if os.environ.get("KTRACE"):
        print("EXEC_TIME_NS", res.exec_time_ns)
        _TAB["_res"] = res
    r = res.results
    if os.environ.get("KDBG"):
        _TAB["_dbg"] = {k: np.asarray(v).astype(np.float32) for k, v in r[0].items() if k.startswith("dbg_")}
    y_prompt = np.concatenate([r[i]["yp"].reshape(4, LP, D) for i in range(NCORES)], axis=0)
    y_sample = np.stack([r[0]["ys"], r[4]["ys"]], axis=0)
    new_state = np.concatenate([r[i]["ns"] for i in range(NCORES)], axis=0)
    return (y_prompt.astype(f32), y_sample.astype(f32), new_state.astype(f32))
```

```python
import os
import math
import numpy as np
import ml_dtypes
from contextlib import ExitStack
import concourse.bass as bass
import concourse.mybir as mybir
from concourse.bass_utils import run_bass_kernel_spmd

F32 = mybir.dt.float32
BF16 = mybir.dt.bfloat16
AF = mybir.ActivationFunctionType
ALU = mybir.AluOpType

NCORES = 8
D = 1024
DC = 8
EPS = 1e-6
LP = 256
LS = 2048
NAR = 21184


class Buf:
    __slots__ = ("name", "w", "r", "dsem", "dcnt")

    def __init__(self, name):
        self.name = name
        self.w = None
        self.r = {}
        self.dsem = None
        self.dcnt = 0


class Prog:
    ENG = ("pe", "act", "dve", "pool", "sp")

    def __init__(self, nc, es):
        self.nc = nc
        self.es = es
        self.streams = {e: [] for e in self.ENG}
        self.sem = {}
        for e in self.ENG:
            self.sem[e] = es.enter_context(nc.semaphore("s_" + e))
        self.cnt = {e: 0 for e in self.ENG}
        self.waited = {e: {} for e in self.ENG}
        self.dsems = []
        self.nsem = 0
        self.final = []
        self.stores = []
        self.psb = []
        self.psi = 0
        self.ninstr = 0

    def _wait(self, eng, deps):
        best = {}
        for d in deps:
            if d is None:
                continue
            s, v = d
            if best.get(s, (None, 0))[1] < v:
                best[s] = (s, v)
        for s, v in best.values():
            if eng == "pe" and s is self.sem["pe"]:
                continue
            if self.waited[eng].get(s, 0) >= v:
                continue
            self.waited[eng][s] = v
            self.streams[eng].append(lambda e, s=s, v=v: e.wait_ge(s, v))

    def op(self, eng, fn, reads=(), writes=()):
        deps = []
        own = self.sem[eng]
        for b in reads:
            deps.append(b.w)
        for b in writes:
            if b.w is not None and b.w[0] is not own:
                deps.append(b.w)
            deps.extend((s_, v_) for s_, v_ in b.r.items() if s_ is not own)
        self._wait(eng, deps)
        self.cnt[eng] += 1
        sem = self.sem[eng]
        ev = (sem, self.cnt[eng])
        self.streams[eng].append(lambda e, fn=fn, sem=sem: fn(e).then_inc(sem, 1))
        self.ninstr += 1
        for b in writes:
            b.w = ev
            b.r = {}
        for b in reads:
            if b not in writes:
                b.r[sem] = ev[1]

    def new_dsem(self):
        self.nsem += 1
        return self.es.enter_context(self.nc.semaphore("d%d" % self.nsem))

    def dma(self, q, out, in_, dst=None, src=None, final=False, slow=False):
        buf = dst if dst is not None else src
        if buf.dsem is None:
            buf.dsem = self.new_dsem()
        deps = []
        if dst is not None:
            if dst.w is not None and dst.w[0] is not dst.dsem:
                deps.append(dst.w)
            deps.extend(dst.r.items())
        if src is not None:
            deps.append(src.w)
        self._wait(q, deps)
        buf.dcnt += 16
        ds = buf.dsem
        ev = (ds, buf.dcnt)
        if slow:
            self.streams[q].append(lambda e, out=out, in_=in_, ds=ds: e.dma_start(out=out, in_=in_, allow_slow_non_contiguous=True).then_inc(ds, 16))
        else:
            self.streams[q].append(lambda e, out=out, in_=in_, ds=ds: e.dma_start(out=out, in_=in_).then_inc(ds, 16))
        self.ninstr += 1
        if dst is not None:
            dst.w = ev
            dst.r = {}
        if src is not None:
            src.r[ds] = ev[1]
            self.stores.append(ev)
        if final:
            self.final.append(ev)

    def barrier(self):
        evs = [(self.sem[e], self.cnt[e]) for e in self.ENG if self.cnt[e] > 0] + self.stores
        self.stores = []
        for e in self.ENG:
            self._wait(e, evs)

    def ps(self):
        b = self.psb[self.psi % len(self.psb)]
        self.psi += 1
        return b

    def mm(self, psbuf, mms, reads):
        def fn(e, mms=mms):
            r = None
            for (o, l, rh, st, sp) in mms:
                r = e.matmul(o, lhsT=l, rhs=rh, start=st, stop=sp)
            return r
        self.op("pe", fn, reads=reads, writes=[psbuf])
        self.ninstr += len(mms) - 1


def build_program(nl):
    nc = bass.Bass("TRN2", target_bir_lowering=False)
    es = ExitStack()
    with es:
        def din(name, shape, dt=F32):
            return nc.dram_tensor(name, list(shape), dt, kind="ExternalInput").ap()

        def dout(name, shape):
            return nc.dram_tensor(name, list(shape), F32, kind="ExternalOutput").ap()

        xp_d = din("xp", [4 * LP, D])
        xs_d = din("xs", [LS, D])
        st_d = din("st", [2, 2, 4, 128, 256])
        cv_d = din("cv", [128, DC, 2])
        modw_d = din("modw", [4, D, 3 * D])
        modb_d = din("modb", [128, 4, 24])
        ng_d = din("ng", [128, 4, DC])
        fng_d = din("fng", [128, DC])
        gwin_d = din("gwin", [2, D, 3104])
        gwdec_d = din("gwdec", [2, 33, 1024])
        gon_d = din("gon", [128, 2, 2])
        gwout_d = din("gwout", [2, D, D])
        fwin_d = din("fwin", [D, 2 * D])
        fwout_d = din("fwout", [D, D])
        hwin_d = din("hwin", [D, 4 * D])
        hcw_d = din("hcw", [128, 3, 24])
        hcb_d = din("hcb", [128, 24])
        hw1_d = din("hw1", [33, 64])
        hw2_d = din("hw2", [64, 64])
        hw3_d = din("hw3", [64, 64])
        hw4_d = din("hw4", [64, 4 * D])
        hb_d = din("hb", [64, 4])
        hd_d = din("hd", [128, 2, DC])
        hwout_d = din("hwout", [D, D])
        ident_d = din("ident", [128, 128])
        cm_d = din("cm", [128, 4, 129], BF16)
        mk_d = din("mk", [128, 2, 128])
        fcs_d = din("fcs", [256, 512], BF16)
        flt2_ = {LP: min(256, LP // 2), LS: min(256, LS // 2)}
        ftab_d = {L_: din("ftab" + nm_, [(L_ // 2) // flt2_[L_], 128, 2 * (L_ // 128) * flt2_[L_]], BF16) for nm_, L_ in (("p", LP), ("s", LS))}
        fcol_d = {L_: din("fcol" + nm_, [128, L_ // 128], BF16) for nm_, L_ in (("p", LP), ("s", LS))}
        nkc_ = {LP: (LP + 1 + 127) // 128, LS: (LS + 1 + 127) // 128}
        htf_d = {LP: din("htfp", [2, nkc_[LP], 128, LP], BF16), LS: din("htfs", [2, nkc_[LS], 128, LS], BF16)}
        nt2_ = {LP: min(LP // 2, 512), LS: min(LS // 2, 512)}
        hti_d = {L_: din("hti" + nm_, [nkc_[L_], (L_ // 2) // nt2_[L_], 128, 2 * nt2_[L_]], BF16) for nm_, L_ in (("p", LP), ("s", LS))}
        htic_d = {L_: din("htic" + nm_, [128, nkc_[L_] * 2], BF16) for nm_, L_ in (("p", LP), ("s", LS))}
        zpos_d = {LP: din("zposp", [33, LP]), LS: din("zposs", [33, LS])}
        tcol_d = {LP: din("tcolp", [128, LP // 128]), LS: din("tcols", [128, LS // 128])}
        bm_d = din("bm", [128, 1])
        wk_d = {LP: din("wkp", [128, 3]), LS: din("wks", [128, 17])}
        delta_d = din("delta", [128, D])

        yp_d = dout("yp", [4 * LP, D])
        ys_d = dout("ys", [LS, D])
        ns_d = dout("ns", [4, 2, 2, 4, 128, 256])

        pg = Prog(nc, es)
        dbg_on = bool(os.environ.get("KDBG"))
        dbg_seen = set()

        def dbg(name, ap, buf, dt=F32):
            if not dbg_on or name in dbg_seen:
                return
            dbg_seen.add(name)
            shp = list(ap.shape)
            dd = nc.dram_tensor("dbg_" + name, shp, dt, kind="ExternalOutput").ap()
            pg.dma("sp", dd, ap, src=buf, final=True)

        def sb(name, shape, dt=F32):
            return es.enter_context(nc.sbuf_tensor("sb_" + name, list(shape), dt))

        ident = sb("ident", [128, 128])
        ones_bf = sb("ones_bf", [128, 128], BF16)
        cm = sb("cm", [128, 4, 129], BF16)
        mk = sb("mk", [128, 2, 128])
        SH = sb("SH", [128, 4, DC, 2])
        AA = sb("AA", [128, 4, DC, 2])
        GG = sb("GG", [128, 4, DC, 2])
        ng = sb("ng", [128, 4, DC])
        fng = sb("fng", [128, DC])
        gon = sb("gon", [128, 2, 2])
        xT = sb("xT", [128, DC, LS])
        hT = sb("hT", [128, DC, LS], BF16)
        Wr = [sb("W0", [128, DC, 768], BF16), sb("W1", [128, DC, 768], BF16)]
        AR = sb("arena", [128, NAR])
        for i in range(8):
            pst = es.enter_context(nc.psum_tensor("ps%d" % i, [128, 512], F32))
            pg.psb.append((Buf("ps%d" % i), pst))

        B_const = Buf("const")
        B_mod = Buf("mod")
        B_W = [Buf("W0"), Buf("W1")]
        wi = [0]

        def next_w():
            i = wi[0] % 2
            wi[0] += 1
            return B_W[i], Wr[i]

        ar = {"off": 0}

        def ar_reset():
            ar["off"] = 0

        pbufs = {}

        def pb(name):
            if name not in pbufs:
                pbufs[name] = Buf(name)
            return pbufs[name]

        def af(n, name="a"):
            o = ar["off"]
            ar["off"] += n
            assert ar["off"] <= NAR, ("arena overflow", name, ar["off"])
            ar["hw"] = max(ar.get("hw", 0), ar["off"])
            return AR[:, o:o + n], pb(name)

        def ab(n, name="a"):
            n2 = (n + 1) // 2
            o = ar["off"]
            ar["off"] += n2
            assert ar["off"] <= NAR, ("arena overflow", name, ar["off"])
            ar["hw"] = max(ar.get("hw", 0), ar["off"])
            return AR[:, o:o + n2].bitcast(BF16)[:, 0:n], pb(name)

        pg.dma("sp", ident[:], ident_d[:, :], dst=B_const)
        pg.dma("sp", cm[:], cm_d[:, :, :], dst=B_const)
        pg.dma("sp", mk[:], mk_d[:, :, :], dst=B_const)
        pg.dma("sp", ng[:], ng_d[:, :, :], dst=B_const)
        pg.dma("sp", fng[:], fng_d[:, :], dst=B_const)
        pg.dma("sp", gon[:], gon_d[:, :, :], dst=B_const)
        pg.op("pool", lambda e: e.memset(ones_bf[:], 1.0), writes=[B_const])

        cv = sb("cv", [128, DC * 2])
        cvb = sb("cvb", [128, DC * 2], BF16)
        modb = sb("modbs", [128, 4 * 24])
        mv = sb("mv", [128, 4 * 24 * 2])
        B_cv, B_cvb, B_modb = Buf("cv"), Buf("cvb"), Buf("modb")
        B_mvL = [Buf("mv%d" % l) for l in range(4)]
        B_modL = [Buf("mod%d" % l) for l in range(4)]
        cv3 = cv[:, :].rearrange("p (c m) -> p c m", m=2)
        cvb3 = cvb[:, :].rearrange("p (c m) -> p c m", m=2)
        modb3 = modb[:, :].rearrange("p (l e) -> p l e", e=24)
        mv4 = mv[:, :].rearrange("p (l e m) -> p l e m", l=4, e=24)
        pg.dma("sp", cv3, cv_d[:, :, :], dst=B_cv)
        pg.dma("sp", modb3, modb_d[:, :, :], dst=B_modb)
        pg.op("act", lambda e: e.activation(out=cvb[:, :], in_=cv[:, :], func=AF.Silu), reads=[B_cv], writes=[B_cvb])

        def emit_mod(l):
            psB, psT = pg.ps()
            for quarter in range(4):
                Bw, Wt = next_w()
                pg.dma("pool", Wt[:, :, :], modw_d[l, :, quarter * 768:(quarter + 1) * 768].rearrange("(c p) n -> p c n", p=128), dst=Bw)
                mms = []
                for e6 in range(6):
                    ec = quarter * 6 + e6
                    for kc in range(DC):
                        mms.append((psT[:, ec * 2:ec * 2 + 2], Wt[:, kc, e6 * 128:(e6 + 1) * 128], cvb3[:, kc, :],
                                    kc == 0, kc == DC - 1))
                pg.mm(psB, mms, reads=[Bw, B_cvb])
            pg.op("dve", lambda e: e.tensor_tensor(
                out=mv4[:, l], in0=psT[:, 0:48].rearrange("p (e m) -> p e m", m=2),
                in1=modb3[:, l, :].unsqueeze(2).to_broadcast([128, 24, 2]), op=ALU.add),
                reads=[psB, B_modb], writes=[B_mvL[l]])
            pg.op("dve", lambda e: e.tensor_copy(out=SH[:, l], in_=mv4[:, l, 0:8, :]), reads=[B_mvL[l]], writes=[B_modL[l]])
            pg.op("dve", lambda e: e.tensor_copy(out=GG[:, l], in_=mv4[:, l, 16:24, :]), reads=[B_mvL[l]], writes=[B_modL[l]])
            pg.op("dve", lambda e: e.tensor_scalar(out=AA[:, l], in0=mv4[:, l, 8:16, :], scalar1=1.0, scalar2=None,
                                                   op0=ALU.add), reads=[B_mvL[l]], writes=[B_modL[l]])
            pg.op("dve", lambda e: e.tensor_tensor(out=AA[:, l], in0=AA[:, l], in1=ng[:, l, :].unsqueeze(2).to_broadcast([128, DC, 2]),
                                                   op=ALU.mult), reads=[B_const, B_modL[l]], writes=[B_modL[l]])

        if nl > 0:
            emit_mod(0)

        def run_group(gi, x_d, y_d, T, nseq, L, m):
            ntt = T // 512
            nbt = T // 128
            nb = L // 128
            pg.barrier()
            ar_reset()
            B_x = [pb("x%d" % i) for i in range(ntt)]
            B_h = [pb("h%d" % i) for i in range(ntt)]

            stg = [af(D, "stg%d" % i) for i in range(2)]
            for bt in range(nbt):
                s_ap, s_b = stg[bt % 2]
                pg.dma("sp", s_ap, x_d[bt * 128:(bt + 1) * 128, :], dst=s_b)
                for hf in range(2):
                    psB, psT = pg.ps()

                    def fn(e, s_ap=s_ap, psT=psT, hf=hf):
                        r = None
                        for c4 in range(4):
                            c = hf * 4 + c4
                            r = e.transpose(out=psT[:, c4 * 128:(c4 + 1) * 128], in_=s_ap[:, c * 128:(c + 1) * 128],
                                            identity=ident[:])
                        return r
                    pg.op("pe", fn, reads=[s_b, B_const], writes=[psB])
                    eng = "act" if hf == 0 else "dve"
                    o_ap = xT[:, hf * 4:(hf + 1) * 4, bt * 128:(bt + 1) * 128]
                    i_ap = psT[:, :].rearrange("p (c t) -> p c t", c=4)
                    if eng == "act":
                        pg.op("act", lambda e, o_ap=o_ap, i_ap=i_ap: e.activation(out=o_ap, in_=i_ap, func=AF.Copy),
                              reads=[psB], writes=[B_x[bt // 4]])
                    else:
                        pg.op("dve", lambda e, o_ap=o_ap, i_ap=i_ap: e.tensor_copy(out=o_ap, in_=i_ap),
                              reads=[psB], writes=[B_x[bt // 4]])

            ar_reset()

            def rms_rstd(tt, tmp_sq, B_sq, tmp_r, B_r):
                ts = slice(tt * 512, (tt + 1) * 512)
                sq3 = tmp_sq.rearrange("p (c t) -> p c t", c=DC)
                pg.op("act", lambda e: e.activation(out=sq3, in_=xT[:, :, ts], func=AF.Square),
                      reads=[B_x[tt]], writes=[B_sq])
                psB, psT = pg.ps()
                pg.mm(psB, [(psT[:, :], ones_bf[:], sq3[:, c, :], c == 0, c == DC - 1) for c in range(DC)],
                      reads=[B_sq, B_const])
                pg.op("act", lambda e: e.activation(out=tmp_r, in_=psT[:, :], func=AF.Ln, scale=1.0 / D, bias=EPS),
                      reads=[psB], writes=[B_r])
                pg.op("act", lambda e: e.activation(out=tmp_r, in_=tmp_r, func=AF.Exp, scale=-0.5),
                      reads=[B_r], writes=[B_r])

            def compute_h(li):
                mark = ar["off"]
                gens = []
                for tt in range(ntt):
                    sq, B_sq = ab(DC * 512, "sq%d" % tt)
                    rr, B_r = af(512, "rstd%d" % tt)
                    tmps = [af(512, "htmp%d_%d" % (tt, i)) for i in range(2)]
                    gens.append(h_tile(li, tt, sq, B_sq, rr, B_r, tmps))
                run_staged(gens)
                ar["off"] = mark
                pg.barrier()

            def h_tile(li, tt, sq, B_sq, rr, B_r, tmps):
                ts = slice(tt * 512, (tt + 1) * 512)
                sq3 = sq.rearrange("p (c t) -> p c t", c=DC)
                pg.op("act", lambda e: e.activation(out=sq3, in_=xT[:, :, ts], func=AF.Square),
                      reads=[B_x[tt]], writes=[B_sq])
                yield
                psB, psT = pg.ps()
                pg.mm(psB, [(psT[:, :], ones_bf[:], sq3[:, c, :], c == 0, c == DC - 1) for c in range(DC)],
                      reads=[B_sq, B_const])
                yield
                pg.op("act", lambda e: e.activation(out=rr, in_=psT[:, :], func=AF.Ln, scale=1.0 / D, bias=EPS),
                      reads=[psB], writes=[B_r])
                yield
                pg.op("act", lambda e: e.activation(out=rr, in_=rr, func=AF.Exp, scale=-0.5),
                      reads=[B_r], writes=[B_r])
                yield
                for c in range(DC):
                    t_ap, t_b = tmps[c % 2]
                    pg.op("dve", lambda e, t_ap=t_ap, c=c: e.tensor_tensor(
                        out=t_ap, in0=xT[:, c, ts], in1=rr, op=ALU.mult), reads=[B_x[tt], B_r], writes=[t_b])
                    pg.op("act", lambda e, t_ap=t_ap, c=c: e.activation(
                        out=hT[:, c, ts], in_=t_ap, func=AF.Identity, scale=AA[:, li, c, m:m + 1],
                        bias=SH[:, li, c, m:m + 1]), reads=[t_b, B_modL[li]], writes=[B_h[tt]])
                    yield

            def run_staged(gens):
                outs = [None] * len(gens)
                live = list(range(len(gens)))
                while live:
                    nxt = []
                    for gi_ in live:
                        try:
                            next(gens[gi_])
                            nxt.append(gi_)
                        except StopIteration as si:
                            outs[gi_] = si.value
                    live = nxt
                return outs

            def load_w_cols(dram2d, colspecs, queue="pool"):
                Bw, Wt = next_w()
                o = 0
                for (c0, n) in colspecs:
                    pg.dma(queue, Wt[:, :, o:o + n], dram2d[:, c0:c0 + n].rearrange("(c p) n -> p c n", p=128), dst=Bw)
                    o += n
                return Bw, Wt

            def out_proj_acc(li, wo_ap, B_wo, nk, yT_ap, B_y):
                for tt in range(ntt):
                    ts = slice(tt * 512, (tt + 1) * 512)
                    for dc in range(DC):
                        psB, psT = pg.ps()
                        pg.mm(psB, [(psT[:, :], wo_ap[:, k, dc * 128:(dc + 1) * 128], yT_ap[:, k, ts], k == 0, k == nk - 1)
                                    for k in range(nk)], reads=[B_wo] + (list(B_y) if isinstance(B_y, (list, tuple)) else [B_y]))
                        pg.op("dve", lambda e, psT=psT, dc=dc, ts=ts: e.scalar_tensor_tensor(
                            out=xT[:, dc, ts], in0=psT[:, :], scalar=GG[:, li, dc, m:m + 1], in1=xT[:, dc, ts],
                            op0=ALU.mult, op1=ALU.add), reads=[psB, B_modL[li]], writes=[B_x[tt]])

            def proj_fm(Bw, Wt, c0, dst_fn, reads_extra=()):
                for tt in range(ntt):
                    ts = slice(tt * 512, (tt + 1) * 512)
                    psB, psT = pg.ps()
                    pg.mm(psB, [(psT[:, :], Wt[:, c, c0:c0 + 128], hT[:, c, ts], c == 0, c == DC - 1) for c in range(DC)],
                          reads=[Bw, B_h[tt]])
                    dst_fn(tt, ts, psB, psT)

            def gla_layer(li, j):
                mark0 = ar["off"]
                lra, B_lra = ab(T, "lra")
                w2a, B_w2a = ab(1024, "w2a")
                wlr, B_wlr = ab(DC * 32, "wlr")
                wlr3 = wlr.rearrange("p (c n) -> p c n", c=DC)
                pg.dma("pool", w2a[0:33, :], gwdec_d[j, :, :], dst=B_w2a)
                pg.dma("pool", wlr3, gwin_d[j, :, 3072:3104].rearrange("(c p) n -> p c n", p=128), dst=B_wlr)
                pg.op("pool", lambda e: e.memset(lra[32:33, :], 1.0), writes=[B_lra])
                for tt in range(ntt):
                    ts = slice(tt * 512, (tt + 1) * 512)
                    psB, psT = pg.ps()
                    pg.mm(psB, [(psT[0:32, :], wlr3[:, c, :], hT[:, c, ts], c == 0, c == DC - 1) for c in range(DC)],
                          reads=[B_wlr, B_h[tt]])
                    pg.op("act", lambda e, psT=psT, ts=ts: e.activation(out=lra[0:32, ts], in_=psT[0:32, :], func=AF.Copy),
                          reads=[psB], writes=[B_lra])
                mark1 = ar["off"]
                for h in range(4):
                    ar["off"] = mark1
                    gwin = gwin_d[j]
                    Bw, Wt = load_w_cols(gwin, [(h * 128, 128), (512 + h * 128, 128), (1024 + h * 256, 256),
                                                (2048 + h * 256, 256)])
                    wo, B_wo = ab(2 * D, "wo")
                    wo3 = wo.rearrange("p (k n) -> p k n", k=2)
                    pg.dma("pool", wo3, gwout_d[j, h * 256:(h + 1) * 256, :].rearrange("(k p) n -> p k n", p=128), dst=B_wo)
                    qky, B_qky = ab(2 * T, "qky")
                    qk3 = qky.rearrange("p (a t) -> p a t", a=2)
                    kvt, B_kvt = ab(nbt * 384, "kvt")
                    kvt3 = kvt.rearrange("p (b n) -> p b n", n=384)
                    late_rs = (nseq == 1)
                    if late_rs:
                        rs, B_rs = kvt[:, 0:2 * T], B_kvt
                    else:
                        rs, B_rs = ab(2 * T, "rs")
                    rs3 = rs.rearrange("p (a t) -> p a t", a=2)
                    oT, _ = af(2 * L, "oT")
                    oT3 = oT.rearrange("p (a t) -> p a t", a=2)
                    B_o = [pb("o%d" % i) for i in range(nb)]
                    proj_fm(Bw, Wt, 0, lambda tt, ts, psB, psT: pg.op(
                        "act", lambda e: e.activation(out=qk3[:, 0, ts], in_=psT[:, :], func=AF.Copy, scale=128.0 ** -0.5),
                        reads=[psB], writes=[B_qky]))
                    proj_fm(Bw, Wt, 128, lambda tt, ts, psB, psT: pg.op(
                        "dve", lambda e: e.tensor_copy(out=qk3[:, 1, ts], in_=psT[:, :]), reads=[psB], writes=[B_qky]))
                    def proj_r():
                        for a in range(2):
                            proj_fm(Bw, Wt, 512 + a * 128, lambda tt, ts, psB, psT, a=a: pg.op(
                                "act", lambda e: e.activation(out=rs3[:, a, ts], in_=psT[:, :], func=AF.Silu),
                                reads=[psB], writes=[B_rs]))
                    if not late_rs:
                        proj_r()
                    for bt in range(nbt):
                        bs = slice(bt * 128, (bt + 1) * 128)
                        psB, psT = pg.ps()
                        pg.mm(psB, [(psT[:, 0:384], hT[:, c, bs], Wt[:, c, 128:512], c == 0, c == DC - 1) for c in range(DC)],
                              reads=[Bw, B_h[bt // 4]])
                        pg.op("dve" if bt % 2 else "act",
                              (lambda e, psT=psT, bt=bt: e.tensor_copy(out=kvt3[:, bt, :], in_=psT[:, 0:384])) if bt % 2 else
                              (lambda e, psT=psT, bt=bt: e.activation(out=kvt3[:, bt, :], in_=psT[:, 0:384], func=AF.Copy)),
                              reads=[psB], writes=[B_kvt])
                    dbg("hT", hT[:, :, 0:512], B_h[0], BF16)
                    dbg("qk", qk3[:, :, 0:256], B_qky, BF16)
                    dbg("kvt", kvt3[:, 0:2, :], B_kvt, BF16)
                    dbg("lra", lra[0:33, 0:256], B_lra, BF16)
                    mark2 = ar["off"]
                    Sf = [af(256, "S%d" % d) for d in range(2)]
                    Sb = [[ab(256, "Sb%d%d" % (d, i)) for i in range(2)] for d in range(2)]
                    t1s = [[[af(128, "t1_%d_%d_%d" % (gp_, d, i)) for i in range(2)] for d in range(2)] for gp_ in range(2)]
                    gps = [[[ab(128, "gp_%d_%d_%d" % (gp_, d, i)) for i in range(2)] for d in range(2)] for gp_ in range(2)]
                    E1s = [[[af(129, "E1_%d_%d_%d" % (gp_, d, i)) for i in range(2)] for d in range(2)] for gp_ in range(2)]
                    E2s = [[[af(128, "E2_%d_%d_%d" % (gp_, d, i)) for i in range(2)] for d in range(2)] for gp_ in range(2)]
                    E3s = [[[af(128, "E3_%d_%d_%d" % (gp_, d, i)) for i in range(2)] for d in range(2)] for gp_ in range(2)]
                    qts = [[[ab(128, "qt_%d_%d_%d" % (gp_, d, i)) for i in range(2)] for d in range(2)] for gp_ in range(2)]
                    kts = [[[ab(128, "kt_%d_%d_%d" % (gp_, d, i)) for i in range(2)] for d in range(2)] for gp_ in range(2)]
                    khs = [[[ab(128, "kh_%d_%d_%d" % (gp_, d, i)) for i in range(2)] for d in range(2)] for gp_ in range(2)]
                    ats = [[[ab(128, "at_%d_%d_%d" % (gp_, d, i)) for i in range(2)] for d in range(2)] for gp_ in range(2)]
                    nt = min(L, 256)
                    sq, B_sq = ab(2 * nt, "osq")
                    sq3 = sq.rearrange("p (a t) -> p a t", a=2)
                    rr, B_rr = af(nt, "orstd")
                    tmp, B_tmp = af(2 * nt, "otmp")
                    tmp3 = tmp.rearrange("p (a t) -> p a t", a=2)
                    def gla_prep(s, step, d, gpar):
                        blk = step if d == 0 else nb - 1 - step
                        tok0 = s * L + blk * 128
                        tk = slice(tok0, tok0 + 128)
                        btg = tok0 // 128
                        pi = step % 2
                        t1, B_t1 = t1s[gpar][d][pi]
                        gp, B_gp = gps[gpar][d][pi]
                        E1, B_E1 = E1s[gpar][d][pi]
                        E2, B_E2 = E2s[gpar][d][pi]
                        E3, B_E3 = E3s[gpar][d][pi]
                        qt, B_qt = qts[gpar][d][pi]
                        kt, B_kt = kts[gpar][d][pi]
                        kh, B_kh = khs[gpar][d][pi]
                        at, B_at = ats[gpar][d][pi]
                        psB, psT = pg.ps()
                        c0 = d * 512 + h * 128
                        pg.mm(psB, [(psT[:, 0:128], lra[0:33, tk], w2a[0:33, c0:c0 + 128], True, True)],
                              reads=[B_lra, B_w2a])
                        yield
                        pg.op("act", lambda e: e.activation(out=t1, in_=psT[:, 0:128], func=AF.Exp, scale=-1.0),
                              reads=[psB], writes=[B_t1])
                        yield
                        pg.op("act", lambda e: e.activation(out=gp, in_=t1, func=AF.Ln, bias=1.0),
                              reads=[B_t1], writes=[B_gp])
                        yield
                        psB2, psT2 = pg.ps()
                        pg.mm(psB2, [(psT2[:, 0:129], gp, cm[:, 2 * d, 0:129], True, True),
                                     (psT2[:, 256:384], cm[:, 2 * d + 1, 0:128], gp, True, True)],
                              reads=[B_gp, B_const])
                        yield
                        pg.op("act", lambda e: e.activation(out=E1, in_=psT2[:, 0:129], func=AF.Exp, scale=-1.0 / 16),
                              reads=[psB2], writes=[B_E1])
                        pg.op("act", lambda e: e.activation(out=E2, in_=psT2[:, 0:128], func=AF.Exp, scale=1.0 / 16),
                              reads=[psB2], writes=[B_E2])
                        pg.op("act", lambda e: e.activation(out=E3, in_=psT2[:, 256:384], func=AF.Exp, scale=-1.0 / 16),
                              reads=[psB2], writes=[B_E3])
                        yield
                        pg.op("dve", lambda e: e.tensor_tensor(out=qt, in0=qk3[:, 0, tk], in1=E1[:, 0:128], op=ALU.mult),
                              reads=[B_qky, B_E1], writes=[B_qt])
                        pg.op("pool", lambda e: e.tensor_tensor(out=kt, in0=qk3[:, 1, tk], in1=E2, op=ALU.mult),
                              reads=[B_qky, B_E2], writes=[B_kt])
                        pg.op("pool", lambda e: e.tensor_tensor(out=kh, in0=kvt3[:, btg, 0:128], in1=E3, op=ALU.mult),
                              reads=[B_kvt, B_E3], writes=[B_kh])
                        yield
                        psB3, psT3 = pg.ps()
                        pg.mm(psB3, [(psT3[:, 0:128], kt, qt, True, True)], reads=[B_kt, B_qt])
                        yield
                        pg.op("dve", lambda e: e.tensor_tensor(out=at, in0=psT3[:, 0:128], in1=mk[:, d, :], op=ALU.mult),
                              reads=[psB3, B_const], writes=[B_at])

                    def gla_state_gen(s, step0, gp):
                        for step in (step0, step0 + 1):
                            for d in range(2):
                                blk = step if d == 0 else nb - 1 - step
                                tok0 = s * L + blk * 128
                                btg = tok0 // 128
                                pi = step % 2
                                E1, B_E1 = E1s[gp][d][pi]
                                qt, B_qt = qts[gp][d][pi]
                                at, B_at = ats[gp][d][pi]
                                kh, B_kh = khs[gp][d][pi]
                                S_ap, S_b = Sf[d]
                                if step == 0:
                                    if m == 1:
                                        pg.dma("sp", S_ap, st_d[j, d, h, :, :], dst=S_b)
                                    else:
                                        pg.op("pool", lambda e, S_ap=S_ap: e.memset(S_ap, 0.0), writes=[S_b])
                                    sb_ap, sb_b = Sb[d][0]
                                    pg.op("act", lambda e, S_ap=S_ap, sb_ap=sb_ap: e.activation(out=sb_ap, in_=S_ap, func=AF.Copy),
                                          reads=[S_b], writes=[sb_b])
                                sbi_ap, sbi_b = Sb[d][step % 2]
                                sbo_ap, sbo_b = Sb[d][(step + 1) % 2]
                                psB5, psT5 = pg.ps()
                                pg.mm(psB5, [(psT5[:, 0:256], kh, kvt3[:, btg, 128:384], True, True)], reads=[B_kh, B_kvt])
                                psB4, psT4 = pg.ps()
                                mms = []
                                for hf in range(2):
                                    mms.append((psT4[:, hf * 128:(hf + 1) * 128], kvt3[:, btg, 128 + hf * 128:256 + hf * 128], at, True, False))
                                    mms.append((psT4[:, hf * 128:(hf + 1) * 128], sbi_ap[:, hf * 128:(hf + 1) * 128], qt, False, True))
                                pg.mm(psB4, mms, reads=[B_kvt, B_at, sbi_b, B_qt])
                                o_ap = oT3[:, :, blk * 128:(blk + 1) * 128]
                                p_ap = psT4[:, 0:256].rearrange("p (a t) -> p a t", a=2)
                                other = nb - 1 - step
                                is_first = (step <= other) if d == 0 else (step < other)
                                if is_first:
                                    pg.op("act", lambda e, o_ap=o_ap, p_ap=p_ap: e.activation(out=o_ap, in_=p_ap, func=AF.Copy),
                                          reads=[psB4], writes=[B_o[blk]])
                                else:
                                    pg.op("dve", lambda e, o_ap=o_ap, p_ap=p_ap: e.tensor_tensor(out=o_ap, in0=o_ap, in1=p_ap, op=ALU.add),
                                          reads=[psB4], writes=[B_o[blk]])
                                pg.op("dve", lambda e, S_ap=S_ap, E1=E1, psT5=psT5: e.scalar_tensor_tensor(
                                    out=S_ap, in0=S_ap, scalar=E1[:, 128:129], in1=psT5[:, 0:256], op0=ALU.mult, op1=ALU.add),
                                    reads=[psB5, B_E1], writes=[S_b])
                                if step < nb - 1:
                                    pg.op("act", lambda e, S_ap=S_ap, sbo_ap=sbo_ap: e.activation(out=sbo_ap, in_=S_ap, func=AF.Copy),
                                          reads=[S_b], writes=[sbo_b])
                                elif m == 0:
                                    pg.dma("sp", ns_d[s, j, d, h, :, :], S_ap, src=S_b, final=True)
                                yield

                    def gla_norm(s):
                        for t0 in range(0, L, nt):
                            blks = range(t0 // 128, (t0 + nt) // 128)
                            gs = slice(s * L + t0, s * L + t0 + nt)
                            ls = slice(t0, t0 + nt)
                            Bos = [B_o[b] for b in blks]
                            pg.op("act", lambda e, sq3=sq3, ls=ls: e.activation(out=sq3, in_=oT3[:, :, ls], func=AF.Square),
                                  reads=Bos, writes=[B_sq])
                            psB, psT = pg.ps()
                            pg.mm(psB, [(psT[:, 0:nt], ones_bf[:], sq3[:, a, :], a == 0, a == 1) for a in range(2)],
                                  reads=[B_sq, B_const])
                            pg.op("act", lambda e, rr=rr, psT=psT, nt=nt: e.activation(out=rr, in_=psT[:, 0:nt], func=AF.Ln, scale=1.0 / 256, bias=EPS),
                                  reads=[psB], writes=[B_rr])
                            pg.op("act", lambda e, rr=rr: e.activation(out=rr, in_=rr, func=AF.Exp, scale=-0.5),
                                  reads=[B_rr], writes=[B_rr])
                            pg.op("dve", lambda e, tmp3=tmp3, ls=ls, rr=rr, nt=nt: e.tensor_tensor(
                                out=tmp3, in0=oT3[:, :, ls], in1=rr.unsqueeze(1).to_broadcast([128, 2, nt]), op=ALU.mult),
                                reads=Bos + [B_rr], writes=[B_tmp])
                            for a in range(2):
                                pg.op("dve", lambda e, tmp3=tmp3, a=a, gs=gs: e.scalar_tensor_tensor(
                                    out=qk3[:, a, gs], in0=tmp3[:, a, :], scalar=gon[:, j, a:a + 1], in1=rs3[:, a, gs],
                                    op0=ALU.mult, op1=ALU.mult), reads=[B_tmp, B_rs, B_const], writes=[B_qky])
                    groups = [(s_, st0) for s_ in range(nseq) for st0 in range(0, nb, 2)]
                    prev = None
                    for gidx, (s_, st0) in enumerate(groups):
                        gp = gidx % 2
                        gens = [gla_prep(s_, st, d, gp) for st in (st0, st0 + 1) for d in range(2)]
                        if prev is not None:
                            gens.append(gla_state_gen(*prev))
                        run_staged(gens)
                        if prev is not None and prev[1] == nb - 2 and not late_rs:
                            gla_norm(prev[0])
                        prev = (s_, st0, gp)
                    run_staged([gla_state_gen(*prev)])
                    if late_rs:
                        proj_r()
                    gla_norm(prev[0])
                    out_proj_acc(li, wo3, B_wo, 2, qk3, B_qky)
                ar["off"] = mark0
                pg.barrier()


            def fnet_layer(li):
                mark0 = ar["off"]
                fcs, B_fcs = ab(2 * 512, "fcs")
                fcs3 = fcs.rearrange("p (c n) -> p c n", c=2)
                pg.dma("sp", fcs3, fcs_d.rearrange("(c p) n -> p c n", p=128), dst=B_fcs)
                LT = min(256, L // 2)
                nlt = (L // 2) // LT
                fcol, B_fcol = ab(nb, "fcol")
                pg.dma("sp", fcol, fcol_d[L][:, :], dst=B_fcol)
                mark1 = ar["off"]
                for g in range(4):
                    ar["off"] = mark1
                    Bw, Wt = load_w_cols(fwin_d, [(g * 256, 256), (1024 + g * 256, 256)])
                    wo, B_wo = ab(2 * D, "wo")
                    wo3 = wo.rearrange("p (k n) -> p k n", k=2)
                    pg.dma("pool", wo3, fwout_d[g * 256:(g + 1) * 256, :].rearrange("(k p) n -> p k n", p=128), dst=B_wo)
                    uy, B_uy = ab(2 * T, "uy")
                    uy3 = uy.rearrange("p (a t) -> p a t", a=2)
                    zs, B_zs = ab(2 * T, "zs")
                    zs3 = zs.rearrange("p (a t) -> p a t", a=2)
                    PQ, B_PQ = ab(nb * 512, "PQ")
                    PQ3 = PQ.rearrange("p (b n) -> p b n", n=512)
                    tabs = [ab(2 * nb * LT, "ftab%d" % i) for i in range(2)]
                    etA, B_etA = af(LT, "fetA")
                    ft1, B_ft1 = af(LT, "fft1")
                    ft2, B_ft2 = af(LT, "fft2")
                    for a in range(2):
                        proj_fm(Bw, Wt, a * 128, lambda tt, ts, psB, psT, a=a: pg.op(
                            "dve", lambda e: e.tensor_copy(out=uy3[:, a, ts], in_=psT[:, :]), reads=[psB], writes=[B_uy]))
                        proj_fm(Bw, Wt, 256 + a * 128, lambda tt, ts, psB, psT, a=a: pg.op(
                            "act", lambda e: e.activation(out=zs3[:, a, ts], in_=psT[:, :], func=AF.Silu),
                            reads=[psB], writes=[B_zs]))
                    ti = 0
                    for s in range(nseq):
                        for blk in range(nb):
                            tk = slice(s * L + blk * 128, s * L + blk * 128 + 128)
                            psB, psT = pg.ps()
                            pg.mm(psB, [(psT[:, :], uy3[:, cc, tk], fcs3[:, cc, :], cc == 0, cc == 1) for cc in range(2)],
                                  reads=[B_uy, B_fcs])
                            if blk % 2:
                                pg.op("dve", lambda e, psT=psT, blk=blk: e.tensor_copy(out=PQ3[:, blk, :], in_=psT[:, :]),
                                      reads=[psB], writes=[B_PQ])
                            else:
                                pg.op("act", lambda e, psT=psT, blk=blk: e.activation(out=PQ3[:, blk, :], in_=psT[:, :], func=AF.Copy),
                                      reads=[psB], writes=[B_PQ])
                        for lt in range(nlt):
                            tab_ap, tab_b = tabs[ti % 2]
                            ti += 1
                            tab4 = tab_ap.rearrange("p (a c n) -> p a c n", a=2, c=nb)
                            pg.dma("sp", tab_ap, ftab_d[L][lt, :, :], dst=tab_b)
                            t0 = lt * LT
                            gs = slice(s * L + t0, s * L + t0 + LT)
                            c1 = 1 if t0 == 0 else 0
                            n2 = LT - c1
                            jlo = s * L + L - t0 - LT + 1
                            for cc in range(2):
                                psAB, psA = pg.ps()
                                pg.mm(psAB, [(psA[:, 0:LT], PQ3[:, lc, cc * 128:(cc + 1) * 128], tab4[:, 0, lc, :], lc == 0, lc == nb - 1)
                                             for lc in range(nb)], reads=[B_PQ, tab_b])
                                psBB, psBt = pg.ps()
                                pg.mm(psBB, [(psBt[:, 0:LT], PQ3[:, lc, 256 + cc * 128:256 + (cc + 1) * 128], tab4[:, 1, lc, :], lc == 0, lc == nb - 1)
                                             for lc in range(nb)], reads=[B_PQ, tab_b])
                                pg.op("act", lambda e, psA=psA: e.activation(out=etA, in_=psA[:, 0:LT], func=AF.Copy),
                                      reads=[psAB], writes=[B_etA])
                                pg.op("dve", lambda e, psBt=psBt: e.tensor_tensor(out=ft1, in0=etA, in1=psBt[:, 0:LT], op=ALU.add),
                                      reads=[B_etA, psBB], writes=[B_ft1])
                                pg.op("dve", lambda e, cc=cc, gs=gs: e.tensor_tensor(out=uy3[:, cc, gs], in0=ft1, in1=zs3[:, cc, gs], op=ALU.mult),
                                      reads=[B_ft1, B_zs], writes=[B_uy])
                                pg.op("dve", lambda e, psBt=psBt, c1=c1, n2=n2: e.tensor_tensor(
                                    out=ft2[:, 0:n2], in0=etA[:, c1:LT][:, ::-1], in1=psBt[:, c1:LT][:, ::-1], op=ALU.subtract),
                                    reads=[B_etA, psBB], writes=[B_ft2])
                                pg.op("dve", lambda e, cc=cc, jlo=jlo, n2=n2: e.tensor_tensor(
                                    out=uy3[:, cc, jlo:jlo + n2], in0=ft2[:, 0:n2], in1=zs3[:, cc, jlo:jlo + n2], op=ALU.mult),
                                    reads=[B_ft2, B_zs], writes=[B_uy])
                        hh = s * L + L // 2
                        for cc in range(2):
                            psB, psT = pg.ps()
                            pg.mm(psB, [(psT[:, 0:1], PQ3[:, lc, cc * 128:(cc + 1) * 128], fcol[:, lc:lc + 1], lc == 0, lc == nb - 1)
                                        for lc in range(nb)], reads=[B_PQ, B_fcol])
                            pg.op("dve", lambda e, psT=psT, cc=cc, hh=hh: e.tensor_tensor(
                                out=uy3[:, cc, hh:hh + 1], in0=psT[:, 0:1], in1=zs3[:, cc, hh:hh + 1], op=ALU.mult),
                                reads=[psB, B_zs], writes=[B_uy])
                    out_proj_acc(li, wo3, B_wo, 2, uy3, B_uy)
                ar["off"] = mark0
                pg.barrier()

            def hyena_layer(li):
                mark0 = ar["off"]
                PI = math.pi
                nkc = (L + 1 + 127) // 128
                NT = min(L, 512)
                a3, B_a3 = ab(L, "a3")
                hbt, B_hb = af(8, "hbt")
                hws = [af(64, "hw%d" % i) for i in range(3)]
                tcol, B_small = af(nb, "tcol")
                bm, _ = af(1, "bm")
                wk, _ = af(nkc, "wk")
                hcw, _ = af(72, "hcw")
                hcw3 = hcw.rearrange("p (k c) -> p k c", k=3)
                hcb, _ = af(24, "hcb")
                hd, _ = af(16, "hd")
                hd3 = hd.rearrange("p (n c) -> p n c", n=2)
                pg.dma("sp", hbt[0:64, 0:4], hb_d[:, :], dst=B_hb)
                pg.dma("sp", hws[0][0][0:33, :], hw1_d[:, :], dst=hws[0][1])
                pg.dma("sp", hws[1][0][0:64, :], hw2_d[:, :], dst=hws[1][1])
                pg.dma("sp", hws[2][0][0:64, :], hw3_d[:, :], dst=hws[2][1])
                pg.dma("sp", tcol, tcol_d[L][:, :], dst=B_small)
                pg.dma("sp", bm, bm_d[:, :], dst=B_small)
                pg.dma("sp", wk, wk_d[L][:, :], dst=B_small)
                pg.dma("sp", hcw3, hcw_d[:, :, :], dst=B_small)
                pg.dma("sp", hcb, hcb_d[:, :], dst=B_small)
                pg.dma("sp", hd3, hd_d[:, :, :], dst=B_small)
                pg.op("dve", lambda e: e.tensor_scalar(out=hbt[0:64, 4:7], in0=hbt[0:64, 0:3], scalar1=hbt[0:64, 3:4], scalar2=None,
                                                       op0=ALU.mult), reads=[B_hb], writes=[B_hb])
                markf = ar["off"]
                zp, B_zp = af(L, "zp")
                aA, B_aA = af(L, "aA")
                aB, B_aB = af(L, "aB")
                arg, B_arg = af(512, "arg")
                mw, B_mw = af(512, "mwrap")
                pg.dma("sp", zp[0:33, :], zpos_d[L][:, :], dst=B_zp)
                srcs = [(zp, B_zp, 33), (aA, B_aA, 64), (aB, B_aB, 64)]
                dsts = [(aA, B_aA), (aB, B_aB), (a3, B_a3)]
                for i in range(3):
                    s_ap, s_b, K = srcs[i]
                    d_ap, d_b = dsts[i]
                    w_ap, w_b = hws[i]
                    for t0 in range(0, L, 512):
                        n = min(512, L - t0)
                        psB, psT = pg.ps()
                        pg.mm(psB, [(psT[0:64, 0:n], w_ap[0:K, :], s_ap[0:K, t0:t0 + n], True, True)], reads=[w_b, s_b])
                        pg.op("act", lambda e, psT=psT, n=n, i=i: e.activation(out=arg[0:64, 0:n], in_=psT[0:64, 0:n], func=AF.Identity,
                                                                            scale=hbt[0:64, 3:4], bias=hbt[0:64, 4 + i:5 + i]),
                              reads=[psB, B_hb], writes=[B_arg])
                        pg.op("dve", lambda e, n=n: e.tensor_scalar(out=mw[0:64, 0:n], in0=arg[0:64, 0:n], scalar1=PI, scalar2=2 * PI,
                                                                    op0=ALU.is_gt, op1=ALU.mult), reads=[B_arg], writes=[B_mw])
                        pg.op("dve", lambda e, n=n: e.tensor_tensor(out=arg[0:64, 0:n], in0=arg[0:64, 0:n], in1=mw[0:64, 0:n],
                                                                    op=ALU.subtract), reads=[B_mw], writes=[B_arg])
                        pg.op("dve", lambda e, n=n: e.tensor_scalar(out=mw[0:64, 0:n], in0=arg[0:64, 0:n], scalar1=-PI, scalar2=2 * PI,
                                                                    op0=ALU.is_lt, op1=ALU.mult), reads=[B_arg], writes=[B_mw])
                        pg.op("dve", lambda e, n=n: e.tensor_tensor(out=arg[0:64, 0:n], in0=arg[0:64, 0:n], in1=mw[0:64, 0:n],
                                                                    op=ALU.add), reads=[B_mw], writes=[B_arg])
                        pg.op("act", lambda e, n=n, d_ap=d_ap, t0=t0: e.activation(out=d_ap[0:64, t0:t0 + n], in_=arg[0:64, 0:n], func=AF.Sin),
                              reads=[B_arg], writes=[d_b])
                ar["off"] = markf
                pg.barrier()
                resident = (L <= 256)
                NT2 = min(L // 2, 512)
                ntile2 = (L // 2) // NT2
                nodd = (L // 2) // 128
                if resident:
                    NFR, NIR = 2 * nkc, 2 * nkc * ntile2
                else:
                    NFR, NIR = 5, 8
                ftr = [ab(nb * 128, "ftr%d" % i) for i in range(NFR)]
                itr = [ab(NT2, "itr%d" % i) for i in range(NIR)]
                hcol, B_hcol = ab(nkc * 2, "hcol")
                pg.dma("sp", hcol, htic_d[L][:, :], dst=B_hcol)
                fi = [0]
                ii = [0]
                fcache = {}
                icache = {}

                def get_ftab(a, kc):
                    if resident and (a, kc) in fcache:
                        return fcache[(a, kc)]
                    f_ap, f_b = ftr[fi[0] % NFR]
                    fi[0] += 1
                    if not (os.environ.get("KNODMA") and fi[0] > 3):
                        pg.dma("sp", f_ap, htf_d[L][a, kc, :, :], dst=f_b)
                    r = (f_ap.rearrange("p (c n) -> p c n", c=nb), f_b)
                    fcache[(a, kc)] = r
                    return r

                def get_itab(kc, ti, a):
                    if resident and (kc, ti, a) in icache:
                        return icache[(kc, ti, a)]
                    i_ap, i_b = itr[ii[0] % NIR]
                    ii[0] += 1
                    if not (os.environ.get("KNODMA") and ii[0] > 2):
                        pg.dma("sp", i_ap, hti_d[L][kc, ti, :, a * NT2:(a + 1) * NT2], dst=i_b)
                    r = (i_ap, i_b)
                    icache[(kc, ti, a)] = r
                    return r

                mark1 = ar["off"]
                for cb in range(8):
                    ar["off"] = mark1
                    Bw, Wt = load_w_cols(hwin_d, [(cb * 128, 128), (1024 + cb * 128, 128), (2048 + cb * 128, 128), (3072 + cb * 128, 128)])
                    wo, B_wo = ab(D, "wo")
                    wo3 = wo.rearrange("p (k n) -> p k n", k=1)
                    pg.dma("pool", wo3[:, 0, :], hwout_d[cb * 128:(cb + 1) * 128, :], dst=B_wo)
                    w4c, B_w4c = ab(512, "w4c")
                    w4c3 = w4c.rearrange("p (a n) -> p a n", a=4)
                    for nd in range(4):
                        pg.dma("pool", w4c3[0:64, nd, :], hw4_d[:, nd * 1024 + cb * 128:nd * 1024 + (cb + 1) * 128], dst=B_w4c)
                    dlc, B_dlc = af(128, "dlc")
                    pg.dma("sp", dlc, delta_d[:, cb * 128:(cb + 1) * 128], dst=B_dlc)
                    G2, B_G2 = ab(T, "G2")
                    G23 = G2.rearrange("p (k t) -> p k t", k=1)
                    B_G2s = [pb("G2s%d" % s_) for s_ in range(nseq)]

                    def hy_seq(s, cb=cb, Bw=Bw, Wt=Wt, w4c3=w4c3, B_w4c=B_w4c, dlc=dlc, B_dlc=B_dlc, G2=G2, B_G2s=B_G2s):
                        sfx = "_s%d" % s
                        YAB, B_AB = ab(nb * 384, "YAB" + sfx)
                        YAB3 = YAB.rearrange("p (b n) -> p b n", n=384)
                        B_ytok = pb("ytok" + sfx)
                        tFs = [af(128, "tF%d%s" % (i, sfx)) for i in range(2)]
                        tBs = [af(128, "tB%d%s" % (i, sfx)) for i in range(2)]
                        dws = [af(128, "dw%d%s" % (i, sfx)) for i in range(2)]
                        raw = YAB.bitcast(F32)[:, 0:L + 2]
                        B_raw = pb("raw" + sfx)
                        yT, B_y = af(L, "yT" + sfx)
                        R1, B_R1 = af(max(L, nkc * 128), "R1" + sfx)
                        ctmp = R1[:, 0:L]
                        Zb = R1[:, 0:nkc * 128].bitcast(BF16).rearrange("p (k a n) -> p k a n", k=nkc, a=2)
                        G1, B_G1 = ab(L, "G1" + sfx)
                        etmp, B_etmp = af(NT2, "etmp" + sfx)
                        Xs, B_Xs = af(256, "Xs" + sfx)
                        Hs, B_Hs = af(256, "Hs" + sfx)
                        ttb, _ = af(max(512, NT2), "ttb" + sfx)
                        tt4 = [(ttb[:, i * 128:(i + 1) * 128], pb("tt%d%s" % (i, sfx))) for i in range(4)]
                        etmp2, B_etmp2 = ttb[:, 0:NT2], pb("etmp2" + sfx)
                        B_G2 = B_G2s[s]
                        x2c = G2[:, s * L:(s + 1) * L]
                        yield

                        def proj_seq(c0, evac):
                            for t0 in range(0, L, NT):
                                gs = slice(s * L + t0, s * L + t0 + NT)
                                psB, psT = pg.ps()
                                pg.mm(psB, [(psT[:, 0:NT], Wt[:, c, c0:c0 + 128], hT[:, c, gs], c == 0, c == DC - 1) for c in range(DC)],
                                      reads=[Bw] + B_h)
                                yield
                                evac(t0, psB, psT)
                                yield

                        def shortconv(ci, dst_ap, dst_b):
                            ch = ci * 8 + cb
                            pg.op("dve", lambda e: e.tensor_scalar(out=ctmp, in0=raw[:, 1:L + 1], scalar1=hcw3[:, 1, ch:ch + 1],
                                                                   scalar2=hcb[:, ch:ch + 1], op0=ALU.mult, op1=ALU.add),
                                  reads=[B_raw, B_small], writes=[B_R1])
                            yield
                            pg.op("dve", lambda e: e.scalar_tensor_tensor(out=ctmp, in0=raw[:, 0:L], scalar=hcw3[:, 0, ch:ch + 1], in1=ctmp,
                                                                          op0=ALU.mult, op1=ALU.add), reads=[B_raw, B_small], writes=[B_R1])
                            yield
                            pg.op("dve", lambda e: e.scalar_tensor_tensor(out=dst_ap, in0=raw[:, 2:L + 2], scalar=hcw3[:, 2, ch:ch + 1], in1=ctmp,
                                                                          op0=ALU.mult, op1=ALU.add), reads=[B_raw, B_small, B_R1], writes=[dst_b])
                            yield

                        dsts = [(G1, B_G1), (x2c, B_G2), (yT, B_y)]
                        pg.op("pool", lambda e: e.memset(raw[:, 0:1], 0.0), writes=[B_raw, B_AB, B_ytok])
                        pg.op("pool", lambda e: e.memset(raw[:, L + 1:L + 2], 0.0), writes=[B_raw, B_AB, B_ytok])
                        yield
                        for ci in range(3):
                            yield from proj_seq(ci * 128, lambda t0, psB, psT: pg.op(
                                "act", lambda e: e.activation(out=raw[:, 1 + t0:1 + t0 + NT], in_=psT[:, 0:NT], func=AF.Copy),
                                reads=[psB], writes=[B_raw]))
                            yield from shortconv(ci, dsts[ci][0], dsts[ci][1])
                        pg.op("pool", lambda e: e.memset(raw[:, 0:1], 0.0), reads=[B_raw], writes=[B_AB, B_ytok])
                        yield
                        for n in range(2):
                            gate_ap, gate_b = (G1, B_G1) if n == 0 else (x2c, B_G2)
                            for blk in range(nb):
                                tF, B_tF = tFs[blk % 2]
                                tB, B_tB = tBs[blk % 2]
                                dw, B_dw = dws[blk % 2]
                                psB, psT = pg.ps()
                                pg.mm(psB, [(psT[:, 0:256], a3[0:64, blk * 128:(blk + 1) * 128],
                                             w4c3[0:64, 2 * n:2 * n + 2, :], True, True)], reads=[B_a3, B_w4c])
                                pg.op("act", lambda e, blk=blk, dw=dw: e.activation(out=dw, in_=dlc, func=AF.Exp, scale=tcol[:, blk:blk + 1]),
                                      reads=[B_dlc, B_small], writes=[B_dw])
                                yield
                                pg.op("dve", lambda e, psT=psT, tF=tF, dw=dw: e.tensor_tensor(out=tF, in0=psT[:, 0:128], in1=dw, op=ALU.mult),
                                      reads=[psB, B_dw], writes=[B_tF])
                                if blk == 0:
                                    pg.op("dve", lambda e, psT=psT, tB=tB, dw=dw: e.scalar_tensor_tensor(out=tB, in0=psT[:, 128:256], scalar=bm[:, 0:1], in1=dw,
                                                                                                     op0=ALU.mult, op1=ALU.mult),
                                          reads=[psB, B_dw, B_small], writes=[B_tB])
                                else:
                                    pg.op("dve", lambda e, psT=psT, tB=tB, dw=dw: e.tensor_tensor(out=tB, in0=psT[:, 128:256], in1=dw, op=ALU.mult),
                                          reads=[psB, B_dw], writes=[B_tB])
                                yield
                                pg.op("pool", lambda e, blk=blk, tF=tF, tB=tB: e.tensor_tensor(out=YAB3[:, blk, 0:128], in0=tF, in1=tB, op=ALU.add),
                                      reads=[B_tF, B_tB], writes=[B_AB])
                                pg.op("pool", lambda e, blk=blk, tF=tF, tB=tB: e.tensor_tensor(out=YAB3[:, blk, 256:384], in0=tF, in1=tB, op=ALU.subtract),
                                      reads=[B_tF, B_tB], writes=[B_AB])
                                yield
                            for b0 in range(0, nb, 4):
                                nbb = min(4, nb - b0)
                                psB, psT = pg.ps()

                                def fn(e, psT=psT, b0=b0, nbb=nbb):
                                    r = None
                                    for bb in range(nbb):
                                        r = e.transpose(out=psT[:, bb * 128:(bb + 1) * 128], in_=yT[:, (b0 + bb) * 128:(b0 + bb + 1) * 128],
                                                        identity=ident[:])
                                    return r
                                pg.op("pe", fn, reads=[B_y, B_const], writes=[psB])
                                yield
                                pg.op("act", lambda e, psT=psT, b0=b0, nbb=nbb: e.activation(
                                    out=YAB3[:, b0:b0 + nbb, 128:256], in_=psT[:, 0:nbb * 128].rearrange("p (b n) -> p b n", n=128), func=AF.Copy),
                                    reads=[psB], writes=[B_ytok])
                                yield
                            for kc in range(nkc):
                                kn = min(128, L + 1 - kc * 128)
                                tabs2 = [get_ftab(a, kc) for a in range(2)]
                                psB, psT = pg.ps()
                                mms = []
                                for sc in range(nb):
                                    mms.append((psT[0:kn, 0:256], tabs2[0][0][:, sc, 0:kn], YAB3[:, sc, 0:256], sc == 0, sc == nb - 1))
                                for sc in range(nb):
                                    mms.append((psT[0:kn, 256:512], tabs2[1][0][:, sc, 0:kn], YAB3[:, sc, 128:384], sc == 0, sc == nb - 1))
                                pg.mm(psB, mms, reads=[tabs2[0][1], tabs2[1][1], B_ytok, B_AB])
                                yield
                                pg.op("act", lambda e, psT=psT, kn=kn: e.activation(out=Xs[0:kn, :], in_=psT[0:kn, 128:384], func=AF.Copy),
                                      reads=[psB], writes=[B_Xs])
                                pg.op("act", lambda e, psT=psT, kn=kn, kc=kc: e.activation(out=Hs[0:kn, 0:128], in_=psT[0:kn, 0:128], func=AF.Copy,
                                                                                        scale=wk[0:kn, kc:kc + 1]),
                                      reads=[psB, B_small], writes=[B_Hs])
                                pg.op("act", lambda e, psT=psT, kn=kn, kc=kc: e.activation(out=Hs[0:kn, 128:256], in_=psT[0:kn, 384:512], func=AF.Copy,
                                                                                        scale=wk[0:kn, kc:kc + 1]),
                                      reads=[psB, B_small], writes=[B_Hs])
                                yield
                                if kc == 0:
                                    dbg("hy_Xs", Xs, B_Xs)
                                    dbg("hy_Hs", Hs, B_Hs)
                                (t1, b1), (t2, b2), (t3, b3), (t4, b4) = tt4
                                pg.op("dve", lambda e, kn=kn, t1=t1: e.tensor_tensor(out=t1[0:kn, :], in0=Xs[0:kn, 0:128], in1=Hs[0:kn, 0:128], op=ALU.mult),
                                      reads=[B_Xs, B_Hs], writes=[b1])
                                pg.op("dve", lambda e, kn=kn, t2=t2: e.tensor_tensor(out=t2[0:kn, :], in0=Xs[0:kn, 128:256], in1=Hs[0:kn, 128:256], op=ALU.mult),
                                      reads=[B_Xs, B_Hs], writes=[b2])
                                pg.op("pool", lambda e, kn=kn, t3=t3: e.tensor_tensor(out=t3[0:kn, :], in0=Xs[0:kn, 0:128], in1=Hs[0:kn, 128:256], op=ALU.mult),
                                      reads=[B_Xs, B_Hs], writes=[b3])
                                pg.op("pool", lambda e, kn=kn, t4=t4: e.tensor_tensor(out=t4[0:kn, :], in0=Xs[0:kn, 128:256], in1=Hs[0:kn, 0:128], op=ALU.mult),
                                      reads=[B_Xs, B_Hs], writes=[b4])
                                yield
                                pg.op("dve", lambda e, kn=kn, kc=kc, t1=t1, t2=t2: e.tensor_tensor(out=Zb[0:kn, kc, 0, :], in0=t1[0:kn, :], in1=t2[0:kn, :], op=ALU.subtract),
                                      reads=[b1, b2], writes=[B_R1])
                                pg.op("pool", lambda e, kn=kn, kc=kc, t3=t3, t4=t4: e.tensor_tensor(out=Zb[0:kn, kc, 1, :], in0=t3[0:kn, :], in1=t4[0:kn, :], op=ALU.add),
                                      reads=[b3, b4], writes=[B_R1])
                                yield
                            dbg("hy_Z", R1[:, 0:nkc * 128].bitcast(BF16), B_R1, BF16)
                            dcol = hd3[:, n, cb:cb + 1]
                            for ti in range(ntile2):
                                t0 = ti * NT2
                                psEB, psE = pg.ps()
                                psOB, psO = pg.ps()
                                for (pB, pT, isE) in ((psEB, psE, True), (psOB, psO, False)):
                                    for kc in range(nkc):
                                        kn = min(128, L + 1 - kc * 128)
                                        odd = kc < nodd
                                        a = (1 if odd else 0) if isE else (0 if odd else 1)
                                        i_ap, i_b = get_itab(kc, ti, a)
                                        pg.mm(pB, [(pT[:, 0:NT2], Zb[0:kn, kc, a, :], i_ap[0:kn, :], kc == 0, kc == nkc - 1)],
                                              reads=[B_R1, i_b])
                                pg.op("dve", lambda e, psE=psE, t0=t0, dcol=dcol: e.scalar_tensor_tensor(
                                    out=etmp[:, 0:NT2], in0=yT[:, t0:t0 + NT2], scalar=dcol, in1=psE[:, 0:NT2],
                                    op0=ALU.mult, op1=ALU.add), reads=[psEB, B_y, B_small], writes=[B_etmp])
                                pg.op("dve", lambda e, psO=psO: e.tensor_tensor(
                                    out=etmp[:, 0:NT2], in0=etmp[:, 0:NT2], in1=psO[:, 0:NT2], op=ALU.add),
                                    reads=[psOB], writes=[B_etmp])
                                dbg("hy_etmp", etmp[:, 0:NT2], B_etmp)
                                if resident:
                                    for q_ in range(6):
                                        dbg("hy_itr%d" % q_, itr[q_][0], itr[q_][1], BF16)
                                pg.op("dve", lambda e, t0=t0, gate_ap=gate_ap: e.tensor_tensor(
                                    out=yT[:, t0:t0 + NT2], in0=etmp[:, 0:NT2], in1=gate_ap[:, t0:t0 + NT2], op=ALU.mult),
                                    reads=[B_etmp, gate_b], writes=[B_y])
                                c1 = 1 if t0 == 0 else 0
                                n2 = NT2 - c1
                                jlo = L - t0 - NT2 + 1
                                pg.op("dve", lambda e, psE=psE, jlo=jlo, n2=n2, c1=c1, dcol=dcol: e.scalar_tensor_tensor(
                                    out=etmp2[:, 0:n2], in0=yT[:, jlo:jlo + n2], scalar=dcol, in1=psE[:, c1:NT2][:, ::-1],
                                    op0=ALU.mult, op1=ALU.add), reads=[psEB, B_y, B_small], writes=[B_etmp2])
                                pg.op("dve", lambda e, psO=psO, n2=n2, c1=c1: e.tensor_tensor(
                                    out=etmp2[:, 0:n2], in0=etmp2[:, 0:n2], in1=psO[:, c1:NT2][:, ::-1], op=ALU.subtract),
                                    reads=[psOB], writes=[B_etmp2])
                                pg.op("dve", lambda e, jlo=jlo, n2=n2, gate_ap=gate_ap: e.tensor_tensor(
                                    out=yT[:, jlo:jlo + n2], in0=etmp2[:, 0:n2], in1=gate_ap[:, jlo:jlo + n2], op=ALU.mult),
                                    reads=[B_etmp2, gate_b], writes=[B_y])
                                yield
                            psB, psT = pg.ps()
                            mms = []
                            for kc in range(nkc):
                                kn = min(128, L + 1 - kc * 128)
                                for a in range(2):
                                    mms.append((psT[:, 0:1], Zb[0:kn, kc, a, :], hcol[0:kn, kc * 2 + a:kc * 2 + a + 1],
                                                kc == 0 and a == 0, kc == nkc - 1 and a == 1))
                            pg.mm(psB, mms, reads=[B_R1, B_hcol])
                            yield
                            hh = L // 2
                            pg.op("dve", lambda e, psT=psT, dcol=dcol: e.scalar_tensor_tensor(
                                out=etmp[:, 0:1], in0=yT[:, hh:hh + 1], scalar=dcol, in1=psT[:, 0:1],
                                op0=ALU.mult, op1=ALU.add), reads=[psB, B_y, B_small], writes=[B_etmp])
                            pg.op("dve", lambda e, gate_ap=gate_ap: e.tensor_tensor(
                                out=yT[:, hh:hh + 1], in0=etmp[:, 0:1], in1=gate_ap[:, hh:hh + 1], op=ALU.mult),
                                reads=[B_etmp, gate_b], writes=[B_y])
                            yield
                            dbg("hy_y1", yT, B_y)
                            if n == 0:
                                yield from proj_seq(3 * 128, lambda t0, psB, psT: pg.op(
                                    "act", lambda e: e.activation(out=G1[:, t0:t0 + NT], in_=psT[:, 0:NT], func=AF.Silu),
                                    reads=[psB], writes=[B_G1]))
                        pg.op("dve", lambda e: e.tensor_tensor(out=x2c, in0=yT, in1=G1, op=ALU.mult),
                              reads=[B_y, B_G1], writes=[B_G2])
                        yield

                    run_staged([hy_seq(s_) for s_ in range(nseq)])
                    B_G2 = B_G2s
                    out_proj_acc(li, wo3, B_wo, 1, G23, B_G2s)
                ar["off"] = mark0
                pg.barrier()

            for li in range(nl):
                pg.barrier()
                compute_h(li)
                if gi == 0 and li + 1 < nl:
                    emit_mod(li + 1)
                kind, j = li % 3, li // 3
                ar["hw"] = 0
                if kind == 0:
                    gla_layer(li, j)
                elif kind == 1:
                    fnet_layer(li)
                else:
                    hyena_layer(li)
                print("arena high-water group", gi, "layer", li, ar["hw"], "/", NAR)

            pg.barrier()
            mark = ar["off"]
            sq, B_sq = ab(DC * 512, "fsq")
            rr, B_r = af(512, "frstd")
            tmps = [af(512, "ftmp%d" % i) for i in range(2)]
            osts = [af(D, "ost%d" % i) for i in range(2)]
            for tt in range(ntt):
                ts = slice(tt * 512, (tt + 1) * 512)
                rms_rstd(tt, sq, B_sq, rr, B_r)
                for c in range(DC):
                    pg.op("dve", lambda e, c=c, ts=ts: e.scalar_tensor_tensor(
                        out=xT[:, c, ts], in0=xT[:, c, ts], scalar=fng[:, c:c + 1], in1=rr, op0=ALU.mult, op1=ALU.mult),
                        reads=[B_r, B_const], writes=[B_x[tt]])
                for b4 in range(4):
                    bt = tt * 4 + b4
                    o_ap, o_b = osts[bt % 2]
                    for hf in range(2):
                        psB, psT = pg.ps()

                        def fn(e, psT=psT, hf=hf, bt=bt):
                            r = None
                            for c4 in range(4):
                                c = hf * 4 + c4
                                r = e.transpose(out=psT[:, c4 * 128:(c4 + 1) * 128], in_=xT[:, c, bt * 128:(bt + 1) * 128],
                                                identity=ident[:])
                            return r
                        pg.op("pe", fn, reads=[B_x[tt], B_const], writes=[psB])
                        if hf == 0:
                            pg.op("act", lambda e, o_ap=o_ap, psT=psT: e.activation(out=o_ap[:, 0:512], in_=psT[:, :], func=AF.Copy),
                                  reads=[psB], writes=[o_b])
                        else:
                            pg.op("dve", lambda e, o_ap=o_ap, psT=psT: e.tensor_copy(out=o_ap[:, 512:1024], in_=psT[:, :]),
                                  reads=[psB], writes=[o_b])
                    pg.dma("sp", y_d[bt * 128:(bt + 1) * 128, :], o_ap, src=o_b, final=True)
            ar["off"] = mark

        run_group(0, xp_d, yp_d, 4 * LP, 4, LP, 0)
        run_group(1, xs_d, ys_d, LS, 1, LS, 1)

        pg._wait("sp", pg.final)
        pg.barrier()

        with nc.Block() as block:
            @block.tensor
            def _(e):
                for f in pg.streams["pe"]:
                    f(e)

            @block.scalar
            def _(e):
                for f in pg.streams["act"]:
                    f(e)

            @block.vector
            def _(e):
                for f in pg.streams["dve"]:
                    f(e)

            @block.gpsimd
            def _(e):
                for f in pg.streams["pool"]:
                    f(e)

            @block.sync
            def _(e):
                for f in pg.streams["sp"]:
                    f(e)
        print("instructions:", pg.ninstr, "dma sems:", pg.nsem)
    return nc


def _consts():
    bf = ml_dtypes.bfloat16
    c = {}
    c["ident"] = np.eye(128, dtype=np.float32)
    s = np.arange(128)[:, None]
    t = np.arange(128)[None, :]
    cm = np.zeros((128, 4, 129), np.float32)
    cm[:, 0, :128] = (s <= t)
    cm[:, 0, 128] = 1.0
    cm[:, 1, :128] = (s > t)
    cm[:, 2, :128] = (s >= t)
    cm[:, 2, 128] = 1.0
    cm[:, 3, :128] = (s < t)
    c["cm"] = cm.astype(bf)
    mk = np.zeros((128, 2, 128), np.float32)
    mk[:, 0] = (s <= t)
    mk[:, 1] = (s >= t)
    c["mk"] = mk
    return c


_TAB = {}


def _tables():
    if _TAB:
        return _TAB
    bf = ml_dtypes.bfloat16
    f32 = np.float32
    t = {}
    c = np.arange(256)
    ang = 2 * np.pi * ((c[:, None] * c[None, :]) % 256) / 256.0
    t["fcs"] = np.concatenate([np.cos(ang), np.sin(ang)], axis=1).astype(bf)
    for nm, L in (("p", LP), ("s", LS)):
        l = np.arange(L)
        ang = 2 * np.pi * ((l[:, None] * l[None, :]) % L) / float(L)
        sc = 1.0 / math.sqrt(L * 256.0)
        ft = np.stack([np.cos(ang) * sc, -np.sin(ang) * sc])
        nbl = L // 128
        LT2 = min(256, L // 2)
        nlt2 = (L // 2) // LT2
        t["fcol" + nm] = np.ascontiguousarray(ft[0, :, L // 2].reshape(nbl, 128).T).astype(bf)
        ft = ft[:, :, :L // 2].reshape(2, nbl, 128, nlt2, LT2)
        t["ftab" + nm] = np.ascontiguousarray(ft.transpose(3, 2, 0, 1, 4).reshape(nlt2, 128, 2 * nbl * LT2)).astype(bf)
        N = 2 * L
        nkc = (L + 1 + 127) // 128
        F = np.concatenate([np.arange(1, L, 2), np.arange(0, L + 1, 2), -np.ones(nkc * 128 - (L + 1), np.int64)]).astype(np.int64)
        valid = (F >= 0)
        Fk = np.where(valid, F, 0)

        def tfun(r, c):
            ang_ = 2 * np.pi * ((r * c) % N) / float(N)
            return np.stack([np.cos(ang_), -np.sin(ang_)])
        sidx = np.arange(L)
        Tf = tfun(sidx[:, None], Fk[None, :]) * valid[None, None, :]
        hf_ = Tf.reshape(2, nbl, 128, nkc, 128)
        t["htf" + nm] = np.ascontiguousarray(hf_.transpose(0, 3, 2, 1, 4).reshape(2, nkc, 128, L)).astype(bf)
        NT2 = min(L // 2, 512)
        nt2 = (L // 2) // NT2
        tidx = np.arange(L // 2)
        Ti = tfun(Fk[:, None], tidx[None, :]) * valid[None, :, None]
        hi_ = Ti.reshape(2, nkc, 128, nt2, NT2)
        t["hti" + nm] = np.ascontiguousarray(hi_.transpose(1, 3, 2, 0, 4).reshape(nkc, nt2, 128, 2 * NT2)).astype(bf)
        Tc = tfun(Fk, np.full_like(Fk, L // 2)) * valid[None, :]
        t["htic" + nm] = np.ascontiguousarray(Tc.reshape(2, nkc, 128).transpose(2, 1, 0).reshape(128, nkc * 2)).astype(bf)
        wkv = np.where((Fk == 0) | (Fk == L), 1.0, 2.0) / float(N) * valid
        t["wk" + nm] = np.ascontiguousarray(wkv.reshape(nkc, 128).T).astype(f32)
        tl = np.linspace(0.0, 1.0, L, dtype=np.float32).astype(np.float64)
        w = 2.0 * np.pi * np.arange(L, dtype=np.float64) / L
        f = np.linspace(1e-4, 15.0, 16, dtype=np.float32).astype(np.float64)
        zpos = np.concatenate([tl[:, None], np.cos(f[None, :] * w[:, None]), -np.sin(f[None, :] * w[:, None])], axis=1)
        t["zpos" + nm] = np.ascontiguousarray(zpos.T).astype(f32)
        t["tcol" + nm] = np.ascontiguousarray((-tl).reshape(L // 128, 128).T).astype(f32)
    bm = np.ones((128, 1), f32)
    bm[0, 0] = 0.0
    t["bm"] = bm
    mind = math.log(1e-2) / 1.5
    maxd = math.log(1e-2) / 0.3
    delta = np.abs(np.linspace(mind, maxd, D, dtype=np.float32))
    t["delta"] = np.ascontiguousarray(np.broadcast_to(delta[None, :], (128, D))).astype(f32)
    _TAB.update(t)
    return _TAB


def kernel(**inputs):
    nl = int(os.environ.get("KNL", "4"))
    inp = {k: np.asarray(v) for k, v in inputs.items()}
    f32 = np.float32
    consts = _consts()
    dummy = {k: v for k, v in _tables().items() if not k.startswith("_")}

    def pc(v, inner=()):
        return v

    def cols(v):
        return np.ascontiguousarray(v.reshape(-1, 128).T)

    shared = dict(consts)
    shared.update(dummy)
    shared["modw"] = inp["mod_w"]
    shared["modb"] = np.ascontiguousarray(np.stack([cols(inp["mod_b"][l]) for l in range(4)], axis=1))
    shared["ng"] = np.ascontiguousarray(np.stack([cols(inp["norm_g"][l]) for l in range(4)], axis=1))
    shared["fng"] = cols(inp["final_norm_g"])
    shared["gwin"] = inp["gla_w_in"]
    gwdec = np.zeros((2, 33, 1024), f32)
    for j in range(2):
        gwdec[j, 0:16, 0:512] = inp["gla_w_dec"][j, 0]
        gwdec[j, 16:32, 512:1024] = inp["gla_w_dec"][j, 1]
        gwdec[j, 32, 0:512] = inp["gla_b_dec"][j, 0]
        gwdec[j, 32, 512:1024] = inp["gla_b_dec"][j, 1]
    shared["gwdec"] = gwdec
    shared["gon"] = np.ascontiguousarray(np.stack([cols(inp["gla_onorm_g"][j]) for j in range(2)], axis=1))
    shared["gwout"] = inp["gla_w_out"]
    shared["fwin"] = inp["fn_w_in"][0]
    shared["fwout"] = inp["fn_w_out"][0]
    shared["hwin"] = inp["hy_w_in"][0]
    shared["hcw"] = np.ascontiguousarray(np.stack([cols(inp["hy_conv_w"][0, k]) for k in range(3)], axis=1))
    shared["hcb"] = cols(inp["hy_conv_b"][0])
    shared["hw1"] = inp["hy_ffn_w1"][0]
    shared["hw2"] = inp["hy_ffn_w2"][0]
    shared["hw3"] = inp["hy_ffn_w3"][0]
    shared["hw4"] = inp["hy_ffn_w4"][0]
    shared["hb"] = np.ascontiguousarray(np.stack([inp["hy_ffn_b1"][0], inp["hy_ffn_b2"][0], inp["hy_ffn_b3"][0],
                                                  inp["hy_freq"][0]], axis=1))
    shared["hd"] = np.ascontiguousarray(np.stack([cols(inp["hy_d"][0, n]) for n in range(2)], axis=1))
    shared["hwout"] = inp["hy_w_out"][0]
    shared = {k: np.ascontiguousarray(v) for k, v in shared.items()}

    in_maps = []
    for i in range(NCORES):
        s = i // 4
        mp = dict(shared)
        mp["xp"] = np.ascontiguousarray(inp["x_prompt"][4 * i:4 * i + 4].reshape(4 * LP, D))
        mp["xs"] = np.ascontiguousarray(inp["x_sample"][s])
        mp["st"] = np.ascontiguousarray(inp["state_gla"][s])
        cvv = np.stack([cols(inp["c_ctx"]), cols(inp["c"][s])], axis=2)
        mp["cv"] = np.ascontiguousarray(cvv)
        in_maps.append(mp)

    nc = build_program(nl)
    res = run_bass_kernel_spmd(nc, in_maps, core_ids=list(range(NCORES)), trace=bool(os.environ.get("KTRACE")))
    if os.environ.get("KTRACE"):
        print("EXEC_TIME_NS", res.exec_time_ns)
        _TAB["_res"] = res
    r = res.results
    if os.environ.get("KDBG"):
        _TAB["_dbg"] = {k: np.asarray(v).astype(np.float32) for k, v in r[0].items() if k.startswith("dbg_")}
    y_prompt = np.concatenate([r[i]["yp"].reshape(4, LP, D) for i in range(NCORES)], axis=0)
    y_sample = np.stack([r[0]["ys"], r[4]["ys"]], axis=0)
    new_state = np.concatenate([r[i]["ns"] for i in range(NCORES)], axis=0)
    return (y_prompt.astype(f32), y_sample.astype(f32), new_state.astype(f32))
```
